# Optimizing a Trainium2 kernel written in Bass

```python
import jax, jax.numpy as jnp
from jax import lax
import numpy as np

D_MODEL = 1024
BATCH = 2
SEQ = 8192
DEPTH = 2

HEAD_DIM = 64
FOX_HEADS = 8
NSA_HEADS = 8
NSA_KV_HEADS = 2
NSA_GROUP = NSA_HEADS // NSA_KV_HEADS
CMP_BLOCK = 32
CMP_STRIDE = 16
CMP_HIDDEN = 4 * HEAD_DIM
SEL_BLOCK = 64
SEL_TOPK = 16
WINDOW = 512
Q_BLOCK = 128
ROPE_THETA = 500000.0
ROPE_DIM = HEAD_DIM // 4
D_FF = 4 * D_MODEL
RMS_EPS = 1e-6
FORGET_BIAS_INIT = 3.0
FORCE_SCORE = 1e6
NEG_BIG = -1e30

FOX_W = FOX_HEADS * HEAD_DIM
NSA_W = NSA_HEADS * HEAD_DIM
KV_W = NSA_KV_HEADS * HEAD_DIM
SPLITS = (FOX_W, FOX_W, FOX_W, FOX_HEADS, NSA_W, 6 * KV_W, 3 * NSA_HEADS, D_MODEL, D_MODEL)
D_IN = sum(SPLITS)

kernel_name = 'hybrid_fox_nsa_block'


def rmsnorm(x, g):
    xf = x.astype(jnp.float32)
    y = xf * lax.rsqrt(jnp.mean(xf * xf, axis=-1, keepdims=True) + RMS_EPS)
    return (y * g.astype(jnp.float32)).astype(x.dtype)


def partial_rope(t, pos):
    half = ROPE_DIM // 2
    inv = ROPE_THETA ** (-jnp.arange(half, dtype=jnp.float32) / half)
    ang = pos.astype(jnp.float32)[:, None] * inv[None, :]
    cos, sin = jnp.cos(ang), jnp.sin(ang)
    x1 = t[..., :half].astype(jnp.float32)
    x2 = t[..., half:ROPE_DIM].astype(jnp.float32)
    rot = jnp.concatenate([x1 * cos - x2 * sin, x2 * cos + x1 * sin], axis=-1)
    return jnp.concatenate([rot.astype(t.dtype), t[..., ROPE_DIM:]], axis=-1)


def to_heads(t, n):
    B, T, _ = t.shape
    return t.reshape(B, T, n, HEAD_DIM).transpose(0, 2, 1, 3)


def from_heads(o):
    B, H, T, dh = o.shape
    return o.transpose(0, 2, 1, 3).reshape(B, T, H * dh)


def n_cmp_blocks(T):
    return (T - CMP_BLOCK) // CMP_STRIDE + 1


def cmp_end_positions(T):
    return jnp.arange(n_cmp_blocks(T)) * CMP_STRIDE + CMP_BLOCK - 1


def cmp_to_sel_matrix(T):
    nc, n_sel = n_cmp_blocks(T), T // SEL_BLOCK
    cs = np.arange(nc) * CMP_STRIDE
    ce = cs + CMP_BLOCK
    ss = np.arange(n_sel) * SEL_BLOCK
    se = ss + SEL_BLOCK
    ov = np.clip(np.minimum(ce[:, None], se[None, :]) - np.maximum(cs[:, None], ss[None, :]), 0, None)
    return (ov / CMP_BLOCK).astype(np.float32)


def fox_attention(q, k, v, log_f):
    B, H, T, dh = q.shape
    c = jnp.cumsum(log_f, axis=-1)
    nb = T // Q_BLOCK
    qb = q.reshape(B, H, nb, Q_BLOCK, dh).transpose(2, 0, 1, 3, 4)
    cb = c.reshape(B, H, nb, Q_BLOCK).transpose(2, 0, 1, 3)
    kpos = jnp.arange(T)
    scale = HEAD_DIM ** -0.5

    def block(args):
        i, qi, ci = args
        qpos = i * Q_BLOCK + jnp.arange(Q_BLOCK)
        s = jnp.einsum('bhqd,bhkd->bhqk', qi, k, preferred_element_type=jnp.float32) * scale
        s = s + ci[..., :, None] - c[..., None, :]
        s = jnp.where(kpos[None, :] <= qpos[:, None], s, -jnp.inf)
        p = jax.nn.softmax(s, axis=-1)
        return jnp.einsum('bhqk,bhkd->bhqd', p.astype(v.dtype), v)

    out = lax.map(block, (jnp.arange(nb), qb, cb))
    return out.transpose(1, 2, 0, 3, 4).reshape(B, H, T, dh)


def compress_blocks(t, pos_emb, w1, b1, w2, b2):
    B, G, T, dh = t.shape
    nc = n_cmp_blocks(T)
    idx = np.arange(nc)[:, None] * CMP_STRIDE + np.arange(CMP_BLOCK)[None, :]
    blocks = t[:, :, idx, :] + pos_emb
    flat = blocks.reshape(B, G, nc, CMP_BLOCK * dh)
    return jax.nn.gelu(flat @ w1 + b1) @ w2 + b2


def nsa_attention(q, kc, vc, ks, vs, kw, vw):
    B, G, R, T, dh = q.shape
    n_sel = T // SEL_BLOCK
    top = min(SEL_TOPK, n_sel)
    nb = T // Q_BLOCK
    scale = HEAD_DIM ** -0.5
    cmp_end = cmp_end_positions(T)
    sel_map = jnp.asarray(cmp_to_sel_matrix(T))
    ks_blocks = ks.reshape(B, G, n_sel, SEL_BLOCK, dh)
    vs_blocks = vs.reshape(B, G, n_sel, SEL_BLOCK, dh)
    pad = ((0, 0), (0, 0), (WINDOW, 0), (0, 0))
    kw_pad = jnp.pad(kw, pad)
    vw_pad = jnp.pad(vw, pad)
    gather = jax.vmap(jax.vmap(lambda blocks, ix: blocks[ix]))
    qb = q.reshape(B, G, R, nb, Q_BLOCK, dh).transpose(3, 0, 1, 2, 4, 5)
    sel_ids = jnp.arange(n_sel)
    blk_start = sel_ids * SEL_BLOCK

    def block(args):
        i, qi = args
        q0 = i * Q_BLOCK
        qpos = q0 + jnp.arange(Q_BLOCK)
        s = jnp.einsum('bgrqd,bgnd->bgrqn', qi, kc, preferred_element_type=jnp.float32) * scale
        valid_c = cmp_end[None, :] <= qpos[:, None]
        p_c = jax.nn.softmax(jnp.where(valid_c, s, NEG_BIG), axis=-1) * valid_c
        o_c = jnp.einsum('bgrqn,bgnd->bgrqd', p_c.astype(vc.dtype), vc)
        imp = jnp.einsum('bgrqn,ns->bgqs', p_c, sel_map)
        future = blk_start[None, :] > qpos[:, None]
        forced = (sel_ids[None, :] == (qpos // SEL_BLOCK)[:, None]) | (sel_ids[None, :] == 0)
        imp = jnp.where(future, -1.0, jnp.where(forced, FORCE_SCORE, imp))
        _, idx = lax.top_k(imp, top)
        kg = gather(ks_blocks, idx).reshape(B, G, Q_BLOCK, top * SEL_BLOCK, dh)
        vg = gather(vs_blocks, idx).reshape(B, G, Q_BLOCK, top * SEL_BLOCK, dh)
        kpos = (idx[..., None] * SEL_BLOCK + jnp.arange(SEL_BLOCK)).reshape(B, G, Q_BLOCK, top * SEL_BLOCK)
        valid_s = kpos <= qpos[None, None, :, None]
        s = jnp.einsum('bgrqd,bgqkd->bgrqk', qi, kg, preferred_element_type=jnp.float32) * scale
        p_s = jax.nn.softmax(jnp.where(valid_s[:, :, None], s, -jnp.inf), axis=-1)
        o_s = jnp.einsum('bgrqk,bgqkd->bgrqd', p_s.astype(vg.dtype), vg)
        kwin = lax.dynamic_slice_in_dim(kw_pad, q0, WINDOW + Q_BLOCK, axis=2)
        vwin = lax.dynamic_slice_in_dim(vw_pad, q0, WINDOW + Q_BLOCK, axis=2)
        kpos_w = q0 - WINDOW + jnp.arange(WINDOW + Q_BLOCK)
        diff = qpos[:, None] - kpos_w[None, :]
        valid_w = (kpos_w[None, :] >= 0) & (diff >= 0) & (diff < WINDOW)
        s = jnp.einsum('bgrqd,bgkd->bgrqk', qi, kwin, preferred_element_type=jnp.float32) * scale
        p_w = jax.nn.softmax(jnp.where(valid_w, s, -jnp.inf), axis=-1)
        o_w = jnp.einsum('bgrqk,bgkd->bgrqd', p_w.astype(vwin.dtype), vwin)
        return o_c, o_s, o_w

    o_c, o_s, o_w = lax.map(block, (jnp.arange(nb), qb))
    unblock = lambda o: o.transpose(1, 2, 3, 0, 4, 5).reshape(B, G, R, T, dh)
    return unblock(o_c), unblock(o_s), unblock(o_w)


def hybrid_layer(x, norm_mix, w_in, b_forget, cmp_pos, cmp_w1, cmp_b1, cmp_w2, cmp_b2,
                 w_o_fox, w_o_nsa, w_out, norm_mlp, w_up, w_down):
    B, T, _ = x.shape
    h = rmsnorm(x, norm_mix)
    proj = h @ w_in
    q_a, k_a, v_a, f_a, q_b, kv_b, g_b, gate_a, gate_b = jnp.split(
        proj, np.cumsum(SPLITS)[:-1].tolist(), axis=-1)
    log_f = jax.nn.log_sigmoid((f_a + b_forget).astype(jnp.float32)).transpose(0, 2, 1)
    o_a = fox_attention(to_heads(q_a, FOX_HEADS), to_heads(k_a, FOX_HEADS), to_heads(v_a, FOX_HEADS), log_f)
    y_a = from_heads(o_a) @ w_o_fox
    pos = jnp.arange(T)
    q_n = partial_rope(to_heads(q_b, NSA_HEADS), pos).reshape(B, NSA_KV_HEADS, NSA_GROUP, T, HEAD_DIM)
    kc, vc, ks, vs, kw, vw = [to_heads(t, NSA_KV_HEADS) for t in jnp.split(kv_b, 6, axis=-1)]
    kc = compress_blocks(kc, cmp_pos[0], cmp_w1[0], cmp_b1[0], cmp_w2[0], cmp_b2[0])
    kc = partial_rope(kc, cmp_end_positions(T))
    vc = compress_blocks(vc, cmp_pos[1], cmp_w1[1], cmp_b1[1], cmp_w2[1], cmp_b2[1])
    ks = partial_rope(ks, pos)
    kw = partial_rope(kw, pos)
    o_c, o_s, o_w = nsa_attention(q_n, kc, vc, ks, vs, kw, vw)
    g = jax.nn.sigmoid(g_b).reshape(B, T, NSA_HEADS, 3).transpose(0, 2, 1, 3)
    g = g.reshape(B, NSA_KV_HEADS, NSA_GROUP, T, 3)
    o_n = g[..., 0:1] * o_c + g[..., 1:2] * o_s + g[..., 2:3] * o_w
    y_b = from_heads(o_n.reshape(B, NSA_HEADS, T, HEAD_DIM)) @ w_o_nsa
    mixed = jax.nn.sigmoid(gate_a) * y_a + jax.nn.sigmoid(gate_b) * y_b
    x = x + mixed @ w_out
    h = rmsnorm(x, norm_mlp)
    return x + jnp.square(jax.nn.relu(h @ w_up)) @ w_down


def setup_inputs(seed: int = 0) -> dict:
    key = jax.random.key(seed)
    ks = jax.random.split(key, 17)
    f32 = jnp.float32
    nrm = lambda k, shape, fan_in: jax.random.normal(k, shape, f32) * fan_in ** -0.5
    return {
        'x': jax.random.normal(ks[0], (BATCH, SEQ, D_MODEL), f32),
        'norm_mix': 1.0 + 0.02 * jax.random.normal(ks[1], (DEPTH, D_MODEL), f32),
        'w_in': nrm(ks[2], (DEPTH, D_MODEL, D_IN), D_MODEL),
        'b_forget': FORGET_BIAS_INIT + 0.1 * jax.random.normal(ks[3], (DEPTH, FOX_HEADS), f32),
        'cmp_pos': 0.02 * jax.random.normal(ks[4], (DEPTH, 2, CMP_BLOCK, HEAD_DIM), f32),
        'cmp_w1': nrm(ks[5], (DEPTH, 2, CMP_BLOCK * HEAD_DIM, CMP_HIDDEN), CMP_BLOCK * HEAD_DIM),
        'cmp_b1': 0.02 * jax.random.normal(ks[6], (DEPTH, 2, CMP_HIDDEN), f32),
        'cmp_w2': nrm(ks[7], (DEPTH, 2, CMP_HIDDEN, HEAD_DIM), CMP_HIDDEN),
        'cmp_b2': 0.02 * jax.random.normal(ks[8], (DEPTH, 2, HEAD_DIM), f32),
        'w_o_fox': nrm(ks[9], (DEPTH, FOX_W, D_MODEL), FOX_W),
        'w_o_nsa': nrm(ks[10], (DEPTH, NSA_W, D_MODEL), NSA_W),
        'w_out': nrm(ks[11], (DEPTH, D_MODEL, D_MODEL), D_MODEL),
        'norm_mlp': 1.0 + 0.02 * jax.random.normal(ks[12], (DEPTH, D_MODEL), f32),
        'w_up': nrm(ks[13], (DEPTH, D_MODEL, D_FF), D_MODEL),
        'w_down': 0.5 * nrm(ks[14], (DEPTH, D_FF, D_MODEL), D_FF),
        'norm_final': 1.0 + 0.02 * jax.random.normal(ks[15], (D_MODEL,), f32),
    }


def reference(x, norm_mix, w_in, b_forget, cmp_pos, cmp_w1, cmp_b1, cmp_w2, cmp_b2,
              w_o_fox, w_o_nsa, w_out, norm_mlp, w_up, w_down, norm_final):
    for l in range(DEPTH):
        x = hybrid_layer(x, norm_mix[l], w_in[l], b_forget[l], cmp_pos[l], cmp_w1[l], cmp_b1[l],
                         cmp_w2[l], cmp_b2[l], w_o_fox[l], w_o_nsa[l], w_out[l],
                         norm_mlp[l], w_up[l], w_down[l])
    return rmsnorm(x, norm_final)
```

```python
import numpy as np
import concourse.bass as bass
import concourse.mybir as mybir

F32 = mybir.dt.float32
BF16 = mybir.dt.bfloat16
AF = mybir.ActivationFunctionType
ALU = mybir.AluOpType


class Prog:
    NDMA = 6

    def __init__(self, nc):
        self.nc = nc
        self.eng = {'pe': nc.tensor, 'act': nc.scalar, 'dve': nc.vector,
                    'pool': nc.gpsimd, 'sp': nc.sync}
        self.ops = []

    def add(self, eng, fn, r=(), w=(), dma=False):
        self.ops.append((eng, fn, tuple(r), tuple(w), dma))

    def pe(self, fn, r=(), w=()): self.add('pe', fn, r, w)
    def act(self, fn, r=(), w=()): self.add('act', fn, r, w)
    def dve(self, fn, r=(), w=()): self.add('dve', fn, r, w)
    def pool(self, fn, r=(), w=()): self.add('pool', fn, r, w)
    def dma(self, q, fn, r=(), w=()): self.add(q, fn, r, w, True)

    def finalize(self):
        nc = self.nc
        ops = self.ops
        n = len(ops)
        last_w = {}
        readers = {}
        deps = [None] * n
        for i, (eng, fn, r, w, dma) in enumerate(ops):
            d = {}
            for t in r:
                j = last_w.get(t)
                if j is not None:
                    d[j] = 'raw'
            for t in w:
                j = last_w.get(t)
                if j is not None and j not in d:
                    d[j] = 'waw'
                for j in readers.get(t, ()):
                    if j not in d:
                        d[j] = 'war'
            for t in r:
                readers.setdefault(t, []).append(i)
            for t in w:
                last_w[t] = i
                readers[t] = []
            keep = []
            for j, kind in d.items():
                if j == i:
                    continue
                je, _, _, _, jdma = ops[j]
                if jdma:
                    keep.append(j)
                elif je == eng and not dma:
                    if (kind == 'raw' and eng != 'pe') or eng == 'pool':
                        keep.append(j)
                elif je == eng and dma:
                    keep.append(j)
                else:
                    keep.append(j)
            deps[i] = keep
        signal = [False] * n
        for i in range(n):
            for j in deps[i]:
                signal[j] = True
        csem = {e: nc.alloc_semaphore(f"c_{e}") for e in ['pe', 'act', 'dve', 'pool']}
        dsem = {q: [nc.alloc_semaphore(f"d_{q}{k}") for k in range(self.NDMA)] for q in ['sp', 'pool']}
        ccount = {e: 0 for e in csem}
        dcount = {q: 0 for q in dsem}
        semval = [None] * n
        for i, (eng, fn, r, w, dma) in enumerate(ops):
            if dma:
                k = dcount[eng]
                dcount[eng] += 1
                semval[i] = (dsem[eng][k % self.NDMA], 16 * (k // self.NDMA + 1), k)
            elif signal[i]:
                ccount[eng] += 1
                semval[i] = (csem[eng], ccount[eng], None)
        waited = {e: {} for e in self.eng}
        nwaits = 0
        for i, (eng, fn, r, w, dma) in enumerate(ops):
            E = self.eng[eng]
            wl = {}
            for j in deps[i]:
                s, v, _ = semval[j]
                key = id(s)
                if waited[eng].get(key, 0) >= v:
                    continue
                if key not in wl or wl[key][1] < v:
                    wl[key] = (s, v)
            if dma:
                s, v, k = semval[i]
                if k >= self.NDMA:
                    key = id(s)
                    pv = v - 16
                    if waited[eng].get(key, 0) < pv and (key not in wl or wl[key][1] < pv):
                        wl[key] = (s, pv)
            for key, (s, v) in wl.items():
                E.wait_ge(s, v)
                waited[eng][key] = v
                nwaits += 1
            ins = fn()
            if dma:
                ins.then_inc(semval[i][0], 16)
            elif signal[i]:
                ins.then_inc(semval[i][0], 1)
        E = nc.sync
        for q in dsem:
            tot = dcount[q]
            for k in range(min(tot, self.NDMA)):
                cnt = (tot - 1 - k) // self.NDMA + 1
                E.wait_ge(dsem[q][k], 16 * cnt)
        end = nc.alloc_semaphore("c_end")
        for e in ['pe', 'act', 'dve', 'pool']:
            self.eng[e].drain().then_inc(end, 1)
        E.wait_ge(end, 4)
        for s_ in list(csem.values()) + [x for q in dsem for x in dsem[q]] + [end]:
            E.sem_clear(s_)
        self.stats = dict(n=n, nwaits=nwaits, counts=dict(ccount), dmas=dict(dcount))
        return self.stats


T = 8192
D = 1024
NCH = 16
NKT = 64
NEG = -30000.0
NW = 774


def nsa_declare(dram, nc):
    d = {}
    d['w_nsa'] = dram("w_nsa", [D, NW])
    d['cosT'] = dram("cosT", [128, T]); d['sinT'] = dram("sinT", [128, T])
    d['cosC'] = dram("cosC", [128, 512]); d['sinC'] = dram("sinC", [128, 512])
    d['c_perm'] = dram("c_perm", [128, 128]); d['c_triU'] = dram("c_triU", [128, 128])
    d['c_E'] = dram("c_E", [128, NKT * 128])
    d['c_selmap'] = dram("c_selmap", [128, 4 * 128])
    d['c_cmask'] = dram("c_cmask", [128, 17 * 128])
    d['c_wk'] = dram("c_wk", [128, 256]); d['c_wa'] = dram("c_wa", [128, 256])
    d['posT'] = dram("posT", [128, 32])
    d['w1'] = dram("w1", [2, 2048, 256])
    d['b1T'] = dram("b1T", [128, 4])
    d['w2kd'] = dram("w2kd", [256, 128]); d['w2v'] = dram("w2v", [256, 64])
    d['b2k'] = dram("b2k", [128, 1]); d['b2v'] = dram("b2v", [128, 64])
    d['kvc'] = nc.dram_tensor("kvc", [128, T], BF16, kind="Internal").ap()
    d['onT'] = dram("onT", [128, T], BF16, kind="ExternalOutput")
    return d


def nsa_emit(L):
    nc = L['nc']; P = L['P']; ps = L['ps']; sb = L['sb']; dd = L['nsa_dram']
    R0 = L['R0']; R1 = L['R1']; R2 = L['R2']; xin0 = L['xin0']; hT = L['hT']
    ident = L['ident']; tri = L['tri']; norm_chunk = L['norm_chunk']; fm_proj = L['fm_proj']
    nq_limit = L['nqt_limit']
    ksT2 = R0; kwT2 = R1
    qbT = R2[:].rearrange("p (i t) -> p i t", i=2)

    wn = sb("wn", [128, 8, NW])
    perm = sb("perm", [128, 128]); triU = sb("triU", [128, 128])
    E = xin0[:].bitcast(BF16).rearrange("p k n -> p (k n)").rearrange("p (k n) -> p k n", n=128)
    selmap = sb("selmap", [128, 4, 128]); cmask = sb("cmask", [128, 17, 128])
    wk = sb("wk", [128, 256], F32); wa = sb("wa", [128, 256], F32)
    posT = sb("posT_sb", [128, 32])
    b1T = sb("b1T_sb", [128, 4], F32)
    w2kd = sb("w2kd_sb", [128, 2, 128]); w2v = sb("w2v_sb", [128, 2, 64])
    b2k = sb("b2k_sb", [128, 1], F32); b2v = sb("b2v_sb", [128, 64], F32)
    cosC = sb("cosC_sb", [128, 512], F32); sinC = sb("sinC_sb", [128, 512], F32)
    g = nc.gpsimd; sp = nc.sync
    P.dma('pool', lambda: g.dma_start(out=wn[:], in_=dd['w_nsa'].rearrange("(c p) n -> p c n", p=128)), w=['wn'])
    P.dma('pool', lambda: g.dma_start(out=perm[:], in_=dd['c_perm'][:, :]), w=['perm'])
    P.dma('pool', lambda: g.dma_start(out=triU[:], in_=dd['c_triU'][:, :]), w=['triU'])
    P.dma('pool', lambda: g.dma_start(out=selmap[:], in_=dd['c_selmap'].rearrange("p (k n) -> p k n", n=128)), w=['selmap'])
    P.dma('pool', lambda: g.dma_start(out=cmask[:], in_=dd['c_cmask'].rearrange("p (k n) -> p k n", n=128)), w=['cmask'])
    P.dma('pool', lambda: g.dma_start(out=posT[:], in_=dd['posT'][:, :]), w=['posT'])
    P.dma('pool', lambda: g.dma_start(out=w2kd[:], in_=dd['w2kd'].rearrange("(c p) n -> p c n", p=128)), w=['w2kd'])
    P.dma('pool', lambda: g.dma_start(out=w2v[:], in_=dd['w2v'].rearrange("(c p) n -> p c n", p=128)), w=['w2v'])
    P.dma('sp', lambda: sp.dma_start(out=wk[:], in_=dd['c_wk'][:, :]), w=['wk'])
    P.dma('sp', lambda: sp.dma_start(out=wa[:], in_=dd['c_wa'][:, :]), w=['wa'])
    P.dma('sp', lambda: sp.dma_start(out=b1T[:], in_=dd['b1T'][:, :]), w=['b1T'])
    P.dma('sp', lambda: sp.dma_start(out=b2k[:], in_=dd['b2k'][:, :]), w=['b2k'])
    P.dma('sp', lambda: sp.dma_start(out=b2v[:], in_=dd['b2v'][:, :]), w=['b2v'])
    P.dma('sp', lambda: sp.dma_start(out=cosC[:], in_=dd['cosC'][:, :]), w=['cosC'])
    P.dma('sp', lambda: sp.dma_start(out=sinC[:], in_=dd['sinC'][:, :]), w=['sinC'])

    vsw = sb("vsw", [128, NKT, 4, 64])
    gsig = sb("gsig", [128, NKT, 6], F32)
    P.pool(lambda: nc.gpsimd.memset(vsw[:, :, 1, :], 1.0), w=['vsw_ones'])
    P.pool(lambda: nc.gpsimd.memset(vsw[:, :, 3, :], 1.0), w=['vsw_ones'])
    ct = sb("ct", [128, 512], F32); st = sb("st", [128, 512], F32)
    xs = sb("xs", [128, 512]); t1 = sb("t1", [128, 512], F32); t2 = sb("t2", [128, 512], F32)
    pt_sh = L['pt']; rcp_sh = L['rcp']; osb_sh = L['osb']
    stg = sb("stg", [128, 512])

    def rope(src_bank, cos_ap, sin_ap, dst_ap, n, scale, rtok, wtok, bias=None):
        if bias is None:
            P.act(lambda: nc.scalar.activation(out=xs[:, 0:n], in_=ps[src_bank][:, 0:n], func=AF.Copy, scale=scale),
                  r=[f'ps{src_bank}'], w=['xs'])
        else:
            P.act(lambda: nc.scalar.activation(out=xs[:, 0:n], in_=ps[src_bank][:, 0:n], func=AF.Identity, bias=bias, scale=scale),
                  r=[f'ps{src_bank}', 'b2k'], w=['xs'])
        P.pe(lambda: nc.tensor.matmul(ps[7][:, 0:n], lhsT=perm[:], rhs=xs[:, 0:n], start=True, stop=True), r=['xs', 'perm'], w=['ps7'])
        P.dve(lambda: nc.vector.tensor_tensor(out=t1[:, 0:n], in0=xs[:, 0:n], in1=cos_ap, op=ALU.mult), r=['xs', *rtok], w=['t1'])
        P.dve(lambda: nc.vector.tensor_tensor(out=t2[:, 0:n], in0=ps[7][:, 0:n], in1=sin_ap, op=ALU.mult), r=['ps7', *rtok], w=['t2'])
        P.pool(lambda: nc.gpsimd.tensor_tensor(out=dst_ap, in0=t1[:, 0:n], in1=t2[:, 0:n], op=ALU.add), r=['t1', 't2'], w=wtok)

    for c in range(NCH):
        norm_chunk(c)
        cs = slice(c * 512, (c + 1) * 512)
        P.dma('sp', lambda cs=cs: sp.dma_start(out=ct[:], in_=dd['cosT'][:, cs]), w=['ct'])
        P.dma('sp', lambda cs=cs: sp.dma_start(out=st[:], in_=dd['sinT'][:, cs]), w=['st'])
        dsts = [(qbT[:, 0, cs], 0.125, [f'qb0_{c}', f'va{c // 2}']),
                (qbT[:, 1, cs], 0.125, [f'qb1_{c}', f'va{8 + c // 2}']),
                (ksT2[:, cs], 1.0, [f'ks{c}', f'qa{c}']),
                (kwT2[:, cs], 1.0, [f'kw{c}', f'ka{c}'])]
        for ti, (dst, scl, wtok) in enumerate(dsts):
            fm_proj(wn, 128 * ti, 6, ['wn'])
            rope(6, ct[:], st[:], dst, 512, scl, ['ct', 'st'], wtok)
        fm_proj(wn, 512, 6, ['wn'])
        P.act(lambda: nc.scalar.copy(out=stg[:], in_=ps[6][:]), r=['ps6'], w=['stg'])
        P.dma('pool', lambda cs=cs: g.dma_start(out=dd['kvc'][:, cs], in_=stg[:]), r=['stg'], w=['kvc'])
        for t in range(4):
            for k in range(8):
                P.pe(lambda k=k, t=t: nc.tensor.matmul(ps[6][:, t * 128:(t + 1) * 128], lhsT=hT[:, k, t * 128:(t + 1) * 128],
                                                       rhs=wn[:, k, 640:768], start=(k == 0), stop=(k == 7)),
                     r=[f'hT{k}', 'wn'], w=['ps6'])
        for t in range(4):
            for k in range(8):
                P.pe(lambda k=k, t=t: nc.tensor.matmul(ps[7][:, t * 6:(t + 1) * 6], lhsT=hT[:, k, t * 128:(t + 1) * 128],
                                                       rhs=wn[:, k, 768:774], start=(k == 0), stop=(k == 7)),
                     r=[f'hT{k}', 'wn'], w=['ps7'])
        psv = ps[6][:].rearrange("p (t j d) -> p t j d", t=4, j=2)
        P.act(lambda c=c, psv=psv: nc.scalar.copy(out=vsw[:, 4 * c:4 * c + 4, 0, :], in_=psv[:, :, 0, :]), r=['ps6', 'vsw_ones'], w=[f'vsw{c}'])
        P.dve(lambda c=c, psv=psv: nc.vector.tensor_copy(out=vsw[:, 4 * c:4 * c + 4, 2, :], in_=psv[:, :, 1, :]), r=['ps6', 'vsw_ones'], w=[f'vsw{c}'])
        P.act(lambda c=c: nc.scalar.activation(out=gsig[:, 4 * c:4 * c + 4, :], in_=ps[7][:, 0:24].rearrange("p (t j) -> p t j", j=6),
                                               func=AF.Sigmoid), r=['ps7'], w=['gsig'])

    if L.get('stage', 9) < 2:
        return
    cb = xin0[:].bitcast(BF16).rearrange("p k n -> p (k n)")
    cbv = cb.rearrange("p (n s) -> p n s", s=16)
    w1b = L['w1b']
    hb = sb("hb", [128, 2], F32)
    hx = t1; gu = t2
    hid = [sb(f"hid{i}", [128, 512]) for i in range(2)]
    kcT2 = sb("kcT2", [128, 512]); vcA = sb("vcA", [128, 4, 128])
    P.dve(lambda: nc.vector.memset(hid[0][:, 511:512], 0.0), w=['hid0'])
    P.dve(lambda: nc.vector.memset(hid[1][:, 511:512], 0.0), w=['hid1'])
    P.dve(lambda: nc.vector.memset(kcT2[:, 511:512], 0.0), w=['kcT2'])
    P.pool(lambda: nc.gpsimd.memset(vcA[:, :, 64:128], 1.0), w=['vcA'])
    for mlp in range(2):
        P.dma('pool', lambda mlp=mlp: g.dma_start(out=w1b[:], in_=dd['w1'][mlp].rearrange("(c p) h -> p c h", p=128)), w=['w1b', 'wfm', 'wtm'])
        P.dma('sp', lambda mlp=mlp: sp.dma_start(out=cb[0:64, :], in_=dd['kvc'][64 * mlp:64 * mlp + 64, :]), r=['kvc'], w=['xin'])
        P.dma('sp', lambda mlp=mlp: sp.dma_start(out=cb[64:128, 0:T - 1], in_=dd['kvc'][64 * mlp:64 * mlp + 64, 1:T]), r=['kvc'], w=['xin'])
        for mt in range(2):
            for c16 in range(16):
                P.pe(lambda mt=mt, c16=c16, mlp=mlp: nc.tensor.matmul(ps[7][:, mt:mt + 1], lhsT=w1b[:, c16, mt * 128:(mt + 1) * 128],
                                                                      rhs=posT[:, 16 * mlp + c16:16 * mlp + c16 + 1],
                                                                      start=(c16 == 0), stop=(c16 == 15)),
                     r=['w1b', 'posT'], w=['ps7'])
        P.dve(lambda mlp=mlp: nc.vector.tensor_tensor(out=hb[:], in0=ps[7][:, 0:2], in1=b1T[:, 2 * mlp:2 * mlp + 2], op=ALU.add),
              r=['ps7', 'b1T'], w=['hb'])
        for mt in range(2):
            for c16 in range(16):
                P.pe(lambda mt=mt, c16=c16: nc.tensor.matmul(ps[6][:, 0:511], lhsT=w1b[:, c16, mt * 128:(mt + 1) * 128],
                                                             rhs=cbv[:, (2 * c16) // 16:(2 * c16) // 16 + 511, (2 * c16) % 16], start=(c16 == 0), stop=(c16 == 15)),
                     r=['w1b', 'xin'], w=['ps6'])
            P.act(lambda mt=mt: nc.scalar.activation(out=hx[:, 0:511], in_=ps[6][:, 0:511], func=AF.Identity, bias=hb[:, mt:mt + 1], scale=1.0),
                  r=['ps6', 'hb'], w=['t1'])
            P.dve(lambda: nc.vector.tensor_tensor(out=gu[:, 0:511], in0=hx[:, 0:511], in1=hx[:, 0:511], op=ALU.mult), r=['t1'], w=['t2'])
            P.dve(lambda: nc.vector.tensor_scalar(out=gu[:, 0:511], in0=gu[:, 0:511], scalar1=0.044715, scalar2=1.0, op0=ALU.mult, op1=ALU.add),
                  r=['t2'], w=['t2'])
            P.dve(lambda: nc.vector.tensor_tensor(out=gu[:, 0:511], in0=gu[:, 0:511], in1=hx[:, 0:511], op=ALU.mult), r=['t2', 't1'], w=['t2'])
            P.act(lambda: nc.scalar.activation(out=gu[:, 0:511], in_=gu[:, 0:511], func=AF.Sigmoid, scale=1.5957691216057308), r=['t2'], w=['t2'])
            P.dve(lambda mt=mt: nc.vector.tensor_tensor(out=hid[mt][:, 0:511], in0=hx[:, 0:511], in1=gu[:, 0:511], op=ALU.mult),
                  r=['t2', 't1'], w=[f'hid{mt}'])
        if mlp == 0:
            for mt in range(2):
                P.pe(lambda mt=mt: nc.tensor.matmul(ps[6][:, 0:511], lhsT=w2kd[:, mt, :], rhs=hid[mt][:, 0:511], start=(mt == 0), stop=(mt == 1)),
                     r=[f'hid{mt}', 'w2kd'], w=['ps6'])
            rope(6, cosC[:, 0:511], sinC[:, 0:511], kcT2[:, 0:511], 511, 1.0, ['cosC', 'sinC'], ['kcT2'], bias=b2k[:])
        else:
            for nt in range(4):
                for mt in range(2):
                    P.pe(lambda mt=mt, nt=nt: nc.tensor.matmul(ps[6][:, nt * 64:(nt + 1) * 64], lhsT=hid[mt][:, nt * 128:(nt + 1) * 128],
                                                               rhs=w2v[:, mt, :], start=(mt == 0), stop=(mt == 1)),
                         r=[f'hid{mt}', 'w2v'], w=['ps6'])
            P.dve(lambda: nc.vector.tensor_tensor(out=vcA[:, :, 0:64], in0=ps[6][:, 0:256].rearrange("p (a b) -> p a b", b=64),
                                                  in1=b2v[:].unsqueeze(1).to_broadcast([128, 4, 64]), op=ALU.add),
                  r=['ps6', 'b2v'], w=['vcA'])

    if L.get('stage', 9) < 3:
        return
    psS = L['psS']
    for q4 in range(4):
        P.dma('pool', lambda q4=q4: g.dma_start(out=E[:, 16 * q4:16 * q4 + 16, :],
                                               in_=dd['c_E'][:, 2048 * q4:2048 * q4 + 2048].rearrange("p (k n) -> p k n", n=128)), w=['E', 'xin'])
    ptn = pt_sh
    rs4 = sb("rs4", [128, 4], F32); rc4 = sb("rc4", [128, 4], F32)
    imp = sb("imp", [128, 128], F32); impm = sb("impm", [128, 128], F32); imp2 = sb("imp2", [128, 128], F32)
    m8 = sb("m8", [128, 16], F32)
    Mb = sb("Mb", [128, 128]); MbT = sb("MbT", [128, 256])
    grep = sb("grep", [128, 2, 6, 64])
    rcn = rcp_sh[0]; rgd = t1
    acc = sb("acc", [64, 512], F32); tmpc = rcp_sh[1]
    obn = osb_sh[0]
    onT = dd['onT']
    it = [0]
    NQ = 32 if nq_limit is None else nq_limit
    OS, OW, OC, UB = 4, 5, 6, 7
    qpc = [sb(f"qpc{b}", [128, 2, 2, 128]) for b in range(2)]
    qps = [sb(f"qps{b}", [128, 2, 256]) for b in range(2)]
    for b in range(2):
        P.pool(lambda b=b: nc.gpsimd.memset(qpc[b][:], 0.0), w=[f'qpc{b}'])
        P.pool(lambda b=b: nc.gpsimd.memset(qps[b][:], 0.0), w=[f'qps{b}'])
    ncmp = [0]

    def sbuf_of(i):
        d = i % 2
        return psS[d], [f'ps{2 * d}', f'ps{2 * d + 1}'], ptn[i % 3], f'pt{i % 3}'

    for qt2 in range(NQ):
        cq = qt2 // 2
        for p in range(2):
            qb = 2 * qt2 + p
            ntmax = qb // 16
            qc = qpc[ncmp[0] % 2]; qctok = f'qpc{ncmp[0] % 2}'; ncmp[0] += 1
            for j in range(2):
                P.pool(lambda j=j, qc=qc, qb=qb: nc.gpsimd.tensor_copy(out=qc[64 * j:64 * j + 64, j, :, :],
                                                                        in_=qbT[64 * j:64 * j + 64, :, qb * 128:(qb + 1) * 128]),
                       r=[f'qb0_{cq}', f'qb1_{cq}'], w=[qctok])
            for nt in range(ntmax + 1):
                i = it[0]; it[0] += 1
                Sd, stok, pt, ptok = sbuf_of(i)
                Dd = qb - 16 * nt
                masked = Dd <= 16
                for j in range(2):
                    for ti in range(2):
                        P.pe(lambda ti=ti, j=j, Sd=Sd, nt=nt, qb=qb, masked=masked, qc=qc: nc.tensor.matmul(
                            Sd[:, j * 512 + ti * 128:j * 512 + ti * 128 + 128], lhsT=kcT2[:, nt * 128:(nt + 1) * 128],
                            rhs=qc[:, j, ti, :], start=(ti == 0), stop=not masked),
                            r=['kcT2', qctok], w=[stok[j]])
                if masked:
                    for j in range(2):
                        for ti in range(2):
                            P.pe(lambda Sd=Sd, Dd=Dd, j=j, ti=ti: nc.tensor.matmul(Sd[:, j * 512 + ti * 128:j * 512 + ti * 128 + 128], lhsT=ident[:],
                                                                                  rhs=cmask[:, Dd, :], start=False, stop=True),
                                 r=['ident', 'cmask'], w=[stok[j]])
                Sv = Sd[:].rearrange("p (j q) -> p j q", j=2)[:, :, 0:256]
                pv_ = pt[:].rearrange("p (j q) -> p j q", j=2)
                P.act(lambda Sv=Sv, pv_=pv_: nc.scalar.activation(out=pv_, in_=Sv, func=AF.Exp), r=stok, w=[ptok])
                for j in range(2):
                    for ti in range(2):
                        h = 2 * j + ti
                        P.pe(lambda h=h, j=j, ti=ti, pt=pt, nt=nt, ntmax=ntmax: nc.tensor.matmul(
                            ps[UB][:, h * 128:(h + 1) * 128], lhsT=pt[:, j * 256 + ti * 128:j * 256 + ti * 128 + 128], rhs=selmap[:, nt, :],
                            start=(nt == 0 and h == 0), stop=(nt == ntmax)), r=[ptok, 'selmap'], w=[f'ps{UB}'])
                for j in range(2):
                    P.pe(lambda j=j, pt=pt, nt=nt, ntmax=ntmax, p=p: nc.tensor.matmul(
                        ps[OC][:, j * 256 + p * 128:j * 256 + p * 128 + 128], lhsT=vcA[:, nt, :], rhs=pt[:, j * 256:j * 256 + 128],
                        start=(nt == 0 and p == 0 and j == 0), stop=(nt == ntmax)), r=[ptok, 'vcA'], w=[f'ps{OC}'])
            U = ps[UB]
            P.dve(lambda U=U: nc.vector.tensor_reduce(out=rs4[:], in_=U[:].rearrange("p (a b) -> p a b", a=4), axis=mybir.AxisListType.X, op=ALU.add),
                  r=[f'ps{UB}'], w=['rs4'])
            P.dve(lambda: nc.vector.tensor_scalar(out=rs4[:], in0=rs4[:], scalar1=1e-30, scalar2=None, op0=ALU.max), r=['rs4'], w=['rs4'])
            P.dve(lambda: nc.vector.reciprocal(out=rc4[:], in_=rs4[:]), r=['rs4'], w=['rc4'])
            P.dve(lambda U=U: nc.vector.tensor_scalar(out=imp[:], in0=U[:, 0:128], scalar1=rc4[:, 0:1], scalar2=None, op0=ALU.mult),
                  r=[f'ps{UB}', 'rc4'], w=['imp'])
            for h4 in range(1, 4):
                P.dve(lambda h4=h4, U=U: nc.vector.scalar_tensor_tensor(out=imp[:], in0=U[:, h4 * 128:(h4 + 1) * 128], scalar=rc4[:, h4:h4 + 1],
                                                                        in1=imp[:], op0=ALU.mult, op1=ALU.add), r=[f'ps{UB}', 'rc4', 'imp'], w=['imp'])
            sl = slice(128 - 2 * qb, 256 - 2 * qb)
            P.dve(lambda sl=sl: nc.vector.tensor_tensor(out=impm[:], in0=imp[:], in1=wk[:, sl], op=ALU.mult), r=['imp', 'wk'], w=['impm'])
            P.dve(lambda sl=sl: nc.vector.tensor_tensor(out=impm[:], in0=impm[:], in1=wa[:, sl], op=ALU.add), r=['impm', 'wa'], w=['impm'])
            P.dve(lambda: nc.vector.memset(impm[:, 0:1], 2e6), r=['impm'], w=['impm'])
            P.dve(lambda: nc.vector.max(out=m8[:, 0:8], in_=impm[:]), r=['impm'], w=['m8a'])
            P.dve(lambda: nc.vector.match_replace(out=imp2[:], in_to_replace=m8[:, 0:8], in_values=impm[:], imm_value=-1e9),
                  r=['impm', 'm8a'], w=['imp2'])
            P.dve(lambda: nc.vector.max(out=m8[:, 8:16], in_=imp2[:]), r=['imp2'], w=['m8b'])
            P.dve(lambda: nc.vector.tensor_scalar(out=Mb[:], in0=impm[:], scalar1=m8[:, 15:16], scalar2=NEG, op0=ALU.is_lt, op1=ALU.mult),
                  r=['impm', 'm8b'], w=['Mb'])
            P.pe(lambda p=p, U=U: nc.tensor.matmul(U[:, p * 128:(p + 1) * 128], lhsT=Mb[:], rhs=ident[:], start=True, stop=True),
                 r=['Mb', 'ident'], w=[f'ps{UB}'])
            P.act(lambda p=p, U=U: nc.scalar.copy(out=MbT[:, p * 128:(p + 1) * 128], in_=U[:, p * 128:(p + 1) * 128]), r=[f'ps{UB}'], w=['MbT'])

        def run_branch(kts, qk_fn, ob, vslot):
            n = len(kts)
            info = {}

            def do_qk(idx):
                kt = kts[idx]
                i = it[0]; it[0] += 1
                Sd, stok, pt, ptok = sbuf_of(i)
                c0, w = qk_fn(kt, Sd, stok)
                Sv = Sd[:].rearrange("p (j q) -> p j q", j=2)[:, :, c0:c0 + w]
                pv_ = pt[:].rearrange("p (j q) -> p j q", j=2)[:, :, c0:c0 + w]
                P.act(lambda Sv=Sv, pv_=pv_: nc.scalar.activation(out=pv_, in_=Sv, func=AF.Exp), r=stok, w=[ptok])
                info[idx] = (i, c0, w)

            def do_pv(idx):
                kt = kts[idx]
                i, c0, w = info[idx]
                _, _, pt, ptok = sbuf_of(i)
                for j in range(2):
                    P.pe(lambda j=j, kt=kt, pt=pt, c0=c0, w=w, idx=idx: nc.tensor.matmul(
                        ps[ob][:, j * 256 + c0:j * 256 + c0 + w], lhsT=vsw[:, kt, vslot:vslot + 2, :], rhs=pt[:, j * 256 + c0:j * 256 + c0 + w],
                        start=(idx == 0 and j == 0), stop=(idx == n - 1)), r=[ptok, f'vsw{kt//4}'], w=[f'ps{ob}'])

            do_qk(0)
            for idx in range(n):
                if idx + 1 < n:
                    do_qk(idx + 1)
                do_pv(idx)

        def qk_sel(kt, Sd, stok, qt2=qt2, cq=cq):
            qs_ = qps[qt2 % 2]; qstok = f'qps{qt2 % 2}'
            r_ = kt - 2 * qt2
            c0 = 128 if r_ == 1 else 0
            w = 256 - c0
            diag = r_ >= 0
            for j in range(2):
                P.pe(lambda j=j: nc.tensor.matmul(Sd[:, j * 512 + c0:j * 512 + 256], lhsT=ksT2[:, kt * 128:(kt + 1) * 128],
                                                  rhs=qs_[:, j, c0:256], start=True, stop=False),
                     r=[f'ks{kt//4}', qstok], w=[stok[j]])
            for j in range(2):
                P.pe(lambda j=j: nc.tensor.matmul(Sd[:, j * 512 + c0:j * 512 + 256], lhsT=E[:, kt, :], rhs=MbT[:, c0:256],
                                                  start=False, stop=(not diag)), r=['E', 'MbT'], w=[stok[j]])
            if diag:
                for j in range(2):
                    P.pe(lambda j=j: nc.tensor.matmul(Sd[:, j * 512 + 128 * r_:j * 512 + 128 * r_ + 128], lhsT=ident[:], rhs=tri[:],
                                                      start=False, stop=True), r=['ident', 'tri'], w=[stok[j]])
            return c0, w

        def qk_win(kt, Sd, stok, qt2=qt2, cq=cq):
            qs_ = qps[qt2 % 2]; qstok = f'qps{qt2 % 2}'
            rel = kt - 2 * qt2
            if rel == -4:
                c0, w, mk = 0, 128, (triU, 0)
            elif rel == -3:
                c0, w, mk = 0, 256, (triU, 128)
            elif rel in (-2, -1):
                c0, w, mk = 0, 256, None
            elif rel == 0:
                c0, w, mk = 0, 256, (tri, 0)
            else:
                c0, w, mk = 128, 128, (tri, 128)
            for j in range(2):
                P.pe(lambda j=j: nc.tensor.matmul(Sd[:, j * 512 + c0:j * 512 + c0 + w], lhsT=kwT2[:, kt * 128:(kt + 1) * 128],
                                                  rhs=qs_[:, j, c0:c0 + w], start=True, stop=(mk is None)),
                     r=[f'kw{kt//4}', qstok], w=[stok[j]])
            if mk is not None:
                mt_, mo = mk
                for j in range(2):
                    P.pe(lambda j=j: nc.tensor.matmul(Sd[:, j * 512 + mo:j * 512 + mo + 128], lhsT=ident[:], rhs=mt_[:], start=False, stop=True),
                         r=['ident', 'tri', 'triU'], w=[stok[j]])
            return c0, w

        qs_ = qps[qt2 % 2]; qstok = f'qps{qt2 % 2}'
        for j in range(2):
            P.pool(lambda j=j, qs_=qs_, qt2=qt2: nc.gpsimd.tensor_copy(out=qs_[64 * j:64 * j + 64, j, :],
                                                                      in_=R2[64 * j:64 * j + 64, qt2 * 256:(qt2 + 1) * 256]),
                   r=[f'qb0_{cq}'], w=[qstok])
        run_branch(list(range(2 * qt2 + 2)), qk_sel, OS, 0)
        rels = [r for r in (-3, -4, -2, -1, 0, 1) if 2 * qt2 + r >= 0]
        run_branch([2 * qt2 + r for r in rels], qk_win, OW, 2)

        G = ps[UB]
        for p in range(2):
            P.dve(lambda p=p, qt2=qt2: nc.vector.tensor_copy(out=grep[:, p, :, :],
                                                             in_=gsig[:, 2 * qt2 + p, :].unsqueeze(2).to_broadcast([128, 6, 64])),
                  r=['gsig'], w=['grep'])
        for br, ob in enumerate((OC, OS, OW)):
            for p in range(2):
                for j in range(2):
                    P.pe(lambda p=p, j=j, br=br: nc.tensor.matmul(G[0:64, j * 256 + p * 128:j * 256 + p * 128 + 128],
                                                                  lhsT=grep[:, p, 3 * j + br, :], rhs=ident[:], start=True, stop=True),
                         r=['grep', 'ident'], w=[f'ps{UB}'])
            if br == 0 and qt2 == 0:
                P.dve(lambda ob=ob: nc.vector.tensor_scalar(out=rgd[64:128, :], in0=ps[ob][64:128, :], scalar1=1e-30, scalar2=None, op0=ALU.max),
                      r=[f'ps{ob}'], w=['t1'])
                P.dve(lambda: nc.vector.reciprocal(out=rcn[:], in_=rgd[64:128, :]), r=['t1'], w=['rcp0'])
            else:
                P.dve(lambda ob=ob: nc.vector.reciprocal(out=rcn[:], in_=ps[ob][64:128, :]), r=[f'ps{ob}'], w=['rcp0'])
            import os
            DBG = os.environ.get('DBG_BR')
            if DBG is not None:
                if int(DBG) == br:
                    P.dve(lambda ob=ob: nc.vector.tensor_tensor(out=obn[:], in0=ps[ob][0:64, :], in1=rcn[:], op=ALU.mult), r=[f'ps{ob}', 'rcp0'], w=['osb0'])
                continue
            P.dve(lambda: nc.vector.tensor_tensor(out=rcn[:], in0=rcn[:], in1=G[0:64, :], op=ALU.mult), r=['rcp0', f'ps{UB}'], w=['rcp0'])
            if br == 0:
                P.dve(lambda ob=ob: nc.vector.tensor_tensor(out=acc[:], in0=ps[ob][0:64, :], in1=rcn[:], op=ALU.mult), r=[f'ps{ob}', 'rcp0'], w=['acc'])
            else:
                P.dve(lambda ob=ob: nc.vector.tensor_tensor(out=tmpc[:], in0=ps[ob][0:64, :], in1=rcn[:], op=ALU.mult), r=[f'ps{ob}', 'rcp0'], w=['rcp1'])
                if br == 1:
                    P.pool(lambda: nc.gpsimd.tensor_tensor(out=acc[:], in0=acc[:], in1=tmpc[:], op=ALU.add), r=['acc', 'rcp1'], w=['acc'])
                else:
                    P.pool(lambda: nc.gpsimd.tensor_tensor(out=obn[:], in0=acc[:], in1=tmpc[:], op=ALU.add), r=['acc', 'rcp1'], w=['osb0'])
        for j in range(2):
            P.dma('pool', lambda j=j, qt2=qt2: g.dma_start(out=onT[64 * j:64 * j + 64, qt2 * 256:(qt2 + 1) * 256], in_=obn[:, j * 256:(j + 1) * 256]),
                  r=['osb0'])


def rope_tables(pos):
    half = 8
    inv = (np.float32(500000.0) ** (-np.arange(half, dtype=np.float32) / np.float32(half))).astype(np.float32)
    ang = pos.astype(np.float32)[:, None] * inv[None, :]
    cos = np.cos(ang).astype(np.float32); sin = np.sin(ang).astype(np.float32)
    L = pos.shape[0]
    c64 = np.ones((64, L), np.float32); s64 = np.zeros((64, L), np.float32)
    c64[0:8] = cos.T; c64[8:16] = cos.T
    s64[0:8] = sin.T; s64[8:16] = sin.T
    return np.concatenate([c64, c64], 0), np.concatenate([s64, s64], 0)


_CONSTS = None


def nsa_consts():
    global _CONSTS
    if _CONSTS is not None:
        return _CONSTS
    d = {}
    cT, sT = rope_tables(np.arange(T))
    d['cosT'] = cT; d['sinT'] = sT
    cc, sc = rope_tables(np.arange(512) * 16 + 31)
    d['cosC'] = cc; d['sinC'] = sc
    pm = np.zeros((128, 128), np.float32)
    for m in range(128):
        mm = m % 64
        if mm < 8:
            pm[m + 8, m] = -1.0
        elif mm < 16:
            pm[m - 8, m] = 1.0
    d['c_perm'] = pm
    k = np.arange(128)[:, None]; q = np.arange(128)[None, :]
    d['c_triU'] = np.where(k <= q, NEG, 0.0).astype(np.float32)
    E = np.zeros((128, NKT, 128), np.float32)
    for kt in range(NKT):
        E[2 * kt, kt, 0:64] = 1.0
        E[2 * kt + 1, kt, 64:128] = 1.0
    d['c_E'] = E.reshape(128, NKT * 128)
    ncb, nsel = 511, 128
    cs_ = np.arange(ncb) * 16; ce = cs_ + 32
    ss = np.arange(nsel) * 64; se = ss + 64
    ov = np.clip(np.minimum(ce[:, None], se[None, :]) - np.maximum(cs_[:, None], ss[None, :]), 0, None) / 32.0
    sm = np.zeros((512, 128), np.float32); sm[:511] = ov
    d['c_selmap'] = np.ascontiguousarray(sm.reshape(4, 128, 128).transpose(1, 0, 2).reshape(128, 512))
    cm = np.zeros((128, 17, 128), np.float32)
    nn = np.arange(128)[:, None]; qq = np.arange(128)[None, :]
    for Dd in range(17):
        cm[:, Dd, :] = np.where(16 * nn - qq <= 128 * Dd - 31, 0.0, NEG)
    d['c_cmask'] = cm.reshape(128, 17 * 128)
    qq = np.arange(128)[:, None]; u = np.arange(256)[None, :]
    dlt = u - 128; own = qq // 64
    d['c_wk'] = (dlt < own).astype(np.float32)
    d['c_wa'] = np.where(dlt == own, 1e6, np.where(dlt > own, -1.0, 0.0)).astype(np.float32)
    _CONSTS = d
    return d


def inputs_nsa(inp, l, b, hp):
    w_in = inp['w_in'][l]
    offs = np.cumsum([0, 512, 512, 512, 8, 512, 768, 24, 1024, 1024])
    gq = hp // 2
    own = w_in[:, offs[4] + 128 * hp: offs[4] + 128 * hp + 128]
    oth_hp = 2 * gq + (1 - hp % 2)
    oth = w_in[:, offs[4] + 128 * oth_hp: offs[4] + 128 * oth_hp + 128]
    kv = [w_in[:, offs[5] + 128 * s + 64 * gq: offs[5] + 128 * s + 64 * gq + 64] for s in range(6)]
    kc, vc, ks, vs, kw, vw = kv
    gcols = w_in[:, offs[6] + 6 * hp: offs[6] + 6 * hp + 6]
    w_nsa = np.concatenate([own, oth, ks, ks, kw, kw, kc, vc, vs, vw, gcols], axis=1)
    assert w_nsa.shape[1] == NW
    pos = inp['cmp_pos'][l]
    posT = np.zeros((128, 32), np.float32)
    for m in range(2):
        pp = pos[m].reshape(16, 2, 64).reshape(16, 128)
        posT[:, 16 * m:16 * m + 16] = pp.T
    b1 = inp['cmp_b1'][l]
    b1T = np.stack([b1[0, 0:128], b1[0, 128:256], b1[1, 0:128], b1[1, 128:256]], axis=1)
    w2 = inp['cmp_w2'][l]
    b2 = inp['cmp_b2'][l]
    m = dict(
        w_nsa=np.ascontiguousarray(w_nsa),
        posT=posT, w1=np.ascontiguousarray(inp['cmp_w1'][l]), b1T=np.ascontiguousarray(b1T),
        w2kd=np.ascontiguousarray(np.concatenate([w2[0], w2[0]], axis=1)), w2v=np.ascontiguousarray(w2[1]),
        b2k=np.ascontiguousarray(np.concatenate([b2[0], b2[0]])[:, None]),
        b2v=np.ascontiguousarray(np.broadcast_to(b2[1][None, :], (128, 64))),
    )
    m.update(nsa_consts())
    return m


T = 8192
D = 1024
NCH = 16
NKT = 64
EPS = 1e-6
NEG = -30000.0


def build_A(do_fox=True, do_nsa=True, nqt_limit=None, stage=9):
    nc = bass.Bass("TRN2", target_bir_lowering=False)
    P = Prog(nc)
    dram = lambda name, shape, dt=F32, kind="ExternalInput": nc.dram_tensor(name, shape, dt, kind=kind).ap()
    xT = dram("xT", [D, T])
    gmix = dram("gmix", [128, 8])
    w_fm_fox = dram("w_fm_fox", [D, 256])
    w_tm_fox = dram("w_tm_fox", [D, 130])
    bfg = dram("bfg", [128, 2])
    c_tri = dram("c_tri", [128, 128])
    c_ident = dram("c_ident", [128, 128])
    c_ut = dram("c_ut", [128, 128])
    oaT = dram("oaT", [128, T], BF16, kind="ExternalOutput")
    if do_nsa:
        nsa_dram = nsa_declare(dram, nc)

    sb = lambda name, shape, dt=BF16: nc.alloc_sbuf_tensor(name, shape, dt)
    psS = [nc.alloc_psum_tensor(f"psS{i}", [128, 1024], F32) for i in range(2)]
    ps = [psS[0][:, 0:512], psS[0][:, 512:1024], psS[1][:, 0:512], psS[1][:, 512:1024]] + \
         [nc.alloc_psum_tensor(f"ps{i}", [128, 512], F32)[:] for i in range(4, 8)]

    g_sb = sb("g_sb", [128, 8], F32)
    tri = sb("tri", [128, 128]); ident = sb("ident", [128, 128]); ut = sb("ut", [128, 128])
    ones_bf = sb("ones_bf", [128, 128])
    eps_c = sb("eps_c", [128, 1], F32); one_c = sb("one_c", [128, 1], F32); zero_c = sb("zero_c", [128, 64], F32)
    w1b = sb("w1b", [128, 16, 256])
    wfm = w1b[:, 0:8, :]; wtm = w1b[:, 8:16, 0:130]
    bneg = sb("bneg", [128, 2], F32)
    P.dma('sp', lambda: nc.sync.dma_start(out=g_sb[:], in_=gmix[:, :]), w=['g'])
    P.dma('sp', lambda: nc.sync.dma_start(out=bneg[:], in_=bfg[:, :]), w=['bneg'])
    P.dma('pool', lambda: nc.gpsimd.dma_start(out=tri[:], in_=c_tri[:, :]), w=['tri'])
    P.dma('pool', lambda: nc.gpsimd.dma_start(out=ident[:], in_=c_ident[:, :]), w=['ident'])
    P.dma('pool', lambda: nc.gpsimd.dma_start(out=ut[:], in_=c_ut[:, :]), w=['ut'])
    P.dma('pool', lambda: nc.gpsimd.dma_start(out=wfm, in_=w_fm_fox.rearrange("(c p) n -> p c n", p=128)), w=['wfm'])
    P.dma('pool', lambda: nc.gpsimd.dma_start(out=wtm, in_=w_tm_fox.rearrange("(c p) n -> p c n", p=128)), w=['wtm'])
    P.dve(lambda: nc.vector.memset(ones_bf[:], 1.0), w=['ones'])
    P.dve(lambda: nc.vector.memset(eps_c[:], EPS), w=['eps'])
    P.dve(lambda: nc.vector.memset(one_c[:], 1.0), w=['one'])
    P.dve(lambda: nc.vector.memset(zero_c[:], 0.0), w=['zero'])
    P.dve(lambda: nc.vector.tensor_scalar(out=bneg[:], in0=bneg[:], scalar1=-1.0, scalar2=None, op0=ALU.mult), r=['bneg'], w=['bneg'])

    R0 = sb("R0", [128, T]); R1 = sb("R1", [128, T]); R2 = sb("R2", [128, 2 * T])
    qaT = R0; kaT = R1
    va = R2[:].rearrange("p (k j d) -> p k j d", k=NKT, j=2)
    ftm = sb("ftm", [128, NKT, 2], F32)
    P.pool(lambda: nc.gpsimd.memset(va[:, :, :, 64:128], 1.0), w=['va_ones'])

    xin0 = sb("xin0", [128, 8, 512], F32)
    xin = [xin0, xin0]
    sq = [sb(f"sq{i}", [128, 512]) for i in range(2)]
    rstd = sb("rstd", [128, 512], F32)
    hT = sb("hT", [128, 8, 512])

    def norm_chunk(c):
        xb = xin[c % 2]
        P.dma('sp', lambda: nc.sync.dma_start(out=xb[:], in_=xT[:, c * 512:(c + 1) * 512].rearrange("(k p) n -> p k n", p=128)),
              w=['xin'])
        for k in range(8):
            s = sq[k % 2]
            P.act(lambda k=k, s=s: nc.scalar.activation(out=s[:], in_=xb[:, k, :], func=AF.Square), r=['xin'], w=[f'sq{k%2}'])
            P.pe(lambda k=k, s=s: nc.tensor.matmul(ps[5][:], lhsT=ones_bf[:], rhs=s[:], start=(k == 0), stop=(k == 7)),
                 r=[f'sq{k%2}', 'ones'], w=['ps5'])
        P.act(lambda: nc.scalar.activation(out=rstd[:], in_=ps[5][:], func=AF.Sqrt, bias=eps_c[:], scale=1.0 / D),
              r=['ps5', 'eps'], w=['rstd'])
        P.dve(lambda: nc.vector.reciprocal(out=rstd[:], in_=rstd[:]), r=['rstd'], w=['rstd'])
        for k in range(8):
            P.dve(lambda k=k: nc.vector.scalar_tensor_tensor(out=hT[:, k, :], in0=xb[:, k, :], scalar=g_sb[:, k:k + 1], in1=rstd[:],
                                                            op0=ALU.mult, op1=ALU.mult),
                  r=['xin', 'rstd', 'g'], w=[f'hT{k}'])

    def fm_proj(wt, col0, bank, extra_r=()):
        for k in range(8):
            P.pe(lambda k=k: nc.tensor.matmul(ps[bank][:], lhsT=wt[:, k, col0:col0 + 128], rhs=hT[:, k, :], start=(k == 0), stop=(k == 7)),
                 r=[f'hT{k}', *extra_r], w=[f'ps{bank}'])

    pt = [sb(f"pt{i}", [128, 512]) for i in range(3)]
    rcp = [sb(f"rcp{i}", [64, 512], F32) for i in range(2)]
    osb = [sb(f"osb{i}", [64, 512]) for i in range(2)]
    if do_fox:
        for c in range(NCH):
            norm_chunk(c)
            cs = slice(c * 512, (c + 1) * 512)
            fm_proj(wfm, 0, 6, ['wfm'])
            P.act(lambda cs=cs: nc.scalar.activation(out=qaT[:, cs], in_=ps[6][:], func=AF.Copy, scale=0.125), r=['ps6'], w=[f'qa{c}'])
            fm_proj(wfm, 128, 7, ['wfm'])
            P.dve(lambda cs=cs: nc.vector.tensor_copy(out=kaT[:, cs], in_=ps[7][:]), r=['ps7'], w=[f'ka{c}'])
            for t in range(4):
                for k in range(8):
                    P.pe(lambda k=k, t=t: nc.tensor.matmul(ps[6][:, t * 128:(t + 1) * 128], lhsT=hT[:, k, t * 128:(t + 1) * 128],
                                                           rhs=wtm[:, k, 0:128], start=(k == 0), stop=(k == 7)),
                         r=[f'hT{k}', 'wtm'], w=['ps6'])
            for t in range(4):
                for k in range(8):
                    P.pe(lambda k=k, t=t: nc.tensor.matmul(ps[7][:, t * 2:(t + 1) * 2], lhsT=hT[:, k, t * 128:(t + 1) * 128],
                                                           rhs=wtm[:, k, 128:130], start=(k == 0), stop=(k == 7)),
                         r=[f'hT{k}', 'wtm'], w=['ps7'])
            P.act(lambda c=c: nc.scalar.copy(out=va[:, 4 * c:4 * c + 4, :, 0:64],
                                             in_=ps[6][:].rearrange("p (t j d) -> p t j d", t=4, j=2)),
                  r=['ps6', 'va_ones'], w=[f'va{c}'])
            P.dve(lambda c=c: nc.vector.tensor_copy(out=ftm[:, 4 * c:4 * c + 4, :], in_=ps[7][:, 0:8].rearrange("p (t j) -> p t j", j=2)),
                  r=['ps7'], w=['ftm'])

        lf = sb("lf", [128, NKT, 2], F32)
        lhi = sb("lhi", [128, 128]); llo = sb("llo", [128, 128])
        ccol = sb("ccol", [128, NKT, 2], F32)
        incl = sb("incl", [128, NKT, 2], F32)
        tot = sb("tot", [128, NKT, 2], F32)
        for j in range(2):
            P.act(lambda j=j: nc.scalar.activation(out=lf[:, :, j], in_=ftm[:, :, j], func=AF.Exp, bias=bneg[:, j:j + 1], scale=-1.0),
                  r=['ftm', 'bneg'], w=['lf'])
        P.act(lambda: nc.scalar.activation(out=lf[:], in_=lf[:], func=AF.Ln, bias=one_c[:], scale=1.0), r=['lf', 'one'], w=['lf'])
        P.dve(lambda: nc.vector.tensor_scalar(out=lf[:], in0=lf[:], scalar1=-1.0, scalar2=None, op0=ALU.mult), r=['lf'], w=['lf'])
        lf2 = lf[:].rearrange("p a b -> p (a b)")
        P.dve(lambda: nc.vector.tensor_copy(out=lhi[:], in_=lf2), r=['lf'], w=['lhi'])
        P.dve(lambda: nc.vector.tensor_tensor(out=llo[:], in0=lf2, in1=lhi[:], op=ALU.subtract), r=['lf', 'lhi'], w=['llo'])
        P.pe(lambda: nc.tensor.matmul(ps[5][:, 0:128], lhsT=ut[:], rhs=lhi[:], start=True, stop=False), r=['ut', 'lhi'], w=['ps5'])
        P.pe(lambda: nc.tensor.matmul(ps[5][:, 0:128], lhsT=ut[:], rhs=llo[:], start=False, stop=True), r=['ut', 'llo'], w=['ps5'])
        P.pe(lambda: nc.tensor.matmul(ps[5][:, 128:256], lhsT=ones_bf[:], rhs=lhi[:], start=True, stop=False), r=['ones', 'lhi'], w=['ps5'])
        P.pe(lambda: nc.tensor.matmul(ps[5][:, 128:256], lhsT=ones_bf[:], rhs=llo[:], start=False, stop=True), r=['ones', 'llo'], w=['ps5'])
        P.dve(lambda: nc.vector.tensor_copy(out=tot[:].rearrange("p a b -> p (a b)"), in_=ps[5][:, 128:256]), r=['ps5'], w=['tot'])
        for j in range(2):
            P.dve(lambda j=j: nc.vector.tensor_tensor_scan(out=incl[:, :, j], data0=tot[:, :, j], data1=zero_c[:, 0:NKT], initial=0.0,
                                                           op0=ALU.add, op1=ALU.add),
                  r=['tot', 'zero'], w=['incl'])
        P.dve(lambda: nc.vector.tensor_tensor(out=ccol[:].rearrange("p a b -> p (a b)"), in0=ps[5][:, 0:128],
                                              in1=incl[:].rearrange("p a b -> p (a b)"), op=ALU.add), r=['ps5', 'incl'], w=['ccol'])
        P.dve(lambda: nc.vector.tensor_tensor(out=ccol[:], in0=ccol[:], in1=tot[:], op=ALU.subtract), r=['ccol', 'tot'], w=['ccol'])

        biasT = [sb(f"biasT{i}", [128, NKT], F32) for i in range(2)]
        qpz = [[sb(f"qpz{j}{b}", [128, 512]) for b in range(2)] for j in range(2)]
        for j in range(2):
            for b in range(2):
                P.pool(lambda j=j, b=b: nc.gpsimd.memset(qpz[j][b][:], 0.0), w=[f'qpz{j}{b}'])
        it = 0
        fin = 0
        nqt = NCH if nqt_limit is None else nqt_limit
        for j in range(2):
            hs = slice(64 * j, 64 * j + 64)
            for qt in range(nqt):
                nk = 4 * qt + 4
                bt = biasT[fin % 2]
                ob = 3 + fin % 2
                P.dve(lambda j=j, qt=qt, nk=nk, bt=bt: nc.vector.tensor_scalar(
                    out=bt[:, 0:nk], in0=ccol[:, 0:nk, j], scalar1=-1.0, scalar2=incl[:, 4 * qt + 3, j:j + 1], op0=ALU.mult, op1=ALU.add),
                    r=['ccol', 'incl'], w=[f'biasT{fin%2}'])
                qs = slice(qt * 512, (qt + 1) * 512)
                qp = qpz[j][qt % 2]; qptok = f'qpz{j}{qt%2}'
                P.pool(lambda qp=qp, hs=hs, qs=qs: nc.gpsimd.tensor_copy(out=qp[hs, :], in_=qaT[hs, qs]), r=[f'qa{qt}'], w=[qptok])

                def qk(kt, i, qt=qt, j=j, bt=bt, hs=hs, ob=ob, nk=nk, fin=fin, qp=qp, qptok=qptok):
                    r_ = kt - 4 * qt
                    c0 = 128 * r_ if r_ > 0 else 0
                    diag = r_ >= 0
                    S = ps[i % 3]
                    P.pe(lambda: nc.tensor.matmul(S[:, c0:512], lhsT=kaT[:, kt * 128:(kt + 1) * 128], rhs=qp[:, c0:512],
                                                  start=True, stop=not diag),
                         r=[f'ka{kt//4}', qptok], w=[f'ps{i%3}'])
                    if diag:
                        P.pe(lambda: nc.tensor.matmul(S[:, c0:c0 + 128], lhsT=ident[:], rhs=tri[:], start=False, stop=True),
                             r=['ident', 'tri'], w=[f'ps{i%3}'])
                    P.act(lambda: nc.scalar.activation(out=pt[i % 3][:, c0:512], in_=S[:, c0:512], func=AF.Exp, bias=bt[:, kt:kt + 1], scale=1.0),
                          r=[f'ps{i%3}', f'biasT{fin%2}'], w=[f'pt{i%3}'])

                def pv(kt, i, qt=qt, j=j, bt=bt, hs=hs, ob=ob, nk=nk, fin=fin):
                    r_ = kt - 4 * qt
                    c0 = 128 * r_ if r_ > 0 else 0
                    P.pe(lambda: nc.tensor.matmul(ps[ob][:, c0:512], lhsT=va[:, kt, j, :], rhs=pt[i % 3][:, c0:512],
                                                  start=(kt == 0), stop=(kt == nk - 1)),
                         r=[f'pt{i%3}', f'va{kt//4}'], w=[f'ps{ob}'])

                base = it
                for kt in range(min(2, nk)):
                    qk(kt, base + kt)
                for kt in range(nk):
                    if kt + 2 < nk:
                        qk(kt + 2, base + kt + 2)
                    pv(kt, base + kt)
                it += nk
                rc = rcp[fin % 2]; o_ = osb[fin % 2]
                P.dve(lambda rc=rc, ob=ob: nc.vector.reciprocal(out=rc[:], in_=ps[ob][64:128, :]), r=[f'ps{ob}'], w=[f'rcp{fin%2}'])
                P.dve(lambda rc=rc, ob=ob, o_=o_: nc.vector.tensor_tensor(out=o_[:], in0=ps[ob][0:64, :], in1=rc[:], op=ALU.mult),
                      r=[f'ps{ob}', f'rcp{fin%2}'], w=[f'osb{fin%2}'])
                P.dma('pool', lambda o_=o_, qs=qs, hs=hs: nc.gpsimd.dma_start(out=oaT[hs, qs], in_=o_[:]), r=[f'osb{fin%2}'])
                fin += 1
    if do_nsa:
        nsa_emit(locals())
    stats = P.finalize()
    return nc, stats


def host_consts():
    k = np.arange(128)[:, None]; q = np.arange(128)[None, :]
    return dict(
        c_tri=np.where(k > q, NEG, 0.0).astype(np.float32),
        c_ident=np.eye(128, dtype=np.float32),
        c_ut=(k <= q).astype(np.float32),
    )


def inputs_A(inp, l, b, hp, x_b=None, nsa=True):
    x = inp['x'][b] if x_b is None else x_b
    w_in = inp['w_in'][l]
    offs = np.cumsum([0, 512, 512, 512, 8, 512, 768, 24, 1024, 1024])
    qa = w_in[:, offs[0] + 128 * hp: offs[0] + 128 * hp + 128]
    ka = w_in[:, offs[1] + 128 * hp: offs[1] + 128 * hp + 128]
    va = w_in[:, offs[2] + 128 * hp: offs[2] + 128 * hp + 128]
    f = w_in[:, offs[3] + 2 * hp: offs[3] + 2 * hp + 2]
    m = dict(
        xT=np.ascontiguousarray(x.T),
        gmix=np.ascontiguousarray(inp['norm_mix'][l].reshape(8, 128).T),
        w_fm_fox=np.ascontiguousarray(np.concatenate([qa, ka], axis=1)),
        w_tm_fox=np.ascontiguousarray(np.concatenate([va, f], axis=1)),
        bfg=np.ascontiguousarray(np.broadcast_to(inp['b_forget'][l][2 * hp:2 * hp + 2][None, :], (128, 2))),
    )
    m.update(host_consts())
    if nsa:
        m.update(inputs_nsa(inp, l, b, hp))
    return m


D = 1024
NTB = 2048
EPS = 1e-6


def _norm(P, nc, ps, bank, xt, xtok, g_sb, gtok, out_bf, otok, ones_bf, eps_c, sq, rstd, n):
    for k in range(8):
        s = sq[k % 2]
        P.act(lambda k=k, s=s: nc.scalar.activation(out=s[:, 0:n], in_=xt[:, k, 0:n], func=AF.Square), r=[xtok], w=[f'sq{k%2}'])
        P.pe(lambda k=k, s=s: nc.tensor.matmul(ps[bank][:, 0:n], lhsT=ones_bf[:], rhs=s[:, 0:n], start=(k == 0), stop=(k == 7)),
             r=[f'sq{k%2}', 'ones'], w=[f'ps{bank}'])
    P.act(lambda: nc.scalar.activation(out=rstd[:, 0:n], in_=ps[bank][:, 0:n], func=AF.Sqrt, bias=eps_c[:], scale=1.0 / D),
          r=[f'ps{bank}', 'eps'], w=['rstd'])
    P.dve(lambda: nc.vector.reciprocal(out=rstd[:, 0:n], in_=rstd[:, 0:n]), r=['rstd'], w=['rstd'])
    for k in range(8):
        P.dve(lambda k=k: nc.vector.scalar_tensor_tensor(out=out_bf[:, k, 0:n], in0=xt[:, k, 0:n], scalar=g_sb[:, k:k + 1], in1=rstd[:, 0:n],
                                                        op0=ALU.mult, op1=ALU.mult), r=[xtok, 'rstd', gtok], w=[otok])


def build_B1():
    nc = bass.Bass("TRN2", target_bir_lowering=False)
    P = Prog(nc)
    dram = lambda name, shape, dt=F32, kind="ExternalInput": nc.dram_tensor(name, shape, dt, kind=kind).ap()
    xT = dram("xT", [D, NTB]); oaT = dram("oaT", [512, NTB], BF16); onT = dram("onT", [512, NTB], BF16)
    gmix = dram("gmix", [128, 8]); gmlp = dram("gmlp", [128, 8])
    w_of = dram("w_of", [512, D]); w_on = dram("w_on", [512, D])
    w_ga = dram("w_ga", [D, D]); w_gb = dram("w_gb", [D, D]); w_out = dram("w_out", [D, D])
    x1T = dram("x1T", [D, NTB], F32, kind="ExternalOutput")
    h2T = dram("h2T", [D, NTB], BF16, kind="ExternalOutput")
    sb = lambda name, shape, dt=BF16: nc.alloc_sbuf_tensor(name, shape, dt)
    ps = [nc.alloc_psum_tensor(f"ps{i}", [128, 512], F32) for i in range(8)]
    g = nc.gpsimd; sp = nc.sync
    gm = sb("gm", [128, 8], F32); gl = sb("gl", [128, 8], F32)
    ones_bf = sb("ones_bf", [128, 128]); eps_c = sb("eps_c", [128, 1], F32)
    wof = sb("wof", [128, 4, D]); won = sb("won", [128, 4, D])
    wga = sb("wga", [128, 8, D]); wgb = sb("wgb", [128, 8, D]); wout = sb("wout", [128, 8, D])
    P.dma('sp', lambda: sp.dma_start(out=gm[:], in_=gmix[:, :]), w=['gm'])
    P.dma('sp', lambda: sp.dma_start(out=gl[:], in_=gmlp[:, :]), w=['gl'])
    P.dve(lambda: nc.vector.memset(ones_bf[:], 1.0), w=['ones'])
    P.dve(lambda: nc.vector.memset(eps_c[:], EPS), w=['eps'])
    for nm, wt, src, kc in (('wof', wof, w_of, 4), ('won', won, w_on, 4), ('wga', wga, w_ga, 8), ('wgb', wgb, w_gb, 8), ('wout', wout, w_out, 8)):
        for k in range(kc):
            P.dma('pool', lambda wt=wt, src=src, k=k: g.dma_start(out=wt[:, k, :], in_=src[k * 128:(k + 1) * 128, :]), w=[nm])
    xt = sb("xt", [128, 8, 512], F32); oat = sb("oat", [128, 4, 512]); ont = sb("ont", [128, 4, 512])
    sq = [sb(f"sq{i}", [128, 512]) for i in range(2)]
    rstd = sb("rstd", [128, 512], F32)
    hT = sb("hT", [128, 8, 512]); mixT = sb("mixT", [128, 8, 512])
    sa = sb("sa", [128, 512], F32); sb_ = sb("sb_", [128, 512], F32); ma = sb("ma", [128, 512], F32); mb = sb("mb", [128, 512], F32)
    x1 = sb("x1", [128, 8, 512], F32); h2 = sb("h2", [128, 8, 512])
    for t in range(NTB // 512):
        ts_ = slice(t * 512, (t + 1) * 512)
        P.dma('sp', lambda ts_=ts_: sp.dma_start(out=xt[:], in_=xT[:, ts_].rearrange("(k p) n -> p k n", p=128)), w=['xt'])
        P.dma('sp', lambda ts_=ts_: sp.dma_start(out=oat[:], in_=oaT[:, ts_].rearrange("(k p) n -> p k n", p=128)), w=['oat'])
        P.dma('sp', lambda ts_=ts_: sp.dma_start(out=ont[:], in_=onT[:, ts_].rearrange("(k p) n -> p k n", p=128)), w=['ont'])
        _norm(P, nc, ps, 7, xt, 'xt', gm, 'gm', hT, 'hT', ones_bf, eps_c, sq, rstd, 512)
        for dc in range(8):
            ds_ = slice(dc * 128, (dc + 1) * 128)
            b0 = 4 * (dc % 2)
            for k in range(4):
                P.pe(lambda k=k, ds_=ds_, b0=b0: nc.tensor.matmul(ps[b0][:], lhsT=wof[:, k, ds_], rhs=oat[:, k, :], start=(k == 0), stop=(k == 3)),
                     r=['wof', 'oat'], w=[f'ps{b0}'])
            for k in range(8):
                P.pe(lambda k=k, ds_=ds_, b0=b0: nc.tensor.matmul(ps[b0 + 1][:], lhsT=wga[:, k, ds_], rhs=hT[:, k, :], start=(k == 0), stop=(k == 7)),
                     r=['wga', 'hT'], w=[f'ps{b0+1}'])
            for k in range(4):
                P.pe(lambda k=k, ds_=ds_, b0=b0: nc.tensor.matmul(ps[b0 + 2][:], lhsT=won[:, k, ds_], rhs=ont[:, k, :], start=(k == 0), stop=(k == 3)),
                     r=['won', 'ont'], w=[f'ps{b0+2}'])
            for k in range(8):
                P.pe(lambda k=k, ds_=ds_, b0=b0: nc.tensor.matmul(ps[b0 + 3][:], lhsT=wgb[:, k, ds_], rhs=hT[:, k, :], start=(k == 0), stop=(k == 7)),
                     r=['wgb', 'hT'], w=[f'ps{b0+3}'])
            P.act(lambda b0=b0: nc.scalar.activation(out=sa[:], in_=ps[b0 + 1][:], func=AF.Sigmoid), r=[f'ps{b0+1}'], w=['sa'])
            P.act(lambda b0=b0: nc.scalar.activation(out=sb_[:], in_=ps[b0 + 3][:], func=AF.Sigmoid), r=[f'ps{b0+3}'], w=['sb_'])
            P.dve(lambda b0=b0: nc.vector.tensor_tensor(out=ma[:], in0=ps[b0][:], in1=sa[:], op=ALU.mult), r=[f'ps{b0}', 'sa'], w=['ma'])
            P.dve(lambda b0=b0: nc.vector.tensor_tensor(out=mb[:], in0=ps[b0 + 2][:], in1=sb_[:], op=ALU.mult), r=[f'ps{b0+2}', 'sb_'], w=['mb'])
            P.pool(lambda dc=dc: nc.gpsimd.tensor_tensor(out=mixT[:, dc, :], in0=ma[:], in1=mb[:], op=ALU.add), r=['ma', 'mb'], w=['mixT'])
        for dc in range(8):
            ds_ = slice(dc * 128, (dc + 1) * 128)
            b = dc % 2
            for k in range(8):
                P.pe(lambda k=k, ds_=ds_, b=b: nc.tensor.matmul(ps[b][:], lhsT=wout[:, k, ds_], rhs=mixT[:, k, :], start=(k == 0), stop=(k == 7)),
                     r=['wout', 'mixT'], w=[f'ps{b}'])
            P.dve(lambda dc=dc, b=b: nc.vector.tensor_tensor(out=x1[:, dc, :], in0=ps[b][:], in1=xt[:, dc, :], op=ALU.add), r=[f'ps{b}', 'xt'], w=['x1'])
        P.dma('pool', lambda ts_=ts_: g.dma_start(out=x1T[:, ts_].rearrange("(k p) n -> p k n", p=128), in_=x1[:]), r=['x1'])
        _norm(P, nc, ps, 7, x1, 'x1', gl, 'gl', h2, 'h2', ones_bf, eps_c, sq, rstd, 512)
        P.dma('pool', lambda ts_=ts_: g.dma_start(out=h2T[:, ts_].rearrange("(k p) n -> p k n", p=128), in_=h2[:]), r=['h2'])
    return nc, P.finalize()


def build_B2():
    nc = bass.Bass("TRN2", target_bir_lowering=False)
    P = Prog(nc)
    dram = lambda name, shape, dt=F32, kind="ExternalInput": nc.dram_tensor(name, shape, dt, kind=kind).ap()
    x1T = dram("x1T", [D, NTB]); h2T = dram("h2T", [D, NTB], BF16)
    gfin = dram("gfin", [128, 8])
    w_up = dram("w_up", [D, 4096]); w_down = dram("w_down", [4096, D])
    x2T = dram("x2T", [D, NTB], F32, kind="ExternalOutput")
    yT = dram("yT", [D, NTB], F32, kind="ExternalOutput")
    sb = lambda name, shape, dt=BF16: nc.alloc_sbuf_tensor(name, shape, dt)
    ps = [nc.alloc_psum_tensor(f"ps{i}", [128, 512], F32) for i in range(8)]
    g = nc.gpsimd; sp = nc.sync
    gf = sb("gf", [128, 8], F32)
    ones_bf = sb("ones_bf", [128, 128]); eps_c = sb("eps_c", [128, 1], F32)
    wup = sb("wup", [128, 8, 4096]); wdn = sb("wdn", [128, 32, D])
    P.dma('sp', lambda: sp.dma_start(out=gf[:], in_=gfin[:, :]), w=['gf'])
    P.dve(lambda: nc.vector.memset(ones_bf[:], 1.0), w=['ones'])
    P.dve(lambda: nc.vector.memset(eps_c[:], EPS), w=['eps'])
    for k in range(8):
        for hh in range(2):
            P.dma('pool', lambda k=k, hh=hh: g.dma_start(out=wup[:, k, hh * 2048:(hh + 1) * 2048], in_=w_up[k * 128:(k + 1) * 128, hh * 2048:(hh + 1) * 2048]), w=['wup'])
    for k in range(32):
        P.dma('pool', lambda k=k: g.dma_start(out=wdn[:, k, :], in_=w_down[k * 128:(k + 1) * 128, :]), w=['wdn'])
    N = 256
    h2 = sb("h2", [128, 8, N]); x1 = sb("x1", [128, 8, N], F32)
    uT = sb("uT", [128, 32, N]); rl = [sb(f"rl{i}", [128, N], F32) for i in range(2)]
    x2 = sb("x2", [128, 8, N], F32); yo = sb("yo", [128, 8, N], F32)
    sq = [sb(f"sq{i}", [128, 512]) for i in range(2)]
    rstd = sb("rstd", [128, 512], F32)
    for t in range(NTB // N):
        ts_ = slice(t * N, (t + 1) * N)
        P.dma('sp', lambda ts_=ts_: sp.dma_start(out=h2[:], in_=h2T[:, ts_].rearrange("(k p) n -> p k n", p=128)), w=['h2'])
        P.dma('sp', lambda ts_=ts_: sp.dma_start(out=x1[:], in_=x1T[:, ts_].rearrange("(k p) n -> p k n", p=128)), w=['x1'])
        for fc in range(32):
            b = fc % 4
            for k in range(8):
                P.pe(lambda k=k, fc=fc, b=b: nc.tensor.matmul(ps[b][:, 0:N], lhsT=wup[:, k, fc * 128:(fc + 1) * 128], rhs=h2[:, k, :],
                                                              start=(k == 0), stop=(k == 7)), r=['wup', 'h2'], w=[f'ps{b}'])
            r_ = rl[fc % 2]
            P.act(lambda b=b, r_=r_: nc.scalar.activation(out=r_[:], in_=ps[b][:, 0:N], func=AF.Relu), r=[f'ps{b}'], w=[f'rl{fc%2}'])
            if fc % 2 == 0:
                P.dve(lambda fc=fc, r_=r_: nc.vector.tensor_tensor(out=uT[:, fc, :], in0=r_[:], in1=r_[:], op=ALU.mult), r=[f'rl{fc%2}'], w=[f'uT{fc}'])
            else:
                P.pool(lambda fc=fc, r_=r_: nc.gpsimd.tensor_tensor(out=uT[:, fc, :], in0=r_[:], in1=r_[:], op=ALU.mult), r=[f'rl{fc%2}'], w=[f'uT{fc}'])
        for dc in range(8):
            b = 4 + dc % 2
            for fc in range(32):
                P.pe(lambda fc=fc, dc=dc, b=b: nc.tensor.matmul(ps[b][:, 0:N], lhsT=wdn[:, fc, dc * 128:(dc + 1) * 128], rhs=uT[:, fc, :],
                                                                start=(fc == 0), stop=(fc == 31)), r=['wdn', f'uT{fc}'], w=[f'ps{b}'])
            P.dve(lambda dc=dc, b=b: nc.vector.tensor_tensor(out=x2[:, dc, :], in0=ps[b][:, 0:N], in1=x1[:, dc, :], op=ALU.add), r=[f'ps{b}', 'x1'], w=['x2'])
        P.dma('pool', lambda ts_=ts_: g.dma_start(out=x2T[:, ts_].rearrange("(k p) n -> p k n", p=128), in_=x2[:]), r=['x2'])
        for k in range(8):
            s = sq[k % 2]
            P.act(lambda k=k, s=s: nc.scalar.activation(out=s[:, 0:N], in_=x2[:, k, :], func=AF.Square), r=['x2'], w=[f'sq{k%2}'])
            P.pe(lambda k=k, s=s: nc.tensor.matmul(ps[7][:, 0:N], lhsT=ones_bf[:], rhs=s[:, 0:N], start=(k == 0), stop=(k == 7)),
                 r=[f'sq{k%2}', 'ones'], w=['ps7'])
        P.act(lambda: nc.scalar.activation(out=rstd[:, 0:N], in_=ps[7][:, 0:N], func=AF.Sqrt, bias=eps_c[:], scale=1.0 / D), r=['ps7', 'eps'], w=['rstd'])
        P.dve(lambda: nc.vector.reciprocal(out=rstd[:, 0:N], in_=rstd[:, 0:N]), r=['rstd'], w=['rstd'])
        for k in range(8):
            P.dve(lambda k=k: nc.vector.scalar_tensor_tensor(out=yo[:, k, :], in0=x2[:, k, :], scalar=gf[:, k:k + 1], in1=rstd[:, 0:N],
                                                            op0=ALU.mult, op1=ALU.mult), r=['x2', 'rstd', 'gf'], w=['yo'])
        P.dma('pool', lambda ts_=ts_: g.dma_start(out=yT[:, ts_].rearrange("(k p) n -> p k n", p=128), in_=yo[:]), r=['yo'])
    return nc, P.finalize()


_PROGS = {}


def _prog(name, fn):
    if name not in _PROGS:
        _PROGS[name] = fn()[0]
    return _PROGS[name]


def _lay(gv):
    return np.ascontiguousarray(np.asarray(gv, np.float32).reshape(8, 128).T)


def kernel(**inputs):
    import ml_dtypes
    from concourse.bass_utils import run_bass_kernel_spmd
    inp = {k: np.asarray(v) for k, v in inputs.items()}
    B = 2
    cores = list(range(8))
    offs = np.cumsum([0, 512, 512, 512, 8, 512, 768, 24, 1024, 1024])
    x = [np.ascontiguousarray(inp['x'][b]) for b in range(B)]
    y = None
    for l in range(2):
        ncA = _prog('A', lambda: build_A(True, True))
        mapsA = [inputs_A(inp, l, c // 4, c % 4, x_b=x[c // 4]) for c in cores]
        resA = run_bass_kernel_spmd(ncA, mapsA, core_ids=cores).results
        oa = [np.concatenate([resA[4 * b + hp]['oaT'] for hp in range(4)], axis=0) for b in range(B)]
        on = [np.concatenate([resA[4 * b + hp]['onT'] for hp in range(4)], axis=0) for b in range(B)]
        w_in = inp['w_in'][l]
        ncB1 = _prog('B1', build_B1)
        mapsB1 = []
        for c in cores:
            b, r = c // 4, c % 4
            rs = slice(2048 * r, 2048 * r + 2048)
            mapsB1.append(dict(xT=np.ascontiguousarray(x[b][rs].T), oaT=np.ascontiguousarray(oa[b][:, rs]), onT=np.ascontiguousarray(on[b][:, rs]),
                               gmix=_lay(inp['norm_mix'][l]), gmlp=_lay(inp['norm_mlp'][l]),
                               w_of=np.ascontiguousarray(inp['w_o_fox'][l]), w_on=np.ascontiguousarray(inp['w_o_nsa'][l]),
                               w_ga=np.ascontiguousarray(w_in[:, offs[7]:offs[8]]), w_gb=np.ascontiguousarray(w_in[:, offs[8]:offs[9]]),
                               w_out=np.ascontiguousarray(inp['w_out'][l])))
        resB1 = run_bass_kernel_spmd(ncB1, mapsB1, core_ids=cores).results
        ncB2 = _prog('B2', build_B2)
        mapsB2 = [dict(x1T=resB1[c]['x1T'], h2T=resB1[c]['h2T'], gfin=_lay(inp['norm_final']),
                       w_up=np.ascontiguousarray(inp['w_up'][l]), w_down=np.ascontiguousarray(inp['w_down'][l])) for c in cores]
        resB2 = run_bass_kernel_spmd(ncB2, mapsB2, core_ids=cores).results
        x = [np.ascontiguousarray(np.concatenate([resB2[4 * b + r]['x2T'].T for r in range(4)], axis=0)) for b in range(B)]
        y = np.stack([np.concatenate([resB2[4 * b + r]['yT'].T for r in range(4)], axis=0) for b in range(B)], axis=0)
    return np.ascontiguousarray(y.astype(np.float32))
```

```python
import numpy as np
import concourse.bass as bass
import concourse.mybir as mybir

F32 = mybir.dt.float32
BF16 = mybir.dt.bfloat16
AF = mybir.ActivationFunctionType
ALU = mybir.AluOpType


class Prog:
    NDMA = 6

    def __init__(self, nc):
        self.nc = nc
        self.eng = {'pe': nc.tensor, 'act': nc.scalar, 'dve': nc.vector,
                    'pool': nc.gpsimd, 'sp': nc.sync}
        self.ops = []

    def add(self, eng, fn, r=(), w=(), dma=False):
        self.ops.append((eng, fn, tuple(r), tuple(w), dma))

    def pe(self, fn, r=(), w=()): self.add('pe', fn, r, w)
    def act(self, fn, r=(), w=()): self.add('act', fn, r, w)
    def dve(self, fn, r=(), w=()): self.add('dve', fn, r, w)
    def pool(self, fn, r=(), w=()): self.add('pool', fn, r, w)
    def dma(self, q, fn, r=(), w=()): self.add(q, fn, r, w, True)

    def finalize(self):
        nc = self.nc
        ops = self.ops
        n = len(ops)
        last_w = {}
        readers = {}
        deps = [None] * n
        for i, (eng, fn, r, w, dma) in enumerate(ops):
            d = {}
            for t in r:
                j = last_w.get(t)
                if j is not None:
                    d[j] = 'raw'
            for t in w:
                j = last_w.get(t)
                if j is not None and j not in d:
                    d[j] = 'waw'
                for j in readers.get(t, ()):
                    if j not in d:
                        d[j] = 'war'
            for t in r:
                readers.setdefault(t, []).append(i)
            for t in w:
                last_w[t] = i
                readers[t] = []
            keep = []
            for j, kind in d.items():
                if j == i:
                    continue
                je, _, _, _, jdma = ops[j]
                if jdma:
                    keep.append(j)
                elif je == eng and not dma:
                    if (kind == 'raw' and eng != 'pe') or eng == 'pool':
                        keep.append(j)
                elif je == eng and dma:
                    keep.append(j)
                else:
                    keep.append(j)
            deps[i] = keep
        signal = [False] * n
        for i in range(n):
            for j in deps[i]:
                signal[j] = True
        csem = {e: nc.alloc_semaphore(f"c_{e}") for e in ['pe', 'act', 'dve', 'pool']}
        dsem = {q: [nc.alloc_semaphore(f"d_{q}{k}") for k in range(self.NDMA)] for q in ['sp', 'pool']}
        ccount = {e: 0 for e in csem}
        dcount = {q: 0 for q in dsem}
        semval = [None] * n
        for i, (eng, fn, r, w, dma) in enumerate(ops):
            if dma:
                k = dcount[eng]
                dcount[eng] += 1
                semval[i] = (dsem[eng][k % self.NDMA], 16 * (k // self.NDMA + 1), k)
            elif signal[i]:
                ccount[eng] += 1
                semval[i] = (csem[eng], ccount[eng], None)
        waited = {e: {} for e in self.eng}
        nwaits = 0
        for i, (eng, fn, r, w, dma) in enumerate(ops):
            E = self.eng[eng]
            wl = {}
            for j in deps[i]:
                s, v, _ = semval[j]
                key = id(s)
                if waited[eng].get(key, 0) >= v:
                    continue
                if key not in wl or wl[key][1] < v:
                    wl[key] = (s, v)
            if dma:
                s, v, k = semval[i]
                if k >= self.NDMA:
                    key = id(s)
                    pv = v - 16
                    if waited[eng].get(key, 0) < pv and (key not in wl or wl[key][1] < pv):
                        wl[key] = (s, pv)
            for key, (s, v) in wl.items():
                E.wait_ge(s, v)
                waited[eng][key] = v
                nwaits += 1
            ins = fn()
            if dma:
                ins.then_inc(semval[i][0], 16)
            elif signal[i]:
                ins.then_inc(semval[i][0], 1)
        E = nc.sync
        for q in dsem:
            tot = dcount[q]
            for k in range(min(tot, self.NDMA)):
                cnt = (tot - 1 - k) // self.NDMA + 1
                E.wait_ge(dsem[q][k], 16 * cnt)
        end = nc.alloc_semaphore("c_end")
        for e in ['pe', 'act', 'dve', 'pool']:
            self.eng[e].drain().then_inc(end, 1)
        E.wait_ge(end, 4)
        for s_ in list(csem.values()) + [x for q in dsem for x in dsem[q]] + [end]:
            E.sem_clear(s_)
        self.stats = dict(n=n, nwaits=nwaits, counts=dict(ccount), dmas=dict(dcount))
        return self.stats


T = 8192
D = 1024
NCH = 16
NKT = 64
NEG = -30000.0
NW = 774


def nsa_declare(dram, nc):
    d = {}
    d['w_nsa'] = dram("w_nsa", [D, NW])
    d['cosT'] = dram("cosT", [128, T]); d['sinT'] = dram("sinT", [128, T])
    d['cosC'] = dram("cosC", [128, 512]); d['sinC'] = dram("sinC", [128, 512])
    d['c_perm'] = dram("c_perm", [128, 128]); d['c_triU'] = dram("c_triU", [128, 128])
    d['c_ind'] = dram("c_ind", [64, T])
    d['c_selmap'] = dram("c_selmap", [128, 4 * 128])
    d['c_cmask'] = dram("c_cmask", [128, 17 * 128])
    d['c_wk'] = dram("c_wk", [128, 256]); d['c_wa'] = dram("c_wa", [128, 256])
    d['posT'] = dram("posT", [128, 32])
    d['w1'] = dram("w1", [2, 2048, 256])
    d['b1T'] = dram("b1T", [128, 4])
    d['w2kd'] = dram("w2kd", [256, 128]); d['w2v'] = dram("w2v", [256, 64])
    d['b2k'] = dram("b2k", [128, 1]); d['b2v'] = dram("b2v", [128, 64])
    d['kvc'] = nc.dram_tensor("kvc", [128, T], BF16, kind="Internal").ap()
    d['onT'] = dram("onT", [128, T], BF16, kind="ExternalOutput")
    return d


def nsa_emit(L):
    nc = L['nc']; P = L['P']; ps = L['ps']; sb = L['sb']; dd = L['nsa_dram']
    R0 = L['R0']; R1 = L['R1']; R2 = L['R2']; xin0 = L['xin0']; hT = L['hT']
    ident = L['ident']; tri = L['tri']; norm_chunk = L['norm_chunk']; fm_proj = L['fm_proj']
    nq_limit = L['nqt_limit']
    ksT2 = R0; kwT2 = R1
    qbT = R2[:].rearrange("p (i t) -> p i t", i=2)

    wn = sb("wn", [128, 8, NW])
    perm = sb("perm", [128, 128]); triU = sb("triU", [128, 128])
    selmap = sb("selmap", [128, 4, 128]); cmask = sb("cmask", [128, 17, 128])
    wk = sb("wk", [128, 256], F32); wa = sb("wa", [128, 256], F32)
    posT = sb("posT_sb", [128, 32])
    b1T = sb("b1T_sb", [128, 4], F32)
    w2kd = sb("w2kd_sb", [128, 2, 128]); w2v = sb("w2v_sb", [128, 2, 64])
    b2k = sb("b2k_sb", [128, 1], F32); b2v = sb("b2v_sb", [128, 64], F32)
    cosC = sb("cosC_sb", [128, 512], F32); sinC = sb("sinC_sb", [128, 512], F32)
    g = nc.gpsimd; sp = nc.sync
    P.dma('pool', lambda: g.dma_start(out=wn[:], in_=dd['w_nsa'].rearrange("(c p) n -> p c n", p=128)), w=['wn'])
    P.dma('pool', lambda: g.dma_start(out=perm[:], in_=dd['c_perm'][:, :]), w=['perm'])
    P.dma('pool', lambda: g.dma_start(out=triU[:], in_=dd['c_triU'][:, :]), w=['triU'])
    P.dma('pool', lambda: g.dma_start(out=selmap[:], in_=dd['c_selmap'].rearrange("p (k n) -> p k n", n=128)), w=['selmap'])
    P.dma('pool', lambda: g.dma_start(out=cmask[:], in_=dd['c_cmask'].rearrange("p (k n) -> p k n", n=128)), w=['cmask'])
    P.dma('pool', lambda: g.dma_start(out=posT[:], in_=dd['posT'][:, :]), w=['posT'])
    P.dma('pool', lambda: g.dma_start(out=w2kd[:], in_=dd['w2kd'].rearrange("(c p) n -> p c n", p=128)), w=['w2kd'])
    P.dma('pool', lambda: g.dma_start(out=w2v[:], in_=dd['w2v'].rearrange("(c p) n -> p c n", p=128)), w=['w2v'])
    P.dma('sp', lambda: sp.dma_start(out=wk[:], in_=dd['c_wk'][:, :]), w=['wk'])
    P.dma('sp', lambda: sp.dma_start(out=wa[:], in_=dd['c_wa'][:, :]), w=['wa'])
    P.dma('sp', lambda: sp.dma_start(out=b1T[:], in_=dd['b1T'][:, :]), w=['b1T'])
    P.dma('sp', lambda: sp.dma_start(out=b2k[:], in_=dd['b2k'][:, :]), w=['b2k'])
    P.dma('sp', lambda: sp.dma_start(out=b2v[:], in_=dd['b2v'][:, :]), w=['b2v'])
    P.dma('sp', lambda: sp.dma_start(out=cosC[:], in_=dd['cosC'][:, :]), w=['cosC'])
    P.dma('sp', lambda: sp.dma_start(out=sinC[:], in_=dd['sinC'][:, :]), w=['sinC'])

    vsw = sb("vsw", [128, NKT, 4, 64])
    gsig = sb("gsig", [128, NKT, 6], F32)
    P.pool(lambda: nc.gpsimd.memset(vsw[:, :, 1, :], 1.0), w=['vsw_ones'])
    P.pool(lambda: nc.gpsimd.memset(vsw[:, :, 3, :], 1.0), w=['vsw_ones'])
    ct = sb("ct", [128, 512], F32); st = sb("st", [128, 512], F32)
    xs = sb("xs", [128, 512]); t1 = sb("t1", [128, 512], F32); t2 = sb("t2", [128, 512], F32)
    pt_sh = L['pt']; rcp_sh = L['rcp']; osb_sh = L['osb']
    stg = sb("stg", [128, 512])

    def rope(src_bank, cos_ap, sin_ap, dst_ap, n, scale, rtok, wtok, bias=None, rows=128):
        if bias is None:
            P.act(lambda: nc.scalar.activation(out=xs[:, 0:n], in_=ps[src_bank][:, 0:n], func=AF.Copy, scale=scale),
                  r=[f'ps{src_bank}'], w=['xs'])
        else:
            P.act(lambda: nc.scalar.activation(out=xs[:, 0:n], in_=ps[src_bank][:, 0:n], func=AF.Identity, bias=bias, scale=scale),
                  r=[f'ps{src_bank}', 'b2k'], w=['xs'])
        P.pe(lambda: nc.tensor.matmul(ps[7][:, 0:n], lhsT=perm[:], rhs=xs[:, 0:n], start=True, stop=True), r=['xs', 'perm'], w=['ps7'])
        P.dve(lambda: nc.vector.tensor_tensor(out=t1[:, 0:n], in0=xs[:, 0:n], in1=cos_ap, op=ALU.mult), r=['xs', *rtok], w=['t1'])
        P.dve(lambda: nc.vector.tensor_tensor(out=t2[:, 0:n], in0=ps[7][:, 0:n], in1=sin_ap, op=ALU.mult), r=['ps7', *rtok], w=['t2'])
        P.pool(lambda: nc.gpsimd.tensor_tensor(out=dst_ap, in0=t1[0:rows, 0:n], in1=t2[0:rows, 0:n], op=ALU.add), r=['t1', 't2'], w=wtok)

    for c in range(NCH):
        norm_chunk(c)
        cs = slice(c * 512, (c + 1) * 512)
        P.dma('sp', lambda cs=cs: sp.dma_start(out=ct[:], in_=dd['cosT'][:, cs]), w=['ct'])
        P.dma('sp', lambda cs=cs: sp.dma_start(out=st[:], in_=dd['sinT'][:, cs]), w=['st'])
        dsts = [(qbT[:, 0, cs], 0.125, [f'qb0_{c}', f'va{c // 2}']),
                (qbT[:, 1, cs], 0.125, [f'qb1_{c}', f'va{8 + c // 2}']),
                (ksT2[0:64, cs], 1.0, [f'ks{c}', f'qa{c}']),
                (kwT2[:, cs], 1.0, [f'kw{c}', f'ka{c}'])]
        for ti, (dst, scl, wtok) in enumerate(dsts):
            fm_proj(wn, 128 * ti, 6, ['wn'])
            rope(6, ct[:], st[:], dst, 512, scl, ['ct', 'st'], wtok, rows=(64 if ti == 2 else 128))
        P.dma('pool', lambda cs=cs: g.dma_start(out=ksT2[64:128, cs], in_=dd['c_ind'][:, cs]), w=[f'ksi{c}', f'qa{c}'])
        fm_proj(wn, 512, 6, ['wn'])
        P.act(lambda: nc.scalar.copy(out=stg[:], in_=ps[6][:]), r=['ps6'], w=['stg'])
        P.dma('pool', lambda cs=cs: g.dma_start(out=dd['kvc'][:, cs], in_=stg[:]), r=['stg'], w=['kvc'])
        for t in range(4):
            for k in range(8):
                P.pe(lambda k=k, t=t: nc.tensor.matmul(ps[6][:, t * 128:(t + 1) * 128], lhsT=hT[:, k, t * 128:(t + 1) * 128],
                                                       rhs=wn[:, k, 640:768], start=(k == 0), stop=(k == 7)),
                     r=[f'hT{k}', 'wn'], w=['ps6'])
        for t in range(4):
            for k in range(8):
                P.pe(lambda k=k, t=t: nc.tensor.matmul(ps[7][:, t * 6:(t + 1) * 6], lhsT=hT[:, k, t * 128:(t + 1) * 128],
                                                       rhs=wn[:, k, 768:774], start=(k == 0), stop=(k == 7)),
                     r=[f'hT{k}', 'wn'], w=['ps7'])
        psv = ps[6][:].rearrange("p (t j d) -> p t j d", t=4, j=2)
        P.act(lambda c=c, psv=psv: nc.scalar.copy(out=vsw[:, 4 * c:4 * c + 4, 0, :], in_=psv[:, :, 0, :]), r=['ps6', 'vsw_ones'], w=[f'vsw{c}'])
        P.dve(lambda c=c, psv=psv: nc.vector.tensor_copy(out=vsw[:, 4 * c:4 * c + 4, 2, :], in_=psv[:, :, 1, :]), r=['ps6', 'vsw_ones'], w=[f'vsw{c}'])
        P.act(lambda c=c: nc.scalar.activation(out=gsig[:, 4 * c:4 * c + 4, :], in_=ps[7][:, 0:24].rearrange("p (t j) -> p t j", j=6),
                                               func=AF.Sigmoid), r=['ps7'], w=['gsig'])

    if L.get('stage', 9) < 2:
        return
    cb = xin0[:].bitcast(BF16).rearrange("p k n -> p (k n)")
    cbv = cb.rearrange("p (n s) -> p n s", s=16)
    w1b = L['w1b']
    hb = sb("hb", [128, 2], F32)
    hx = t1; gu = t2
    hid = [sb(f"hid{i}", [128, 512]) for i in range(2)]
    kcT2 = sb("kcT2", [128, 512]); vcA = sb("vcA", [128, 4, 128])
    P.dve(lambda: nc.vector.memset(hid[0][:, 511:512], 0.0), w=['hid0'])
    P.dve(lambda: nc.vector.memset(hid[1][:, 511:512], 0.0), w=['hid1'])
    P.dve(lambda: nc.vector.memset(kcT2[:, 511:512], 0.0), w=['kcT2'])
    P.pool(lambda: nc.gpsimd.memset(vcA[:, :, 64:128], 1.0), w=['vcA'])
    for mlp in range(2):
        P.dma('pool', lambda mlp=mlp: g.dma_start(out=w1b[:], in_=dd['w1'][mlp].rearrange("(c p) h -> p c h", p=128)), w=['w1b', 'wfm', 'wtm'])
        P.dma('sp', lambda mlp=mlp: sp.dma_start(out=cb[0:64, :], in_=dd['kvc'][64 * mlp:64 * mlp + 64, :]), r=['kvc'], w=['xin'])
        P.dma('sp', lambda mlp=mlp: sp.dma_start(out=cb[64:128, 0:T - 1], in_=dd['kvc'][64 * mlp:64 * mlp + 64, 1:T]), r=['kvc'], w=['xin'])
        for mt in range(2):
            for c16 in range(16):
                P.pe(lambda mt=mt, c16=c16, mlp=mlp: nc.tensor.matmul(ps[7][:, mt:mt + 1], lhsT=w1b[:, c16, mt * 128:(mt + 1) * 128],
                                                                      rhs=posT[:, 16 * mlp + c16:16 * mlp + c16 + 1],
                                                                      start=(c16 == 0), stop=(c16 == 15)),
                     r=['w1b', 'posT'], w=['ps7'])
        P.dve(lambda mlp=mlp: nc.vector.tensor_tensor(out=hb[:], in0=ps[7][:, 0:2], in1=b1T[:, 2 * mlp:2 * mlp + 2], op=ALU.add),
              r=['ps7', 'b1T'], w=['hb'])
        for mt in range(2):
            for c16 in range(16):
                P.pe(lambda mt=mt, c16=c16: nc.tensor.matmul(ps[6][:, 0:511], lhsT=w1b[:, c16, mt * 128:(mt + 1) * 128],
                                                             rhs=cbv[:, (2 * c16) // 16:(2 * c16) // 16 + 511, (2 * c16) % 16], start=(c16 == 0), stop=(c16 == 15)),
                     r=['w1b', 'xin'], w=['ps6'])
            P.act(lambda mt=mt: nc.scalar.activation(out=hx[:, 0:511], in_=ps[6][:, 0:511], func=AF.Identity, bias=hb[:, mt:mt + 1], scale=1.0),
                  r=['ps6', 'hb'], w=['t1'])
            P.dve(lambda: nc.vector.tensor_tensor(out=gu[:, 0:511], in0=hx[:, 0:511], in1=hx[:, 0:511], op=ALU.mult), r=['t1'], w=['t2'])
            P.dve(lambda: nc.vector.tensor_scalar(out=gu[:, 0:511], in0=gu[:, 0:511], scalar1=0.044715, scalar2=1.0, op0=ALU.mult, op1=ALU.add),
                  r=['t2'], w=['t2'])
            P.dve(lambda: nc.vector.tensor_tensor(out=gu[:, 0:511], in0=gu[:, 0:511], in1=hx[:, 0:511], op=ALU.mult), r=['t2', 't1'], w=['t2'])
            P.act(lambda: nc.scalar.activation(out=gu[:, 0:511], in_=gu[:, 0:511], func=AF.Sigmoid, scale=1.5957691216057308), r=['t2'], w=['t2'])
            P.dve(lambda mt=mt: nc.vector.tensor_tensor(out=hid[mt][:, 0:511], in0=hx[:, 0:511], in1=gu[:, 0:511], op=ALU.mult),
                  r=['t2', 't1'], w=[f'hid{mt}'])
        if mlp == 0:
            for mt in range(2):
                P.pe(lambda mt=mt: nc.tensor.matmul(ps[6][:, 0:511], lhsT=w2kd[:, mt, :], rhs=hid[mt][:, 0:511], start=(mt == 0), stop=(mt == 1)),
                     r=[f'hid{mt}', 'w2kd'], w=['ps6'])
            rope(6, cosC[:, 0:511], sinC[:, 0:511], kcT2[:, 0:511], 511, 1.0, ['cosC', 'sinC'], ['kcT2'], bias=b2k[:])
        else:
            for nt in range(4):
                for mt in range(2):
                    P.pe(lambda mt=mt, nt=nt: nc.tensor.matmul(ps[6][:, nt * 64:(nt + 1) * 64], lhsT=hid[mt][:, nt * 128:(nt + 1) * 128],
                                                               rhs=w2v[:, mt, :], start=(mt == 0), stop=(mt == 1)),
                         r=[f'hid{mt}', 'w2v'], w=['ps6'])
            P.dve(lambda: nc.vector.tensor_tensor(out=vcA[:, :, 0:64], in0=ps[6][:, 0:256].rearrange("p (a b) -> p a b", b=64),
                                                  in1=b2v[:].unsqueeze(1).to_broadcast([128, 4, 64]), op=ALU.add),
                  r=['ps6', 'b2v'], w=['vcA'])

    if L.get('stage', 9) < 3:
        return
    psS = L['psS']
    ptn = pt_sh
    rs4 = sb("rs4", [128, 4], F32); rc4 = sb("rc4", [128, 4], F32)
    imp = sb("imp", [128, 128], F32); impm = sb("impm", [128, 128], F32); imp2 = sb("imp2", [128, 128], F32)
    m8 = sb("m8", [128, 16], F32)
    Mb = sb("Mb", [128, 256])
    grep = sb("grep", [128, 2, 6, 64])
    rcn = rcp_sh[0]; rgd = t1
    acc = sb("acc", [64, 512], F32); tmpc = rcp_sh[1]
    obn = osb_sh[0]
    onT = dd['onT']
    it = [0]
    NQ = 32 if nq_limit is None else nq_limit
    OS, OW, OC, UB = 4, 5, 6, 7
    qpc = [sb(f"qpc{b}", [128, 2, 2, 128]) for b in range(2)]
    qsa = [sb(f"qsa{b}", [128, 2, 2, 256]) for b in range(2)]
    qsw = [sb(f"qsw{b}", [128, 2, 256]) for b in range(2)]
    for b in range(2):
        P.pool(lambda b=b: nc.gpsimd.memset(qpc[b][:], 0.0), w=[f'qpc{b}'])
        P.pool(lambda b=b: nc.gpsimd.memset(qsw[b][:], 0.0), w=[f'qsw{b}'])
    ncmp = [0]

    def sbuf_of(i):
        d = i % 2
        return psS[d], [f'ps{2 * d}', f'ps{2 * d + 1}'], ptn[i % 3], f'pt{i % 3}'

    Mb2 = [Mb, sb("Mb1", [128, 256])]
    occ = [ct[0:64, :], st[0:64, :]]; occtok = ['ct', 'st']
    U = ps[UB]

    def part1(qt2):
        cq = qt2 // 2
        for p in range(2):
            qb = 2 * qt2 + p
            ntmax = qb // 16
            qc = qpc[ncmp[0] % 2]; qctok = f'qpc{ncmp[0] % 2}'; ncmp[0] += 1
            for j in range(2):
                P.pool(lambda j=j, qc=qc, qb=qb: nc.gpsimd.tensor_copy(out=qc[64 * j:64 * j + 64, j, :, :],
                                                                        in_=qbT[64 * j:64 * j + 64, :, qb * 128:(qb + 1) * 128]),
                       r=[f'qb0_{cq}', f'qb1_{cq}'], w=[qctok])
            for nt in range(ntmax + 1):
                i = it[0]; it[0] += 1
                Sd, stok, pt, ptok = sbuf_of(i)
                Dd = qb - 16 * nt
                masked = Dd <= 16
                for j in range(2):
                    for ti in range(2):
                        P.pe(lambda ti=ti, j=j, Sd=Sd, nt=nt, masked=masked, qc=qc: nc.tensor.matmul(
                            Sd[:, j * 512 + ti * 128:j * 512 + ti * 128 + 128], lhsT=kcT2[:, nt * 128:(nt + 1) * 128],
                            rhs=qc[:, j, ti, :], start=(ti == 0), stop=not masked),
                            r=['kcT2', qctok], w=[stok[j]])
                if masked:
                    for j in range(2):
                        for ti in range(2):
                            P.pe(lambda Sd=Sd, Dd=Dd, j=j, ti=ti: nc.tensor.matmul(Sd[:, j * 512 + ti * 128:j * 512 + ti * 128 + 128], lhsT=ident[:],
                                                                                  rhs=cmask[:, Dd, :], start=False, stop=True),
                                 r=['ident', 'cmask'], w=[stok[j]])
                Sv = Sd[:].rearrange("p (j q) -> p j q", j=2)[:, :, 0:256]
                pv_ = pt[:].rearrange("p (j q) -> p j q", j=2)
                P.act(lambda Sv=Sv, pv_=pv_: nc.scalar.activation(out=pv_, in_=Sv, func=AF.Exp), r=stok, w=[ptok])
                for j in range(2):
                    for ti in range(2):
                        h = 2 * j + ti
                        P.pe(lambda h=h, j=j, ti=ti, pt=pt, nt=nt, ntmax=ntmax: nc.tensor.matmul(
                            U[:, h * 128:(h + 1) * 128], lhsT=pt[:, j * 256 + ti * 128:j * 256 + ti * 128 + 128], rhs=selmap[:, nt, :],
                            start=(nt == 0 and h == 0), stop=(nt == ntmax)), r=[ptok, 'selmap'], w=[f'ps{UB}'])
                for j in range(2):
                    P.pe(lambda j=j, pt=pt, nt=nt, ntmax=ntmax, p=p: nc.tensor.matmul(
                        ps[OC][:, j * 256 + p * 128:j * 256 + p * 128 + 128], lhsT=vcA[:, nt, :], rhs=pt[:, j * 256:j * 256 + 128],
                        start=(nt == 0 and p == 0 and j == 0), stop=(nt == ntmax)), r=[ptok, 'vcA'], w=[f'ps{OC}'])
            P.dve(lambda: nc.vector.tensor_reduce(out=rs4[:], in_=U[:].rearrange("p (a b) -> p a b", a=4), axis=mybir.AxisListType.X, op=ALU.add),
                  r=[f'ps{UB}'], w=['rs4'])
            P.dve(lambda: nc.vector.tensor_scalar(out=rs4[:], in0=rs4[:], scalar1=1e-30, scalar2=None, op0=ALU.max), r=['rs4'], w=['rs4'])
            P.dve(lambda: nc.vector.reciprocal(out=rc4[:], in_=rs4[:]), r=['rs4'], w=['rc4'])
            P.dve(lambda: nc.vector.tensor_scalar(out=imp[:], in0=U[:, 0:128], scalar1=rc4[:, 0:1], scalar2=None, op0=ALU.mult),
                  r=[f'ps{UB}', 'rc4'], w=['imp'])
            for h4 in range(1, 4):
                P.dve(lambda h4=h4: nc.vector.scalar_tensor_tensor(out=imp[:], in0=U[:, h4 * 128:(h4 + 1) * 128], scalar=rc4[:, h4:h4 + 1],
                                                                   in1=imp[:], op0=ALU.mult, op1=ALU.add), r=[f'ps{UB}', 'rc4', 'imp'], w=['imp'])
            sl = slice(128 - 2 * qb, 256 - 2 * qb)
            P.dve(lambda sl=sl: nc.vector.tensor_tensor(out=impm[:], in0=imp[:], in1=wk[:, sl], op=ALU.mult), r=['imp', 'wk'], w=['impm'])
            P.dve(lambda sl=sl: nc.vector.tensor_tensor(out=impm[:], in0=impm[:], in1=wa[:, sl], op=ALU.add), r=['impm', 'wa'], w=['impm'])
            P.dve(lambda: nc.vector.memset(impm[:, 0:1], 2e6), r=['impm'], w=['impm'])
            P.dve(lambda: nc.vector.max(out=m8[:, 0:8], in_=impm[:]), r=['impm'], w=['m8a'])
            P.dve(lambda: nc.vector.match_replace(out=imp2[:], in_to_replace=m8[:, 0:8], in_values=impm[:], imm_value=-1e9),
                  r=['impm', 'm8a'], w=['imp2'])
            P.dve(lambda: nc.vector.max(out=m8[:, 8:16], in_=imp2[:]), r=['imp2'], w=['m8b'])
            mb = Mb2[p]
            for hh in range(2):
                P.dve(lambda hh=hh, mb=mb: nc.vector.tensor_scalar(out=mb[:, hh * 128:(hh + 1) * 128], in0=impm[:], scalar1=m8[:, 15:16], scalar2=NEG,
                                                                   op0=ALU.is_lt, op1=ALU.mult), r=['impm', 'm8b'], w=[f'Mb{p}'])
        oc_ = occ[qt2 % 2]; octok = occtok[qt2 % 2]
        if qt2 == 0:
            P.dve(lambda: nc.vector.tensor_scalar(out=rgd[64:128, :], in0=ps[OC][64:128, :], scalar1=1e-30, scalar2=None, op0=ALU.max),
                  r=[f'ps{OC}'], w=['t1'])
            P.dve(lambda: nc.vector.reciprocal(out=rcn[:], in_=rgd[64:128, :]), r=['t1'], w=['rcp0'])
        else:
            P.dve(lambda: nc.vector.reciprocal(out=rcn[:], in_=ps[OC][64:128, :]), r=[f'ps{OC}'], w=['rcp0'])
        P.dve(lambda oc_=oc_: nc.vector.tensor_tensor(out=oc_, in0=ps[OC][0:64, :], in1=rcn[:], op=ALU.mult), r=[f'ps{OC}', 'rcp0'], w=[octok])

    def part2(qt2):
        cq = qt2 // 2
        qa_ = qsa[qt2 % 2]; qatok = f'qsa{qt2 % 2}'
        qw_ = qsw[qt2 % 2]; qwtok = f'qsw{qt2 % 2}'
        for p in range(2):
            mb = Mb2[p]
            P.pe(lambda p=p, mb=mb: nc.tensor.matmul(U[:, p * 256:p * 256 + 128], lhsT=mb[:, 64:192], rhs=ident[:], start=True, stop=True),
                 r=[f'Mb{p}', 'ident'], w=[f'ps{UB}'])
            P.pe(lambda p=p, mb=mb: nc.tensor.matmul(U[:, p * 256 + 128:p * 256 + 256], lhsT=mb[:, 0:128], rhs=ident[:], start=False, stop=True),
                 r=[f'Mb{p}', 'ident'], w=[f'ps{UB}'])
            Uv = U[64:128, p * 256:(p + 1) * 256].rearrange("r (h q) -> r h q", h=2)
            P.act(lambda p=p, qa_=qa_, Uv=Uv: nc.scalar.copy(out=qa_[64:128, 0, :, p * 128:(p + 1) * 128], in_=Uv), r=[f'ps{UB}'], w=[qatok])
            P.dve(lambda p=p, qa_=qa_, Uv=Uv: nc.vector.tensor_copy(out=qa_[64:128, 1, :, p * 128:(p + 1) * 128], in_=Uv), r=[f'ps{UB}'], w=[qatok])
        for j in range(2):
            src = R2[64 * j:64 * j + 64, qt2 * 256:(qt2 + 1) * 256]
            for hh in range(2):
                if j == 0:
                    P.pool(lambda hh=hh, qa_=qa_, src=src: nc.gpsimd.tensor_copy(out=qa_[0:64, 0, hh, :], in_=src), r=[f'qb0_{cq}'], w=[qatok])
                else:
                    P.dve(lambda hh=hh, qa_=qa_, src=src: nc.vector.tensor_copy(out=qa_[0:64, 1, hh, :], in_=src), r=[f'qb0_{cq}'], w=[qatok])
            if j == 0:
                P.pool(lambda qw_=qw_, src=src: nc.gpsimd.tensor_copy(out=qw_[0:64, 0, :], in_=src), r=[f'qb0_{cq}'], w=[qwtok])
            else:
                P.dve(lambda qw_=qw_, src=src: nc.vector.tensor_copy(out=qw_[0:64, 1, :], in_=src), r=[f'qb0_{cq}'], w=[qwtok])

    def run_branch(kts, qk_fn, ob, vslot):
        n = len(kts)
        info = {}

        def do_qk(idx):
            kt = kts[idx]
            i = it[0]; it[0] += 1
            Sd, stok, pt, ptok = sbuf_of(i)
            c0, w = qk_fn(kt, Sd, stok)
            Sv = Sd[:].rearrange("p (j q) -> p j q", j=2)[:, :, c0:c0 + w]
            pv_ = pt[:].rearrange("p (j q) -> p j q", j=2)[:, :, c0:c0 + w]
            P.act(lambda Sv=Sv, pv_=pv_: nc.scalar.activation(out=pv_, in_=Sv, func=AF.Exp), r=stok, w=[ptok])
            info[idx] = (i, c0, w)

        def do_pv(idx):
            kt = kts[idx]
            i, c0, w = info[idx]
            _, _, pt, ptok = sbuf_of(i)
            for j in range(2):
                P.pe(lambda j=j, kt=kt, pt=pt, c0=c0, w=w, idx=idx: nc.tensor.matmul(
                    ps[ob][:, j * 256 + c0:j * 256 + c0 + w], lhsT=vsw[:, kt, vslot:vslot + 2, :], rhs=pt[:, j * 256 + c0:j * 256 + c0 + w],
                    start=(idx == 0 and j == 0), stop=(idx == n - 1)), r=[ptok, f'vsw{kt//4}'], w=[f'ps{ob}'])

        do_qk(0)
        for idx in range(n):
            if idx + 1 < n:
                do_qk(idx + 1)
            do_pv(idx)

    def attend(qt2):
        cq = qt2 // 2
        qa_ = qsa[qt2 % 2]; qatok = f'qsa{qt2 % 2}'
        qs_ = qsw[qt2 % 2]; qstok = f'qsw{qt2 % 2}'

        def qk_sel(kt, Sd, stok):
            hh = kt // 32
            r_ = kt - 2 * qt2
            c0 = 128 if r_ == 1 else 0
            w = 256 - c0
            diag = r_ >= 0
            for j in range(2):
                P.pe(lambda j=j: nc.tensor.matmul(Sd[:, j * 512 + c0:j * 512 + 256], lhsT=ksT2[:, kt * 128:(kt + 1) * 128],
                                                  rhs=qa_[:, j, hh, c0:256], start=True, stop=(not diag)),
                     r=[f'ks{kt//4}', f'ksi{kt//4}', qatok], w=[stok[j]])
            if diag:
                for j in range(2):
                    P.pe(lambda j=j: nc.tensor.matmul(Sd[:, j * 512 + 128 * r_:j * 512 + 128 * r_ + 128], lhsT=ident[:], rhs=tri[:],
                                                      start=False, stop=True), r=['ident', 'tri'], w=[stok[j]])
            return c0, w

        def qk_win(kt, Sd, stok):
            rel = kt - 2 * qt2
            if rel == -4:
                c0, w, mk = 0, 128, (triU, 0)
            elif rel == -3:
                c0, w, mk = 0, 256, (triU, 128)
            elif rel in (-2, -1):
                c0, w, mk = 0, 256, None
            elif rel == 0:
                c0, w, mk = 0, 256, (tri, 0)
            else:
                c0, w, mk = 128, 128, (tri, 128)
            for j in range(2):
                P.pe(lambda j=j: nc.tensor.matmul(Sd[:, j * 512 + c0:j * 512 + c0 + w], lhsT=kwT2[:, kt * 128:(kt + 1) * 128],
                                                  rhs=qs_[:, j, c0:c0 + w], start=True, stop=(mk is None)),
                     r=[f'kw{kt//4}', qstok], w=[stok[j]])
            if mk is not None:
                mt_, mo = mk
                for j in range(2):
                    P.pe(lambda j=j: nc.tensor.matmul(Sd[:, j * 512 + mo:j * 512 + mo + 128], lhsT=ident[:], rhs=mt_[:], start=False, stop=True),
                         r=['ident', 'tri', 'triU'], w=[stok[j]])
            return c0, w

        run_branch(list(range(2 * qt2 + 2)), qk_sel, OS, 0)
        rels = [r for r in (-3, -4, -2, -1, 0, 1) if 2 * qt2 + r >= 0]
        run_branch([2 * qt2 + r for r in rels], qk_win, OW, 2)

    def finalize_tile(qt2):
        G = ps[UB]
        oc_ = occ[qt2 % 2]; octok = occtok[qt2 % 2]
        for p in range(2):
            P.dve(lambda p=p: nc.vector.tensor_copy(out=grep[:, p, :, :],
                                                    in_=gsig[:, 2 * qt2 + p, :].unsqueeze(2).to_broadcast([128, 6, 64])),
                  r=['gsig'], w=['grep'])
        for br, ob in ((1, OS), (2, OW), (0, OC)):
            for p in range(2):
                for j in range(2):
                    P.pe(lambda p=p, j=j, br=br: nc.tensor.matmul(G[0:64, j * 256 + p * 128:j * 256 + p * 128 + 128],
                                                                  lhsT=grep[:, p, 3 * j + br, :], rhs=ident[:], start=True, stop=True),
                         r=['grep', 'ident'], w=[f'ps{UB}'])
            if br == 0:
                P.dve(lambda: nc.vector.tensor_tensor(out=tmpc[:], in0=oc_, in1=G[0:64, :], op=ALU.mult), r=[octok, f'ps{UB}'], w=['rcp1'])
                P.pool(lambda: nc.gpsimd.tensor_tensor(out=obn[:], in0=acc[:], in1=tmpc[:], op=ALU.add), r=['acc', 'rcp1'], w=['osb0'])
                continue
            P.dve(lambda ob=ob: nc.vector.reciprocal(out=rcn[:], in_=ps[ob][64:128, :]), r=[f'ps{ob}'], w=['rcp0'])
            P.dve(lambda: nc.vector.tensor_tensor(out=rcn[:], in0=rcn[:], in1=G[0:64, :], op=ALU.mult), r=['rcp0', f'ps{UB}'], w=['rcp0'])
            if br == 1:
                P.dve(lambda ob=ob: nc.vector.tensor_tensor(out=acc[:], in0=ps[ob][0:64, :], in1=rcn[:], op=ALU.mult), r=[f'ps{ob}', 'rcp0'], w=['acc'])
            else:
                P.dve(lambda ob=ob: nc.vector.tensor_tensor(out=tmpc[:], in0=ps[ob][0:64, :], in1=rcn[:], op=ALU.mult), r=[f'ps{ob}', 'rcp0'], w=['rcp1'])
                P.pool(lambda: nc.gpsimd.tensor_tensor(out=acc[:], in0=acc[:], in1=tmpc[:], op=ALU.add), r=['acc', 'rcp1'], w=['acc'])
        for j in range(2):
            P.dma('pool', lambda j=j: g.dma_start(out=onT[64 * j:64 * j + 64, qt2 * 256:(qt2 + 1) * 256], in_=obn[:, j * 256:(j + 1) * 256]),
                  r=['osb0'])

    part1(0)
    part2(0)
    for qt2 in range(NQ):
        if qt2 + 1 < NQ:
            part1(qt2 + 1)
        attend(qt2)
        if qt2 + 1 < NQ:
            part2(qt2 + 1)
        finalize_tile(qt2)


def rope_tables(pos):
    half = 8
    inv = (np.float32(500000.0) ** (-np.arange(half, dtype=np.float32) / np.float32(half))).astype(np.float32)
    ang = pos.astype(np.float32)[:, None] * inv[None, :]
    cos = np.cos(ang).astype(np.float32); sin = np.sin(ang).astype(np.float32)
    L = pos.shape[0]
    c64 = np.ones((64, L), np.float32); s64 = np.zeros((64, L), np.float32)
    c64[0:8] = cos.T; c64[8:16] = cos.T
    s64[0:8] = sin.T; s64[8:16] = sin.T
    return np.concatenate([c64, c64], 0), np.concatenate([s64, s64], 0)


_CONSTS = None


def nsa_consts():
    global _CONSTS
    if _CONSTS is not None:
        return _CONSTS
    d = {}
    cT, sT = rope_tables(np.arange(T))
    d['cosT'] = cT; d['sinT'] = sT
    cc, sc = rope_tables(np.arange(512) * 16 + 31)
    d['cosC'] = cc; d['sinC'] = sc
    pm = np.zeros((128, 128), np.float32)
    for m in range(128):
        mm = m % 64
        if mm < 8:
            pm[m + 8, m] = -1.0
        elif mm < 16:
            pm[m - 8, m] = 1.0
    d['c_perm'] = pm
    k = np.arange(128)[:, None]; q = np.arange(128)[None, :]
    d['c_triU'] = np.where(k <= q, NEG, 0.0).astype(np.float32)
    key = np.arange(T)[None, :]; jj = np.arange(64)[:, None]
    d['c_ind'] = (((key // 64) % 64) == jj).astype(np.float32)
    ncb, nsel = 511, 128
    cs_ = np.arange(ncb) * 16; ce = cs_ + 32
    ss = np.arange(nsel) * 64; se = ss + 64
    ov = np.clip(np.minimum(ce[:, None], se[None, :]) - np.maximum(cs_[:, None], ss[None, :]), 0, None) / 32.0
    sm = np.zeros((512, 128), np.float32); sm[:511] = ov
    d['c_selmap'] = np.ascontiguousarray(sm.reshape(4, 128, 128).transpose(1, 0, 2).reshape(128, 512))
    cm = np.zeros((128, 17, 128), np.float32)
    nn = np.arange(128)[:, None]; qq = np.arange(128)[None, :]
    for Dd in range(17):
        cm[:, Dd, :] = np.where(16 * nn - qq <= 128 * Dd - 31, 0.0, NEG)
    d['c_cmask'] = cm.reshape(128, 17 * 128)
    qq = np.arange(128)[:, None]; u = np.arange(256)[None, :]
    dlt = u - 128; own = qq // 64
    d['c_wk'] = (dlt < own).astype(np.float32)
    d['c_wa'] = np.where(dlt == own, 1e6, np.where(dlt > own, -1.0, 0.0)).astype(np.float32)
    _CONSTS = d
    return d


def inputs_nsa(inp, l, b, hp):
    w_in = inp['w_in'][l]
    offs = np.cumsum([0, 512, 512, 512, 8, 512, 768, 24, 1024, 1024])
    gq = hp // 2
    own = w_in[:, offs[4] + 128 * hp: offs[4] + 128 * hp + 128]
    oth_hp = 2 * gq + (1 - hp % 2)
    oth = w_in[:, offs[4] + 128 * oth_hp: offs[4] + 128 * oth_hp + 128]
    kv = [w_in[:, offs[5] + 128 * s + 64 * gq: offs[5] + 128 * s + 64 * gq + 64] for s in range(6)]
    kc, vc, ks, vs, kw, vw = kv
    gcols = w_in[:, offs[6] + 6 * hp: offs[6] + 6 * hp + 6]
    w_nsa = np.concatenate([own, oth, ks, ks, kw, kw, kc, vc, vs, vw, gcols], axis=1)
    assert w_nsa.shape[1] == NW
    pos = inp['cmp_pos'][l]
    posT = np.zeros((128, 32), np.float32)
    for m in range(2):
        pp = pos[m].reshape(16, 2, 64).reshape(16, 128)
        posT[:, 16 * m:16 * m + 16] = pp.T
    b1 = inp['cmp_b1'][l]
    b1T = np.stack([b1[0, 0:128], b1[0, 128:256], b1[1, 0:128], b1[1, 128:256]], axis=1)
    w2 = inp['cmp_w2'][l]
    b2 = inp['cmp_b2'][l]
    m = dict(
        w_nsa=np.ascontiguousarray(w_nsa),
        posT=posT, w1=np.ascontiguousarray(inp['cmp_w1'][l]), b1T=np.ascontiguousarray(b1T),
        w2kd=np.ascontiguousarray(np.concatenate([w2[0], w2[0]], axis=1)), w2v=np.ascontiguousarray(w2[1]),
        b2k=np.ascontiguousarray(np.concatenate([b2[0], b2[0]])[:, None]),
        b2v=np.ascontiguousarray(np.broadcast_to(b2[1][None, :], (128, 64))),
    )
    m.update(nsa_consts())
    return m


T = 8192
D = 1024
NCH = 16
NKT = 64
EPS = 1e-6
NEG = -30000.0


def build_A(do_fox=True, do_nsa=True, nqt_limit=None, stage=9):
    nc = bass.Bass("TRN2", target_bir_lowering=False)
    P = Prog(nc)
    dram = lambda name, shape, dt=F32, kind="ExternalInput": nc.dram_tensor(name, shape, dt, kind=kind).ap()
    xT = dram("xT", [D, T])
    gmix = dram("gmix", [128, 8])
    w_fm_fox = dram("w_fm_fox", [D, 256])
    w_tm_fox = dram("w_tm_fox", [D, 130])
    bfg = dram("bfg", [128, 2])
    c_tri = dram("c_tri", [128, 128])
    c_ident = dram("c_ident", [128, 128])
    c_ut = dram("c_ut", [128, 128])
    oaT = dram("oaT", [128, T], BF16, kind="ExternalOutput")
    if do_nsa:
        nsa_dram = nsa_declare(dram, nc)

    sb = lambda name, shape, dt=BF16: nc.alloc_sbuf_tensor(name, shape, dt)
    psS = [nc.alloc_psum_tensor(f"psS{i}", [128, 1024], F32) for i in range(2)]
    ps = [psS[0][:, 0:512], psS[0][:, 512:1024], psS[1][:, 0:512], psS[1][:, 512:1024]] + \
         [nc.alloc_psum_tensor(f"ps{i}", [128, 512], F32)[:] for i in range(4, 8)]

    g_sb = sb("g_sb", [128, 8], F32)
    tri = sb("tri", [128, 128]); ident = sb("ident", [128, 128]); ut = sb("ut", [128, 128])
    ones_bf = sb("ones_bf", [128, 128])
    eps_c = sb("eps_c", [128, 1], F32); one_c = sb("one_c", [128, 1], F32); zero_c = sb("zero_c", [128, 64], F32)
    w1b = sb("w1b", [128, 16, 256])
    wfm = w1b[:, 0:8, :]; wtm = w1b[:, 8:16, 0:130]
    bneg = sb("bneg", [128, 2], F32)
    P.dma('sp', lambda: nc.sync.dma_start(out=g_sb[:], in_=gmix[:, :]), w=['g'])
    P.dma('sp', lambda: nc.sync.dma_start(out=bneg[:], in_=bfg[:, :]), w=['bneg'])
    P.dma('pool', lambda: nc.gpsimd.dma_start(out=tri[:], in_=c_tri[:, :]), w=['tri'])
    P.dma('pool', lambda: nc.gpsimd.dma_start(out=ident[:], in_=c_ident[:, :]), w=['ident'])
    P.dma('pool', lambda: nc.gpsimd.dma_start(out=ut[:], in_=c_ut[:, :]), w=['ut'])
    P.dma('pool', lambda: nc.gpsimd.dma_start(out=wfm, in_=w_fm_fox.rearrange("(c p) n -> p c n", p=128)), w=['wfm'])
    P.dma('pool', lambda: nc.gpsimd.dma_start(out=wtm, in_=w_tm_fox.rearrange("(c p) n -> p c n", p=128)), w=['wtm'])
    P.dve(lambda: nc.vector.memset(ones_bf[:], 1.0), w=['ones'])
    P.dve(lambda: nc.vector.memset(eps_c[:], EPS), w=['eps'])
    P.dve(lambda: nc.vector.memset(one_c[:], 1.0), w=['one'])
    P.dve(lambda: nc.vector.memset(zero_c[:], 0.0), w=['zero'])
    P.dve(lambda: nc.vector.tensor_scalar(out=bneg[:], in0=bneg[:], scalar1=-1.0, scalar2=None, op0=ALU.mult), r=['bneg'], w=['bneg'])

    R0 = sb("R0", [128, T]); R1 = sb("R1", [128, T]); R2 = sb("R2", [128, 2 * T])
    qaT = R0; kaT = R1
    va = R2[:].rearrange("p (k j d) -> p k j d", k=NKT, j=2)
    ftm = sb("ftm", [128, NKT, 2], F32)
    P.pool(lambda: nc.gpsimd.memset(va[:, :, :, 64:128], 1.0), w=['va_ones'])

    xin0 = sb("xin0", [128, 8, 512], F32)
    xin = [xin0, xin0]
    sq = [sb(f"sq{i}", [128, 512]) for i in range(2)]
    rstd = sb("rstd", [128, 512], F32)
    hT = sb("hT", [128, 8, 512])

    def norm_chunk(c):
        xb = xin[c % 2]
        P.dma('sp', lambda: nc.sync.dma_start(out=xb[:], in_=xT[:, c * 512:(c + 1) * 512].rearrange("(k p) n -> p k n", p=128)),
              w=['xin'])
        for k in range(8):
            s = sq[k % 2]
            P.act(lambda k=k, s=s: nc.scalar.activation(out=s[:], in_=xb[:, k, :], func=AF.Square), r=['xin'], w=[f'sq{k%2}'])
            P.pe(lambda k=k, s=s: nc.tensor.matmul(ps[5][:], lhsT=ones_bf[:], rhs=s[:], start=(k == 0), stop=(k == 7)),
                 r=[f'sq{k%2}', 'ones'], w=['ps5'])
        P.act(lambda: nc.scalar.activation(out=rstd[:], in_=ps[5][:], func=AF.Sqrt, bias=eps_c[:], scale=1.0 / D),
              r=['ps5', 'eps'], w=['rstd'])
        P.dve(lambda: nc.vector.reciprocal(out=rstd[:], in_=rstd[:]), r=['rstd'], w=['rstd'])
        for k in range(8):
            P.dve(lambda k=k: nc.vector.scalar_tensor_tensor(out=hT[:, k, :], in0=xb[:, k, :], scalar=g_sb[:, k:k + 1], in1=rstd[:],
                                                            op0=ALU.mult, op1=ALU.mult),
                  r=['xin', 'rstd', 'g'], w=[f'hT{k}'])

    def fm_proj(wt, col0, bank, extra_r=()):
        for k in range(8):
            P.pe(lambda k=k: nc.tensor.matmul(ps[bank][:], lhsT=wt[:, k, col0:col0 + 128], rhs=hT[:, k, :], start=(k == 0), stop=(k == 7)),
                 r=[f'hT{k}', *extra_r], w=[f'ps{bank}'])

    pt = [sb(f"pt{i}", [128, 512]) for i in range(3)]
    rcp = [sb(f"rcp{i}", [64, 512], F32) for i in range(2)]
    osb = [sb(f"osb{i}", [64, 512]) for i in range(2)]
    if do_fox:
        for c in range(NCH):
            norm_chunk(c)
            cs = slice(c * 512, (c + 1) * 512)
            fm_proj(wfm, 0, 6, ['wfm'])
            P.act(lambda cs=cs: nc.scalar.activation(out=qaT[:, cs], in_=ps[6][:], func=AF.Copy, scale=0.125), r=['ps6'], w=[f'qa{c}'])
            fm_proj(wfm, 128, 7, ['wfm'])
            P.dve(lambda cs=cs: nc.vector.tensor_copy(out=kaT[:, cs], in_=ps[7][:]), r=['ps7'], w=[f'ka{c}'])
            for t in range(4):
                for k in range(8):
                    P.pe(lambda k=k, t=t: nc.tensor.matmul(ps[6][:, t * 128:(t + 1) * 128], lhsT=hT[:, k, t * 128:(t + 1) * 128],
                                                           rhs=wtm[:, k, 0:128], start=(k == 0), stop=(k == 7)),
                         r=[f'hT{k}', 'wtm'], w=['ps6'])
            for t in range(4):
                for k in range(8):
                    P.pe(lambda k=k, t=t: nc.tensor.matmul(ps[7][:, t * 2:(t + 1) * 2], lhsT=hT[:, k, t * 128:(t + 1) * 128],
                                                           rhs=wtm[:, k, 128:130], start=(k == 0), stop=(k == 7)),
                         r=[f'hT{k}', 'wtm'], w=['ps7'])
            P.act(lambda c=c: nc.scalar.copy(out=va[:, 4 * c:4 * c + 4, :, 0:64],
                                             in_=ps[6][:].rearrange("p (t j d) -> p t j d", t=4, j=2)),
                  r=['ps6', 'va_ones'], w=[f'va{c}'])
            P.dve(lambda c=c: nc.vector.tensor_copy(out=ftm[:, 4 * c:4 * c + 4, :], in_=ps[7][:, 0:8].rearrange("p (t j) -> p t j", j=2)),
                  r=['ps7'], w=['ftm'])

        lf = sb("lf", [128, NKT, 2], F32)
        lhi = sb("lhi", [128, 128]); llo = sb("llo", [128, 128])
        ccol = sb("ccol", [128, NKT, 2], F32)
        incl = sb("incl", [128, NKT, 2], F32)
        tot = sb("tot", [128, NKT, 2], F32)
        for j in range(2):
            P.act(lambda j=j: nc.scalar.activation(out=lf[:, :, j], in_=ftm[:, :, j], func=AF.Exp, bias=bneg[:, j:j + 1], scale=-1.0),
                  r=['ftm', 'bneg'], w=['lf'])
        P.act(lambda: nc.scalar.activation(out=lf[:], in_=lf[:], func=AF.Ln, bias=one_c[:], scale=1.0), r=['lf', 'one'], w=['lf'])
        P.dve(lambda: nc.vector.tensor_scalar(out=lf[:], in0=lf[:], scalar1=-1.0, scalar2=None, op0=ALU.mult), r=['lf'], w=['lf'])
        lf2 = lf[:].rearrange("p a b -> p (a b)")
        P.dve(lambda: nc.vector.tensor_copy(out=lhi[:], in_=lf2), r=['lf'], w=['lhi'])
        P.dve(lambda: nc.vector.tensor_tensor(out=llo[:], in0=lf2, in1=lhi[:], op=ALU.subtract), r=['lf', 'lhi'], w=['llo'])
        P.pe(lambda: nc.tensor.matmul(ps[5][:, 0:128], lhsT=ut[:], rhs=lhi[:], start=True, stop=False), r=['ut', 'lhi'], w=['ps5'])
        P.pe(lambda: nc.tensor.matmul(ps[5][:, 0:128], lhsT=ut[:], rhs=llo[:], start=False, stop=True), r=['ut', 'llo'], w=['ps5'])
        P.pe(lambda: nc.tensor.matmul(ps[5][:, 128:256], lhsT=ones_bf[:], rhs=lhi[:], start=True, stop=False), r=['ones', 'lhi'], w=['ps5'])
        P.pe(lambda: nc.tensor.matmul(ps[5][:, 128:256], lhsT=ones_bf[:], rhs=llo[:], start=False, stop=True), r=['ones', 'llo'], w=['ps5'])
        P.dve(lambda: nc.vector.tensor_copy(out=tot[:].rearrange("p a b -> p (a b)"), in_=ps[5][:, 128:256]), r=['ps5'], w=['tot'])
        for j in range(2):
            P.dve(lambda j=j: nc.vector.tensor_tensor_scan(out=incl[:, :, j], data0=tot[:, :, j], data1=zero_c[:, 0:NKT], initial=0.0,
                                                           op0=ALU.add, op1=ALU.add),
                  r=['tot', 'zero'], w=['incl'])
        P.dve(lambda: nc.vector.tensor_tensor(out=ccol[:].rearrange("p a b -> p (a b)"), in0=ps[5][:, 0:128],
                                              in1=incl[:].rearrange("p a b -> p (a b)"), op=ALU.add), r=['ps5', 'incl'], w=['ccol'])
        P.dve(lambda: nc.vector.tensor_tensor(out=ccol[:], in0=ccol[:], in1=tot[:], op=ALU.subtract), r=['ccol', 'tot'], w=['ccol'])

        biasT = [sb(f"biasT{i}", [128, NKT], F32) for i in range(2)]
        qpz = [[sb(f"qpz{j}{b}", [128, 512]) for b in range(2)] for j in range(2)]
        for j in range(2):
            for b in range(2):
                P.pool(lambda j=j, b=b: nc.gpsimd.memset(qpz[j][b][:], 0.0), w=[f'qpz{j}{b}'])
        it = 0
        fin = 0
        nqt = NCH if nqt_limit is None else nqt_limit
        for j in range(2):
            hs = slice(64 * j, 64 * j + 64)
            for qt in range(nqt):
                nk = 4 * qt + 4
                bt = biasT[fin % 2]
                ob = 3 + fin % 2
                P.dve(lambda j=j, qt=qt, nk=nk, bt=bt: nc.vector.tensor_scalar(
                    out=bt[:, 0:nk], in0=ccol[:, 0:nk, j], scalar1=-1.0, scalar2=incl[:, 4 * qt + 3, j:j + 1], op0=ALU.mult, op1=ALU.add),
                    r=['ccol', 'incl'], w=[f'biasT{fin%2}'])
                qs = slice(qt * 512, (qt + 1) * 512)
                qp = qpz[j][qt % 2]; qptok = f'qpz{j}{qt%2}'
                P.pool(lambda qp=qp, hs=hs, qs=qs: nc.gpsimd.tensor_copy(out=qp[hs, :], in_=qaT[hs, qs]), r=[f'qa{qt}'], w=[qptok])

                def qk(kt, i, qt=qt, j=j, bt=bt, hs=hs, ob=ob, nk=nk, fin=fin, qp=qp, qptok=qptok):
                    r_ = kt - 4 * qt
                    c0 = 128 * r_ if r_ > 0 else 0
                    diag = r_ >= 0
                    S = ps[i % 3]
                    P.pe(lambda: nc.tensor.matmul(S[:, c0:512], lhsT=kaT[:, kt * 128:(kt + 1) * 128], rhs=qp[:, c0:512],
                                                  start=True, stop=not diag),
                         r=[f'ka{kt//4}', qptok], w=[f'ps{i%3}'])
                    if diag:
                        P.pe(lambda: nc.tensor.matmul(S[:, c0:c0 + 128], lhsT=ident[:], rhs=tri[:], start=False, stop=True),
                             r=['ident', 'tri'], w=[f'ps{i%3}'])
                    P.act(lambda: nc.scalar.activation(out=pt[i % 3][:, c0:512], in_=S[:, c0:512], func=AF.Exp, bias=bt[:, kt:kt + 1], scale=1.0),
                          r=[f'ps{i%3}', f'biasT{fin%2}'], w=[f'pt{i%3}'])

                def pv(kt, i, qt=qt, j=j, bt=bt, hs=hs, ob=ob, nk=nk, fin=fin):
                    r_ = kt - 4 * qt
                    c0 = 128 * r_ if r_ > 0 else 0
                    P.pe(lambda: nc.tensor.matmul(ps[ob][:, c0:512], lhsT=va[:, kt, j, :], rhs=pt[i % 3][:, c0:512],
                                                  start=(kt == 0), stop=(kt == nk - 1)),
                         r=[f'pt{i%3}', f'va{kt//4}'], w=[f'ps{ob}'])

                base = it
                for kt in range(min(2, nk)):
                    qk(kt, base + kt)
                for kt in range(nk):
                    if kt + 2 < nk:
                        qk(kt + 2, base + kt + 2)
                    pv(kt, base + kt)
                it += nk
                rc = rcp[fin % 2]; o_ = osb[fin % 2]
                P.dve(lambda rc=rc, ob=ob: nc.vector.reciprocal(out=rc[:], in_=ps[ob][64:128, :]), r=[f'ps{ob}'], w=[f'rcp{fin%2}'])
                P.dve(lambda rc=rc, ob=ob, o_=o_: nc.vector.tensor_tensor(out=o_[:], in0=ps[ob][0:64, :], in1=rc[:], op=ALU.mult),
                      r=[f'ps{ob}', f'rcp{fin%2}'], w=[f'osb{fin%2}'])
                P.dma('pool', lambda o_=o_, qs=qs, hs=hs: nc.gpsimd.dma_start(out=oaT[hs, qs], in_=o_[:]), r=[f'osb{fin%2}'])
                fin += 1
    if do_nsa:
        nsa_emit(locals())
    stats = P.finalize()
    return nc, stats


def host_consts():
    k = np.arange(128)[:, None]; q = np.arange(128)[None, :]
    return dict(
        c_tri=np.where(k > q, NEG, 0.0).astype(np.float32),
        c_ident=np.eye(128, dtype=np.float32),
        c_ut=(k <= q).astype(np.float32),
    )


def inputs_A(inp, l, b, hp, x_b=None, nsa=True):
    x = inp['x'][b] if x_b is None else x_b
    w_in = inp['w_in'][l]
    offs = np.cumsum([0, 512, 512, 512, 8, 512, 768, 24, 1024, 1024])
    qa = w_in[:, offs[0] + 128 * hp: offs[0] + 128 * hp + 128]
    ka = w_in[:, offs[1] + 128 * hp: offs[1] + 128 * hp + 128]
    va = w_in[:, offs[2] + 128 * hp: offs[2] + 128 * hp + 128]
    f = w_in[:, offs[3] + 2 * hp: offs[3] + 2 * hp + 2]
    m = dict(
        xT=np.ascontiguousarray(x.T),
        gmix=np.ascontiguousarray(inp['norm_mix'][l].reshape(8, 128).T),
        w_fm_fox=np.ascontiguousarray(np.concatenate([qa, ka], axis=1)),
        w_tm_fox=np.ascontiguousarray(np.concatenate([va, f], axis=1)),
        bfg=np.ascontiguousarray(np.broadcast_to(inp['b_forget'][l][2 * hp:2 * hp + 2][None, :], (128, 2))),
    )
    m.update(host_consts())
    if nsa:
        m.update(inputs_nsa(inp, l, b, hp))
    return m


D = 1024
NTB = 2048
EPS = 1e-6


def _norm(P, nc, ps, bank, xt, xtok, g_sb, gtok, out_bf, otok, ones_bf, eps_c, sq, rstd, n):
    for k in range(8):
        s = sq[k % 2]
        P.act(lambda k=k, s=s: nc.scalar.activation(out=s[:, 0:n], in_=xt[:, k, 0:n], func=AF.Square), r=[xtok], w=[f'sq{k%2}'])
        P.pe(lambda k=k, s=s: nc.tensor.matmul(ps[bank][:, 0:n], lhsT=ones_bf[:], rhs=s[:, 0:n], start=(k == 0), stop=(k == 7)),
             r=[f'sq{k%2}', 'ones'], w=[f'ps{bank}'])
    P.act(lambda: nc.scalar.activation(out=rstd[:, 0:n], in_=ps[bank][:, 0:n], func=AF.Sqrt, bias=eps_c[:], scale=1.0 / D),
          r=[f'ps{bank}', 'eps'], w=['rstd'])
    P.dve(lambda: nc.vector.reciprocal(out=rstd[:, 0:n], in_=rstd[:, 0:n]), r=['rstd'], w=['rstd'])
    for k in range(8):
        P.dve(lambda k=k: nc.vector.scalar_tensor_tensor(out=out_bf[:, k, 0:n], in0=xt[:, k, 0:n], scalar=g_sb[:, k:k + 1], in1=rstd[:, 0:n],
                                                        op0=ALU.mult, op1=ALU.mult), r=[xtok, 'rstd', gtok], w=[otok])


def build_B1():
    nc = bass.Bass("TRN2", target_bir_lowering=False)
    P = Prog(nc)
    dram = lambda name, shape, dt=F32, kind="ExternalInput": nc.dram_tensor(name, shape, dt, kind=kind).ap()
    xT = dram("xT", [D, NTB]); oaT = dram("oaT", [512, NTB], BF16); onT = dram("onT", [512, NTB], BF16)
    gmix = dram("gmix", [128, 8]); gmlp = dram("gmlp", [128, 8])
    w_of = dram("w_of", [512, D]); w_on = dram("w_on", [512, D])
    w_ga = dram("w_ga", [D, D]); w_gb = dram("w_gb", [D, D]); w_out = dram("w_out", [D, D])
    x1T = dram("x1T", [D, NTB], F32, kind="ExternalOutput")
    h2T = dram("h2T", [D, NTB], BF16, kind="ExternalOutput")
    sb = lambda name, shape, dt=BF16: nc.alloc_sbuf_tensor(name, shape, dt)
    ps = [nc.alloc_psum_tensor(f"ps{i}", [128, 512], F32) for i in range(8)]
    g = nc.gpsimd; sp = nc.sync
    gm = sb("gm", [128, 8], F32); gl = sb("gl", [128, 8], F32)
    ones_bf = sb("ones_bf", [128, 128]); eps_c = sb("eps_c", [128, 1], F32)
    wof = sb("wof", [128, 4, D]); won = sb("won", [128, 4, D])
    wga = sb("wga", [128, 8, D]); wgb = sb("wgb", [128, 8, D]); wout = sb("wout", [128, 8, D])
    P.dma('sp', lambda: sp.dma_start(out=gm[:], in_=gmix[:, :]), w=['gm'])
    P.dma('sp', lambda: sp.dma_start(out=gl[:], in_=gmlp[:, :]), w=['gl'])
    P.dve(lambda: nc.vector.memset(ones_bf[:], 1.0), w=['ones'])
    P.dve(lambda: nc.vector.memset(eps_c[:], EPS), w=['eps'])
    for nm, wt, src, kc in (('wof', wof, w_of, 4), ('won', won, w_on, 4), ('wga', wga, w_ga, 8), ('wgb', wgb, w_gb, 8), ('wout', wout, w_out, 8)):
        for k in range(kc):
            P.dma('pool', lambda wt=wt, src=src, k=k: g.dma_start(out=wt[:, k, :], in_=src[k * 128:(k + 1) * 128, :]), w=[nm])
    xt = sb("xt", [128, 8, 512], F32); oat = sb("oat", [128, 4, 512]); ont = sb("ont", [128, 4, 512])
    sq = [sb(f"sq{i}", [128, 512]) for i in range(2)]
    rstd = sb("rstd", [128, 512], F32)
    hT = sb("hT", [128, 8, 512]); mixT = sb("mixT", [128, 8, 512])
    sa = sb("sa", [128, 512], F32); sb_ = sb("sb_", [128, 512], F32); ma = sb("ma", [128, 512], F32); mb = sb("mb", [128, 512], F32)
    x1 = sb("x1", [128, 8, 512], F32); h2 = sb("h2", [128, 8, 512])
    for t in range(NTB // 512):
        ts_ = slice(t * 512, (t + 1) * 512)
        P.dma('sp', lambda ts_=ts_: sp.dma_start(out=xt[:], in_=xT[:, ts_].rearrange("(k p) n -> p k n", p=128)), w=['xt'])
        P.dma('sp', lambda ts_=ts_: sp.dma_start(out=oat[:], in_=oaT[:, ts_].rearrange("(k p) n -> p k n", p=128)), w=['oat'])
        P.dma('sp', lambda ts_=ts_: sp.dma_start(out=ont[:], in_=onT[:, ts_].rearrange("(k p) n -> p k n", p=128)), w=['ont'])
        _norm(P, nc, ps, 7, xt, 'xt', gm, 'gm', hT, 'hT', ones_bf, eps_c, sq, rstd, 512)
        for dc in range(8):
            ds_ = slice(dc * 128, (dc + 1) * 128)
            b0 = 4 * (dc % 2)
            for k in range(4):
                P.pe(lambda k=k, ds_=ds_, b0=b0: nc.tensor.matmul(ps[b0][:], lhsT=wof[:, k, ds_], rhs=oat[:, k, :], start=(k == 0), stop=(k == 3)),
                     r=['wof', 'oat'], w=[f'ps{b0}'])
            for k in range(8):
                P.pe(lambda k=k, ds_=ds_, b0=b0: nc.tensor.matmul(ps[b0 + 1][:], lhsT=wga[:, k, ds_], rhs=hT[:, k, :], start=(k == 0), stop=(k == 7)),
                     r=['wga', 'hT'], w=[f'ps{b0+1}'])
            for k in range(4):
                P.pe(lambda k=k, ds_=ds_, b0=b0: nc.tensor.matmul(ps[b0 + 2][:], lhsT=won[:, k, ds_], rhs=ont[:, k, :], start=(k == 0), stop=(k == 3)),
                     r=['won', 'ont'], w=[f'ps{b0+2}'])
            for k in range(8):
                P.pe(lambda k=k, ds_=ds_, b0=b0: nc.tensor.matmul(ps[b0 + 3][:], lhsT=wgb[:, k, ds_], rhs=hT[:, k, :], start=(k == 0), stop=(k == 7)),
                     r=['wgb', 'hT'], w=[f'ps{b0+3}'])
            P.act(lambda b0=b0: nc.scalar.activation(out=sa[:], in_=ps[b0 + 1][:], func=AF.Sigmoid), r=[f'ps{b0+1}'], w=['sa'])
            P.act(lambda b0=b0: nc.scalar.activation(out=sb_[:], in_=ps[b0 + 3][:], func=AF.Sigmoid), r=[f'ps{b0+3}'], w=['sb_'])
            P.dve(lambda b0=b0: nc.vector.tensor_tensor(out=ma[:], in0=ps[b0][:], in1=sa[:], op=ALU.mult), r=[f'ps{b0}', 'sa'], w=['ma'])
            P.dve(lambda b0=b0: nc.vector.tensor_tensor(out=mb[:], in0=ps[b0 + 2][:], in1=sb_[:], op=ALU.mult), r=[f'ps{b0+2}', 'sb_'], w=['mb'])
            P.pool(lambda dc=dc: nc.gpsimd.tensor_tensor(out=mixT[:, dc, :], in0=ma[:], in1=mb[:], op=ALU.add), r=['ma', 'mb'], w=['mixT'])
        for dc in range(8):
            ds_ = slice(dc * 128, (dc + 1) * 128)
            b = dc % 2
            for k in range(8):
                P.pe(lambda k=k, ds_=ds_, b=b: nc.tensor.matmul(ps[b][:], lhsT=wout[:, k, ds_], rhs=mixT[:, k, :], start=(k == 0), stop=(k == 7)),
                     r=['wout', 'mixT'], w=[f'ps{b}'])
            P.dve(lambda dc=dc, b=b: nc.vector.tensor_tensor(out=x1[:, dc, :], in0=ps[b][:], in1=xt[:, dc, :], op=ALU.add), r=[f'ps{b}', 'xt'], w=['x1'])
        P.dma('pool', lambda ts_=ts_: g.dma_start(out=x1T[:, ts_].rearrange("(k p) n -> p k n", p=128), in_=x1[:]), r=['x1'])
        _norm(P, nc, ps, 7, x1, 'x1', gl, 'gl', h2, 'h2', ones_bf, eps_c, sq, rstd, 512)
        P.dma('pool', lambda ts_=ts_: g.dma_start(out=h2T[:, ts_].rearrange("(k p) n -> p k n", p=128), in_=h2[:]), r=['h2'])
    return nc, P.finalize()


def build_B2():
    nc = bass.Bass("TRN2", target_bir_lowering=False)
    P = Prog(nc)
    dram = lambda name, shape, dt=F32, kind="ExternalInput": nc.dram_tensor(name, shape, dt, kind=kind).ap()
    x1T = dram("x1T", [D, NTB]); h2T = dram("h2T", [D, NTB], BF16)
    gfin = dram("gfin", [128, 8])
    w_up = dram("w_up", [D, 4096]); w_down = dram("w_down", [4096, D])
    x2T = dram("x2T", [D, NTB], F32, kind="ExternalOutput")
    yT = dram("yT", [D, NTB], F32, kind="ExternalOutput")
    sb = lambda name, shape, dt=BF16: nc.alloc_sbuf_tensor(name, shape, dt)
    ps = [nc.alloc_psum_tensor(f"ps{i}", [128, 512], F32) for i in range(8)]
    g = nc.gpsimd; sp = nc.sync
    gf = sb("gf", [128, 8], F32)
    ones_bf = sb("ones_bf", [128, 128]); eps_c = sb("eps_c", [128, 1], F32)
    wup = sb("wup", [128, 8, 4096]); wdn = sb("wdn", [128, 32, D])
    P.dma('sp', lambda: sp.dma_start(out=gf[:], in_=gfin[:, :]), w=['gf'])
    P.dve(lambda: nc.vector.memset(ones_bf[:], 1.0), w=['ones'])
    P.dve(lambda: nc.vector.memset(eps_c[:], EPS), w=['eps'])
    for k in range(8):
        for hh in range(2):
            P.dma('pool', lambda k=k, hh=hh: g.dma_start(out=wup[:, k, hh * 2048:(hh + 1) * 2048], in_=w_up[k * 128:(k + 1) * 128, hh * 2048:(hh + 1) * 2048]), w=['wup'])
    for k in range(32):
        P.dma('pool', lambda k=k: g.dma_start(out=wdn[:, k, :], in_=w_down[k * 128:(k + 1) * 128, :]), w=['wdn'])
    N = 256
    h2 = sb("h2", [128, 8, N]); x1 = sb("x1", [128, 8, N], F32)
    uT = sb("uT", [128, 32, N]); rl = [sb(f"rl{i}", [128, N], F32) for i in range(2)]
    x2 = sb("x2", [128, 8, N], F32); yo = sb("yo", [128, 8, N], F32)
    sq = [sb(f"sq{i}", [128, 512]) for i in range(2)]
    rstd = sb("rstd", [128, 512], F32)
    for t in range(NTB // N):
        ts_ = slice(t * N, (t + 1) * N)
        P.dma('sp', lambda ts_=ts_: sp.dma_start(out=h2[:], in_=h2T[:, ts_].rearrange("(k p) n -> p k n", p=128)), w=['h2'])
        P.dma('sp', lambda ts_=ts_: sp.dma_start(out=x1[:], in_=x1T[:, ts_].rearrange("(k p) n -> p k n", p=128)), w=['x1'])
        for fc in range(32):
            b = fc % 4
            for k in range(8):
                P.pe(lambda k=k, fc=fc, b=b: nc.tensor.matmul(ps[b][:, 0:N], lhsT=wup[:, k, fc * 128:(fc + 1) * 128], rhs=h2[:, k, :],
                                                              start=(k == 0), stop=(k == 7)), r=['wup', 'h2'], w=[f'ps{b}'])
            r_ = rl[fc % 2]
            P.act(lambda b=b, r_=r_: nc.scalar.activation(out=r_[:], in_=ps[b][:, 0:N], func=AF.Relu), r=[f'ps{b}'], w=[f'rl{fc%2}'])
            if fc % 2 == 0:
                P.dve(lambda fc=fc, r_=r_: nc.vector.tensor_tensor(out=uT[:, fc, :], in0=r_[:], in1=r_[:], op=ALU.mult), r=[f'rl{fc%2}'], w=[f'uT{fc}'])
            else:
                P.pool(lambda fc=fc, r_=r_: nc.gpsimd.tensor_tensor(out=uT[:, fc, :], in0=r_[:], in1=r_[:], op=ALU.mult), r=[f'rl{fc%2}'], w=[f'uT{fc}'])
        for dc in range(8):
            b = 4 + dc % 2
            for fc in range(32):
                P.pe(lambda fc=fc, dc=dc, b=b: nc.tensor.matmul(ps[b][:, 0:N], lhsT=wdn[:, fc, dc * 128:(dc + 1) * 128], rhs=uT[:, fc, :],
                                                                start=(fc == 0), stop=(fc == 31)), r=['wdn', f'uT{fc}'], w=[f'ps{b}'])
            P.dve(lambda dc=dc, b=b: nc.vector.tensor_tensor(out=x2[:, dc, :], in0=ps[b][:, 0:N], in1=x1[:, dc, :], op=ALU.add), r=[f'ps{b}', 'x1'], w=['x2'])
        P.dma('pool', lambda ts_=ts_: g.dma_start(out=x2T[:, ts_].rearrange("(k p) n -> p k n", p=128), in_=x2[:]), r=['x2'])
        for k in range(8):
            s = sq[k % 2]
            P.act(lambda k=k, s=s: nc.scalar.activation(out=s[:, 0:N], in_=x2[:, k, :], func=AF.Square), r=['x2'], w=[f'sq{k%2}'])
            P.pe(lambda k=k, s=s: nc.tensor.matmul(ps[7][:, 0:N], lhsT=ones_bf[:], rhs=s[:, 0:N], start=(k == 0), stop=(k == 7)),
                 r=[f'sq{k%2}', 'ones'], w=['ps7'])
        P.act(lambda: nc.scalar.activation(out=rstd[:, 0:N], in_=ps[7][:, 0:N], func=AF.Sqrt, bias=eps_c[:], scale=1.0 / D), r=['ps7', 'eps'], w=['rstd'])
        P.dve(lambda: nc.vector.reciprocal(out=rstd[:, 0:N], in_=rstd[:, 0:N]), r=['rstd'], w=['rstd'])
        for k in range(8):
            P.dve(lambda k=k: nc.vector.scalar_tensor_tensor(out=yo[:, k, :], in0=x2[:, k, :], scalar=gf[:, k:k + 1], in1=rstd[:, 0:N],
                                                            op0=ALU.mult, op1=ALU.mult), r=['x2', 'rstd', 'gf'], w=['yo'])
        P.dma('pool', lambda ts_=ts_: g.dma_start(out=yT[:, ts_].rearrange("(k p) n -> p k n", p=128), in_=yo[:]), r=['yo'])
    return nc, P.finalize()


_PROGS = {}


def _prog(name, fn):
    if name not in _PROGS:
        _PROGS[name] = fn()[0]
    return _PROGS[name]


def _lay(gv):
    return np.ascontiguousarray(np.asarray(gv, np.float32).reshape(8, 128).T)


def kernel(**inputs):
    import ml_dtypes
    from concourse.bass_utils import run_bass_kernel_spmd
    inp = {k: np.asarray(v) for k, v in inputs.items()}
    B = 2
    cores = list(range(8))
    offs = np.cumsum([0, 512, 512, 512, 8, 512, 768, 24, 1024, 1024])
    x = [np.ascontiguousarray(inp['x'][b]) for b in range(B)]
    y = None
    for l in range(2):
        ncA = _prog('A', lambda: build_A(True, True))
        mapsA = [inputs_A(inp, l, c // 4, c % 4, x_b=x[c // 4]) for c in cores]
        resA = run_bass_kernel_spmd(ncA, mapsA, core_ids=cores).results
        oa = [np.concatenate([resA[4 * b + hp]['oaT'] for hp in range(4)], axis=0) for b in range(B)]
        on = [np.concatenate([resA[4 * b + hp]['onT'] for hp in range(4)], axis=0) for b in range(B)]
        w_in = inp['w_in'][l]
        ncB1 = _prog('B1', build_B1)
        mapsB1 = []
        for c in cores:
            b, r = c // 4, c % 4
            rs = slice(2048 * r, 2048 * r + 2048)
            mapsB1.append(dict(xT=np.ascontiguousarray(x[b][rs].T), oaT=np.ascontiguousarray(oa[b][:, rs]), onT=np.ascontiguousarray(on[b][:, rs]),
                               gmix=_lay(inp['norm_mix'][l]), gmlp=_lay(inp['norm_mlp'][l]),
                               w_of=np.ascontiguousarray(inp['w_o_fox'][l]), w_on=np.ascontiguousarray(inp['w_o_nsa'][l]),
                               w_ga=np.ascontiguousarray(w_in[:, offs[7]:offs[8]]), w_gb=np.ascontiguousarray(w_in[:, offs[8]:offs[9]]),
                               w_out=np.ascontiguousarray(inp['w_out'][l])))
        resB1 = run_bass_kernel_spmd(ncB1, mapsB1, core_ids=cores).results
        ncB2 = _prog('B2', build_B2)
        mapsB2 = [dict(x1T=resB1[c]['x1T'], h2T=resB1[c]['h2T'], gfin=_lay(inp['norm_final']),
                       w_up=np.ascontiguousarray(inp['w_up'][l]), w_down=np.ascontiguousarray(inp['w_down'][l])) for c in cores]
        resB2 = run_bass_kernel_spmd(ncB2, mapsB2, core_ids=cores).results
        x = [np.ascontiguousarray(np.concatenate([resB2[4 * b + r]['x2T'].T for r in range(4)], axis=0)) for b in range(B)]
        y = np.stack([np.concatenate([resB2[4 * b + r]['yT'].T for r in range(4)], axis=0) for b in range(B)], axis=0)
    return np.ascontiguousarray(y.astype(np.float32))
```

```python
import numpy as np
import concourse.bass as bass
import concourse.mybir as mybir

F32 = mybir.dt.float32
BF16 = mybir.dt.bfloat16
AF = mybir.ActivationFunctionType
ALU = mybir.AluOpType


class Prog:
    NDMA = 6

    def __init__(self, nc):
        self.nc = nc
        self.eng = {'pe': nc.tensor, 'act': nc.scalar, 'dve': nc.vector,
                    'pool': nc.gpsimd, 'sp': nc.sync}
        self.ops = []

    def add(self, eng, fn, r=(), w=(), dma=False):
        self.ops.append((eng, fn, tuple(r), tuple(w), dma))

    def pe(self, fn, r=(), w=()): self.add('pe', fn, r, w)
    def act(self, fn, r=(), w=()): self.add('act', fn, r, w)
    def dve(self, fn, r=(), w=()): self.add('dve', fn, r, w)
    def pool(self, fn, r=(), w=()): self.add('pool', fn, r, w)
    def dma(self, q, fn, r=(), w=()): self.add(q, fn, r, w, True)

    def finalize(self):
        nc = self.nc
        ops = self.ops
        n = len(ops)
        last_w = {}
        readers = {}
        deps = [None] * n
        for i, (eng, fn, r, w, dma) in enumerate(ops):
            d = {}
            for t in r:
                j = last_w.get(t)
                if j is not None:
                    d[j] = 'raw'
            for t in w:
                j = last_w.get(t)
                if j is not None and j not in d:
                    d[j] = 'waw'
                for j in readers.get(t, ()):
                    if j not in d:
                        d[j] = 'war'
            for t in r:
                readers.setdefault(t, []).append(i)
            for t in w:
                last_w[t] = i
                readers[t] = []
            keep = []
            for j, kind in d.items():
                if j == i:
                    continue
                je, _, _, _, jdma = ops[j]
                if jdma:
                    keep.append(j)
                elif je == eng and not dma:
                    if (kind == 'raw' and eng != 'pe') or eng == 'pool':
                        keep.append(j)
                elif je == eng and dma:
                    keep.append(j)
                else:
                    keep.append(j)
            deps[i] = keep
        signal = [False] * n
        for i in range(n):
            for j in deps[i]:
                signal[j] = True
        csem = {e: nc.alloc_semaphore(f"c_{e}") for e in ['pe', 'act', 'dve', 'pool']}
        dsem = {q: [nc.alloc_semaphore(f"d_{q}{k}") for k in range(self.NDMA)] for q in ['sp', 'pool']}
        ccount = {e: 0 for e in csem}
        dcount = {q: 0 for q in dsem}
        semval = [None] * n
        for i, (eng, fn, r, w, dma) in enumerate(ops):
            if dma:
                k = dcount[eng]
                dcount[eng] += 1
                semval[i] = (dsem[eng][k % self.NDMA], 16 * (k // self.NDMA + 1), k)
            elif signal[i]:
                ccount[eng] += 1
                semval[i] = (csem[eng], ccount[eng], None)
        waited = {e: {} for e in self.eng}
        nwaits = 0
        for i, (eng, fn, r, w, dma) in enumerate(ops):
            E = self.eng[eng]
            wl = {}
            for j in deps[i]:
                s, v, _ = semval[j]
                key = id(s)
                if waited[eng].get(key, 0) >= v:
                    continue
                if key not in wl or wl[key][1] < v:
                    wl[key] = (s, v)
            if dma:
                s, v, k = semval[i]
                if k >= self.NDMA:
                    key = id(s)
                    pv = v - 16
                    if waited[eng].get(key, 0) < pv and (key not in wl or wl[key][1] < pv):
                        wl[key] = (s, pv)
            for key, (s, v) in wl.items():
                E.wait_ge(s, v)
                waited[eng][key] = v
                nwaits += 1
            ins = fn()
            if dma:
                ins.then_inc(semval[i][0], 16)
            elif signal[i]:
                ins.then_inc(semval[i][0], 1)
        E = nc.sync
        for q in dsem:
            tot = dcount[q]
            for k in range(min(tot, self.NDMA)):
                cnt = (tot - 1 - k) // self.NDMA + 1
                E.wait_ge(dsem[q][k], 16 * cnt)
        end = nc.alloc_semaphore("c_end")
        for e in ['pe', 'act', 'dve', 'pool']:
            self.eng[e].drain().then_inc(end, 1)
        E.wait_ge(end, 4)
        for s_ in list(csem.values()) + [x for q in dsem for x in dsem[q]] + [end]:
            E.sem_clear(s_)
        self.stats = dict(n=n, nwaits=nwaits, counts=dict(ccount), dmas=dict(dcount))
        return self.stats


T = 8192
D = 1024
NCH = 16
NKT = 64
NEG = -30000.0
NW = 774


def nsa_declare(dram, nc):
    d = {}
    d['w_nsa'] = dram("w_nsa", [D, NW])
    d['cosT'] = dram("cosT", [128, T]); d['sinT'] = dram("sinT", [128, T])
    d['cosC'] = dram("cosC", [128, 512]); d['sinC'] = dram("sinC", [128, 512])
    d['c_perm'] = dram("c_perm", [128, 128]); d['c_triU'] = dram("c_triU", [128, 128])
    d['c_ind'] = dram("c_ind", [64, T])
    d['c_selmap'] = dram("c_selmap", [128, 4 * 128])
    d['c_cmask'] = dram("c_cmask", [128, 17 * 128])
    d['c_wk'] = dram("c_wk", [128, 256]); d['c_wa'] = dram("c_wa", [128, 256])
    d['posT'] = dram("posT", [128, 32])
    d['w1'] = dram("w1", [2, 2048, 256])
    d['b1T'] = dram("b1T", [128, 4])
    d['w2kd'] = dram("w2kd", [256, 128]); d['w2v'] = dram("w2v", [256, 64])
    d['b2k'] = dram("b2k", [128, 1]); d['b2v'] = dram("b2v", [128, 64])
    d['kvc'] = nc.dram_tensor("kvc", [128, T], BF16, kind="Internal").ap()
    d['onT'] = dram("onT", [128, T], BF16, kind="ExternalOutput")
    return d


def nsa_emit(L):
    nc = L['nc']; P = L['P']; ps = L['ps']; sb = L['sb']; dd = L['nsa_dram']
    R0 = L['R0']; R1 = L['R1']; R2 = L['R2']; xin0 = L['xin0']; hT = L['hT']
    ident = L['ident']; tri = L['tri']; norm_chunk = L['norm_chunk']; fm_proj = L['fm_proj']
    nq_limit = L['nqt_limit']
    ksT2 = R0; kwT2 = R1
    qbT = R2[:].rearrange("p (i t) -> p i t", i=2)

    wn = sb("wn", [128, 8, NW])
    perm = sb("perm", [128, 128]); triU = sb("triU", [128, 128])
    selmap = sb("selmap", [128, 4, 128]); cmask = sb("cmask", [128, 17, 128])
    wk = sb("wk", [128, 256], F32); wa = sb("wa", [128, 256], F32)
    posT = sb("posT_sb", [128, 32])
    b1T = sb("b1T_sb", [128, 4], F32)
    w2kd = sb("w2kd_sb", [128, 2, 128]); w2v = sb("w2v_sb", [128, 2, 64])
    b2k = sb("b2k_sb", [128, 1], F32); b2v = sb("b2v_sb", [128, 64], F32)
    cosC = sb("cosC_sb", [128, 512], F32); sinC = sb("sinC_sb", [128, 512], F32)
    g = nc.gpsimd; sp = nc.sync
    P.dma('pool', lambda: g.dma_start(out=wn[:], in_=dd['w_nsa'].rearrange("(c p) n -> p c n", p=128)), w=['wn'])
    P.dma('pool', lambda: g.dma_start(out=perm[:], in_=dd['c_perm'][:, :]), w=['perm'])
    P.dma('pool', lambda: g.dma_start(out=triU[:], in_=dd['c_triU'][:, :]), w=['triU'])
    P.dma('pool', lambda: g.dma_start(out=selmap[:], in_=dd['c_selmap'].rearrange("p (k n) -> p k n", n=128)), w=['selmap'])
    P.dma('pool', lambda: g.dma_start(out=cmask[:], in_=dd['c_cmask'].rearrange("p (k n) -> p k n", n=128)), w=['cmask'])
    P.dma('pool', lambda: g.dma_start(out=posT[:], in_=dd['posT'][:, :]), w=['posT'])
    P.dma('pool', lambda: g.dma_start(out=w2kd[:], in_=dd['w2kd'].rearrange("(c p) n -> p c n", p=128)), w=['w2kd'])
    P.dma('pool', lambda: g.dma_start(out=w2v[:], in_=dd['w2v'].rearrange("(c p) n -> p c n", p=128)), w=['w2v'])
    P.dma('sp', lambda: sp.dma_start(out=wk[:], in_=dd['c_wk'][:, :]), w=['wk'])
    P.dma('sp', lambda: sp.dma_start(out=wa[:], in_=dd['c_wa'][:, :]), w=['wa'])
    P.dma('sp', lambda: sp.dma_start(out=b1T[:], in_=dd['b1T'][:, :]), w=['b1T'])
    P.dma('sp', lambda: sp.dma_start(out=b2k[:], in_=dd['b2k'][:, :]), w=['b2k'])
    P.dma('sp', lambda: sp.dma_start(out=b2v[:], in_=dd['b2v'][:, :]), w=['b2v'])
    P.dma('sp', lambda: sp.dma_start(out=cosC[:], in_=dd['cosC'][:, :]), w=['cosC'])
    P.dma('sp', lambda: sp.dma_start(out=sinC[:], in_=dd['sinC'][:, :]), w=['sinC'])

    vsw = sb("vsw", [128, NKT, 4, 64])
    gsig = sb("gsig", [128, NKT, 6], F32)
    P.pool(lambda: nc.gpsimd.memset(vsw[:, :, 1, :], 1.0), w=['vsw_ones'])
    P.pool(lambda: nc.gpsimd.memset(vsw[:, :, 3, :], 1.0), w=['vsw_ones'])
    ct = sb("ct", [128, 512], F32); st = sb("st", [128, 512], F32)
    xs = sb("xs", [128, 512]); t1 = sb("t1", [128, 512], F32); t2 = sb("t2", [128, 512], F32)
    pt_sh = L['pt']; rcp_sh = L['rcp']; osb_sh = L['osb']
    stg = sb("stg", [128, 512])

    def rope(src_bank, cos_ap, sin_ap, dst_ap, n, scale, rtok, wtok, bias=None, rows=128):
        if bias is None:
            P.act(lambda: nc.scalar.activation(out=xs[:, 0:n], in_=ps[src_bank][:, 0:n], func=AF.Copy, scale=scale),
                  r=[f'ps{src_bank}'], w=['xs'])
        else:
            P.act(lambda: nc.scalar.activation(out=xs[:, 0:n], in_=ps[src_bank][:, 0:n], func=AF.Identity, bias=bias, scale=scale),
                  r=[f'ps{src_bank}', 'b2k'], w=['xs'])
        P.pe(lambda: nc.tensor.matmul(ps[7][:, 0:n], lhsT=perm[:], rhs=xs[:, 0:n], start=True, stop=True), r=['xs', 'perm'], w=['ps7'])
        P.dve(lambda: nc.vector.tensor_tensor(out=t1[:, 0:n], in0=xs[:, 0:n], in1=cos_ap, op=ALU.mult), r=['xs', *rtok], w=['t1'])
        P.dve(lambda: nc.vector.tensor_tensor(out=t2[:, 0:n], in0=ps[7][:, 0:n], in1=sin_ap, op=ALU.mult), r=['ps7', *rtok], w=['t2'])
        P.pool(lambda: nc.gpsimd.tensor_tensor(out=dst_ap, in0=t1[0:rows, 0:n], in1=t2[0:rows, 0:n], op=ALU.add), r=['t1', 't2'], w=wtok)

    for c in range(NCH):
        norm_chunk(c)
        cs = slice(c * 512, (c + 1) * 512)
        P.dma('sp', lambda cs=cs: sp.dma_start(out=ct[:], in_=dd['cosT'][:, cs]), w=['ct'])
        P.dma('sp', lambda cs=cs: sp.dma_start(out=st[:], in_=dd['sinT'][:, cs]), w=['st'])
        dsts = [(qbT[:, 0, cs], 0.125, [f'qb0_{c}', f'va{c // 2}']),
                (qbT[:, 1, cs], 0.125, [f'qb1_{c}', f'va{8 + c // 2}']),
                (ksT2[0:64, cs], 1.0, [f'ks{c}', f'qa{c}']),
                (kwT2[:, cs], 1.0, [f'kw{c}', f'ka{c}'])]
        for ti, (dst, scl, wtok) in enumerate(dsts):
            fm_proj(wn, 128 * ti, 6, ['wn'])
            rope(6, ct[:], st[:], dst, 512, scl, ['ct', 'st'], wtok, rows=(64 if ti == 2 else 128))
        P.dma('pool', lambda cs=cs: g.dma_start(out=ksT2[64:128, cs], in_=dd['c_ind'][:, cs]), w=[f'ksi{c}', f'qa{c}'])
        fm_proj(wn, 512, 6, ['wn'])
        P.act(lambda: nc.scalar.copy(out=stg[:], in_=ps[6][:]), r=['ps6'], w=['stg'])
        P.dma('pool', lambda cs=cs: g.dma_start(out=dd['kvc'][:, cs], in_=stg[:]), r=['stg'], w=['kvc'])
        for t in range(4):
            for k in range(8):
                P.pe(lambda k=k, t=t: nc.tensor.matmul(ps[6][:, t * 128:(t + 1) * 128], lhsT=hT[:, k, t * 128:(t + 1) * 128],
                                                       rhs=wn[:, k, 640:768], start=(k == 0), stop=(k == 7)),
                     r=[f'hT{k}', 'wn'], w=['ps6'])
        for t in range(4):
            for k in range(8):
                P.pe(lambda k=k, t=t: nc.tensor.matmul(ps[7][:, t * 6:(t + 1) * 6], lhsT=hT[:, k, t * 128:(t + 1) * 128],
                                                       rhs=wn[:, k, 768:774], start=(k == 0), stop=(k == 7)),
                     r=[f'hT{k}', 'wn'], w=['ps7'])
        psv = ps[6][:].rearrange("p (t j d) -> p t j d", t=4, j=2)
        P.act(lambda c=c, psv=psv: nc.scalar.copy(out=vsw[:, 4 * c:4 * c + 4, 0, :], in_=psv[:, :, 0, :]), r=['ps6', 'vsw_ones'], w=[f'vsw{c}'])
        P.dve(lambda c=c, psv=psv: nc.vector.tensor_copy(out=vsw[:, 4 * c:4 * c + 4, 2, :], in_=psv[:, :, 1, :]), r=['ps6', 'vsw_ones'], w=[f'vsw{c}'])
        P.act(lambda c=c: nc.scalar.activation(out=gsig[:, 4 * c:4 * c + 4, :], in_=ps[7][:, 0:24].rearrange("p (t j) -> p t j", j=6),
                                               func=AF.Sigmoid), r=['ps7'], w=['gsig'])

    if L.get('stage', 9) < 2:
        return
    cb = xin0[:].bitcast(BF16).rearrange("p k n -> p (k n)")
    cbv = cb.rearrange("p (n s) -> p n s", s=16)
    w1b = L['w1b']
    hb = sb("hb", [128, 2], F32)
    hx = t1; gu = t2
    hid = [sb(f"hid{i}", [128, 512]) for i in range(2)]
    kcT2 = sb("kcT2", [128, 512]); vcA = sb("vcA", [128, 4, 128])
    P.dve(lambda: nc.vector.memset(hid[0][:, 511:512], 0.0), w=['hid0'])
    P.dve(lambda: nc.vector.memset(hid[1][:, 511:512], 0.0), w=['hid1'])
    P.dve(lambda: nc.vector.memset(kcT2[:, 511:512], 0.0), w=['kcT2'])
    P.pool(lambda: nc.gpsimd.memset(vcA[:, :, 64:128], 1.0), w=['vcA'])
    for mlp in range(2):
        P.dma('pool', lambda mlp=mlp: g.dma_start(out=w1b[:], in_=dd['w1'][mlp].rearrange("(c p) h -> p c h", p=128)), w=['w1b', 'wfm', 'wtm'])
        P.dma('sp', lambda mlp=mlp: sp.dma_start(out=cb[0:64, :], in_=dd['kvc'][64 * mlp:64 * mlp + 64, :]), r=['kvc'], w=['xin'])
        P.dma('sp', lambda mlp=mlp: sp.dma_start(out=cb[64:128, 0:T - 1], in_=dd['kvc'][64 * mlp:64 * mlp + 64, 1:T]), r=['kvc'], w=['xin'])
        for mt in range(2):
            for c16 in range(16):
                P.pe(lambda mt=mt, c16=c16, mlp=mlp: nc.tensor.matmul(ps[7][:, mt:mt + 1], lhsT=w1b[:, c16, mt * 128:(mt + 1) * 128],
                                                                      rhs=posT[:, 16 * mlp + c16:16 * mlp + c16 + 1],
                                                                      start=(c16 == 0), stop=(c16 == 15)),
                     r=['w1b', 'posT'], w=['ps7'])
        P.dve(lambda mlp=mlp: nc.vector.tensor_tensor(out=hb[:], in0=ps[7][:, 0:2], in1=b1T[:, 2 * mlp:2 * mlp + 2], op=ALU.add),
              r=['ps7', 'b1T'], w=['hb'])
        for mt in range(2):
            for c16 in range(16):
                P.pe(lambda mt=mt, c16=c16: nc.tensor.matmul(ps[6][:, 0:511], lhsT=w1b[:, c16, mt * 128:(mt + 1) * 128],
                                                             rhs=cbv[:, (2 * c16) // 16:(2 * c16) // 16 + 511, (2 * c16) % 16], start=(c16 == 0), stop=(c16 == 15)),
                     r=['w1b', 'xin'], w=['ps6'])
            P.act(lambda mt=mt: nc.scalar.activation(out=hx[:, 0:511], in_=ps[6][:, 0:511], func=AF.Identity, bias=hb[:, mt:mt + 1], scale=1.0),
                  r=['ps6', 'hb'], w=['t1'])
            P.dve(lambda: nc.vector.tensor_tensor(out=gu[:, 0:511], in0=hx[:, 0:511], in1=hx[:, 0:511], op=ALU.mult), r=['t1'], w=['t2'])
            P.dve(lambda: nc.vector.tensor_scalar(out=gu[:, 0:511], in0=gu[:, 0:511], scalar1=0.044715, scalar2=1.0, op0=ALU.mult, op1=ALU.add),
                  r=['t2'], w=['t2'])
            P.dve(lambda: nc.vector.tensor_tensor(out=gu[:, 0:511], in0=gu[:, 0:511], in1=hx[:, 0:511], op=ALU.mult), r=['t2', 't1'], w=['t2'])
            P.act(lambda: nc.scalar.activation(out=gu[:, 0:511], in_=gu[:, 0:511], func=AF.Sigmoid, scale=1.5957691216057308), r=['t2'], w=['t2'])
            P.dve(lambda mt=mt: nc.vector.tensor_tensor(out=hid[mt][:, 0:511], in0=hx[:, 0:511], in1=gu[:, 0:511], op=ALU.mult),
                  r=['t2', 't1'], w=[f'hid{mt}'])
        if mlp == 0:
            for mt in range(2):
                P.pe(lambda mt=mt: nc.tensor.matmul(ps[6][:, 0:511], lhsT=w2kd[:, mt, :], rhs=hid[mt][:, 0:511], start=(mt == 0), stop=(mt == 1)),
                     r=[f'hid{mt}', 'w2kd'], w=['ps6'])
            rope(6, cosC[:, 0:511], sinC[:, 0:511], kcT2[:, 0:511], 511, 1.0, ['cosC', 'sinC'], ['kcT2'], bias=b2k[:])
        else:
            for nt in range(4):
                for mt in range(2):
                    P.pe(lambda mt=mt, nt=nt: nc.tensor.matmul(ps[6][:, nt * 64:(nt + 1) * 64], lhsT=hid[mt][:, nt * 128:(nt + 1) * 128],
                                                               rhs=w2v[:, mt, :], start=(mt == 0), stop=(mt == 1)),
                         r=[f'hid{mt}', 'w2v'], w=['ps6'])
            P.dve(lambda: nc.vector.tensor_tensor(out=vcA[:, :, 0:64], in0=ps[6][:, 0:256].rearrange("p (a b) -> p a b", b=64),
                                                  in1=b2v[:].unsqueeze(1).to_broadcast([128, 4, 64]), op=ALU.add),
                  r=['ps6', 'b2v'], w=['vcA'])

    if L.get('stage', 9) < 3:
        return
    psS = L['psS']
    ptn = pt_sh
    rs4 = sb("rs4", [128, 4], F32); rc4 = sb("rc4", [128, 4], F32)
    imp = sb("imp", [128, 128], F32); impm = sb("impm", [128, 128], F32); imp2 = sb("imp2", [128, 128], F32)
    m8 = sb("m8", [128, 16], F32)
    Mb = sb("Mb", [128, 256])
    grep = sb("grep", [128, 2, 6, 64])
    rcn = rcp_sh[0]; rgd = t1
    acc = sb("acc", [64, 512], F32); tmpc = rcp_sh[1]
    obn = osb_sh[0]
    onT = dd['onT']
    it = [0]
    NQ = 32 if nq_limit is None else nq_limit
    OS, OW, OC, UB = 4, 5, 6, 7
    qpc = [sb(f"qpc{b}", [128, 2, 2, 128]) for b in range(2)]
    qsa = [sb(f"qsa{b}", [128, 2, 2, 256]) for b in range(2)]
    qsw = [sb(f"qsw{b}", [128, 2, 256]) for b in range(2)]
    for b in range(2):
        P.pool(lambda b=b: nc.gpsimd.memset(qpc[b][:], 0.0), w=[f'qpc{b}'])
        P.pool(lambda b=b: nc.gpsimd.memset(qsw[b][:], 0.0), w=[f'qsw{b}'])
    ncmp = [0]

    def sbuf_of(i):
        d = i % 2
        return psS[d], [f'ps{2 * d}', f'ps{2 * d + 1}'], ptn[i % 3], f'pt{i % 3}'

    Mb2 = [Mb, sb("Mb1", [128, 256])]
    occ = [ct[0:64, :], st[0:64, :]]; occtok = ['ct', 'st']
    U = ps[UB]

    def part1(qt2):
        cq = qt2 // 2
        for p in range(2):
            qb = 2 * qt2 + p
            ntmax = qb // 16
            qc = qpc[ncmp[0] % 2]; qctok = f'qpc{ncmp[0] % 2}'; ncmp[0] += 1
            for j in range(2):
                P.pool(lambda j=j, qc=qc, qb=qb: nc.gpsimd.tensor_copy(out=qc[64 * j:64 * j + 64, j, :, :],
                                                                        in_=qbT[64 * j:64 * j + 64, :, qb * 128:(qb + 1) * 128]),
                       r=[f'qb0_{cq}', f'qb1_{cq}'], w=[qctok])
            for nt in range(ntmax + 1):
                i = it[0]; it[0] += 1
                Sd, stok, pt, ptok = sbuf_of(i)
                Dd = qb - 16 * nt
                masked = Dd <= 16
                for j in range(2):
                    for ti in range(2):
                        P.pe(lambda ti=ti, j=j, Sd=Sd, nt=nt, masked=masked, qc=qc: nc.tensor.matmul(
                            Sd[:, j * 512 + ti * 128:j * 512 + ti * 128 + 128], lhsT=kcT2[:, nt * 128:(nt + 1) * 128],
                            rhs=qc[:, j, ti, :], start=(ti == 0), stop=not masked),
                            r=['kcT2', qctok], w=[stok[j]])
                if masked:
                    for j in range(2):
                        for ti in range(2):
                            P.pe(lambda Sd=Sd, Dd=Dd, j=j, ti=ti: nc.tensor.matmul(Sd[:, j * 512 + ti * 128:j * 512 + ti * 128 + 128], lhsT=ident[:],
                                                                                  rhs=cmask[:, Dd, :], start=False, stop=True),
                                 r=['ident', 'cmask'], w=[stok[j]])
                Sv = Sd[:].rearrange("p (j q) -> p j q", j=2)[:, :, 0:256]
                pv_ = pt[:].rearrange("p (j q) -> p j q", j=2)
                P.act(lambda Sv=Sv, pv_=pv_: nc.scalar.activation(out=pv_, in_=Sv, func=AF.Exp), r=stok, w=[ptok])
                for j in range(2):
                    for ti in range(2):
                        h = 2 * j + ti
                        P.pe(lambda h=h, j=j, ti=ti, pt=pt, nt=nt, ntmax=ntmax: nc.tensor.matmul(
                            U[:, h * 128:(h + 1) * 128], lhsT=pt[:, j * 256 + ti * 128:j * 256 + ti * 128 + 128], rhs=selmap[:, nt, :],
                            start=(nt == 0 and h == 0), stop=(nt == ntmax)), r=[ptok, 'selmap'], w=[f'ps{UB}'])
                for j in range(2):
                    P.pe(lambda j=j, pt=pt, nt=nt, ntmax=ntmax, p=p: nc.tensor.matmul(
                        ps[OC][:, j * 256 + p * 128:j * 256 + p * 128 + 128], lhsT=vcA[:, nt, :], rhs=pt[:, j * 256:j * 256 + 128],
                        start=(nt == 0 and p == 0 and j == 0), stop=(nt == ntmax)), r=[ptok, 'vcA'], w=[f'ps{OC}'])
            P.dve(lambda: nc.vector.tensor_reduce(out=rs4[:], in_=U[:].rearrange("p (a b) -> p a b", a=4), axis=mybir.AxisListType.X, op=ALU.add),
                  r=[f'ps{UB}'], w=['rs4'])
            P.dve(lambda: nc.vector.tensor_scalar(out=rs4[:], in0=rs4[:], scalar1=1e-30, scalar2=None, op0=ALU.max), r=['rs4'], w=['rs4'])
            P.dve(lambda: nc.vector.reciprocal(out=rc4[:], in_=rs4[:]), r=['rs4'], w=['rc4'])
            P.dve(lambda: nc.vector.tensor_scalar(out=imp[:], in0=U[:, 0:128], scalar1=rc4[:, 0:1], scalar2=None, op0=ALU.mult),
                  r=[f'ps{UB}', 'rc4'], w=['imp'])
            for h4 in range(1, 4):
                P.dve(lambda h4=h4: nc.vector.scalar_tensor_tensor(out=imp[:], in0=U[:, h4 * 128:(h4 + 1) * 128], scalar=rc4[:, h4:h4 + 1],
                                                                   in1=imp[:], op0=ALU.mult, op1=ALU.add), r=[f'ps{UB}', 'rc4', 'imp'], w=['imp'])
            sl = slice(128 - 2 * qb, 256 - 2 * qb)
            P.dve(lambda sl=sl: nc.vector.tensor_tensor(out=impm[:], in0=imp[:], in1=wk[:, sl], op=ALU.mult), r=['imp', 'wk'], w=['impm'])
            P.dve(lambda sl=sl: nc.vector.tensor_tensor(out=impm[:], in0=impm[:], in1=wa[:, sl], op=ALU.add), r=['impm', 'wa'], w=['impm'])
            P.dve(lambda: nc.vector.memset(impm[:, 0:1], 2e6), r=['impm'], w=['impm'])
            P.dve(lambda: nc.vector.max(out=m8[:, 0:8], in_=impm[:]), r=['impm'], w=['m8a'])
            P.dve(lambda: nc.vector.match_replace(out=imp2[:], in_to_replace=m8[:, 0:8], in_values=impm[:], imm_value=-1e9),
                  r=['impm', 'm8a'], w=['imp2'])
            P.dve(lambda: nc.vector.max(out=m8[:, 8:16], in_=imp2[:]), r=['imp2'], w=['m8b'])
            mb = Mb2[p]
            for hh in range(2):
                P.dve(lambda hh=hh, mb=mb: nc.vector.tensor_scalar(out=mb[:, hh * 128:(hh + 1) * 128], in0=impm[:], scalar1=m8[:, 15:16], scalar2=NEG,
                                                                   op0=ALU.is_lt, op1=ALU.mult), r=['impm', 'm8b'], w=[f'Mb{p}'])
        oc_ = occ[qt2 % 2]; octok = occtok[qt2 % 2]
        if qt2 == 0:
            P.dve(lambda: nc.vector.tensor_scalar(out=rgd[64:128, :], in0=ps[OC][64:128, :], scalar1=1e-30, scalar2=None, op0=ALU.max),
                  r=[f'ps{OC}'], w=['t1'])
            P.dve(lambda: nc.vector.reciprocal(out=rcn[:], in_=rgd[64:128, :]), r=['t1'], w=['rcp0'])
        else:
            P.dve(lambda: nc.vector.reciprocal(out=rcn[:], in_=ps[OC][64:128, :]), r=[f'ps{OC}'], w=['rcp0'])
        P.dve(lambda oc_=oc_: nc.vector.tensor_tensor(out=oc_, in0=ps[OC][0:64, :], in1=rcn[:], op=ALU.mult), r=[f'ps{OC}', 'rcp0'], w=[octok])

    def part2(qt2):
        cq = qt2 // 2
        qa_ = qsa[qt2 % 2]; qatok = f'qsa{qt2 % 2}'
        qw_ = qsw[qt2 % 2]; qwtok = f'qsw{qt2 % 2}'
        for p in range(2):
            mb = Mb2[p]
            P.pe(lambda p=p, mb=mb: nc.tensor.matmul(U[:, p * 256:p * 256 + 128], lhsT=mb[:, 64:192], rhs=ident[:], start=True, stop=True),
                 r=[f'Mb{p}', 'ident'], w=[f'ps{UB}'])
            P.pe(lambda p=p, mb=mb: nc.tensor.matmul(U[:, p * 256 + 128:p * 256 + 256], lhsT=mb[:, 0:128], rhs=ident[:], start=False, stop=True),
                 r=[f'Mb{p}', 'ident'], w=[f'ps{UB}'])
            Uv = U[64:128, p * 256:(p + 1) * 256].rearrange("r (h q) -> r h q", h=2)
            P.act(lambda p=p, qa_=qa_, Uv=Uv: nc.scalar.copy(out=qa_[64:128, 0, :, p * 128:(p + 1) * 128], in_=Uv), r=[f'ps{UB}'], w=[qatok])
            P.dve(lambda p=p, qa_=qa_, Uv=Uv: nc.vector.tensor_copy(out=qa_[64:128, 1, :, p * 128:(p + 1) * 128], in_=Uv), r=[f'ps{UB}'], w=[qatok])
        for j in range(2):
            src = R2[64 * j:64 * j + 64, qt2 * 256:(qt2 + 1) * 256]
            for hh in range(2):
                if j == 0:
                    P.pool(lambda hh=hh, qa_=qa_, src=src: nc.gpsimd.tensor_copy(out=qa_[0:64, 0, hh, :], in_=src), r=[f'qb0_{cq}'], w=[qatok])
                else:
                    P.dve(lambda hh=hh, qa_=qa_, src=src: nc.vector.tensor_copy(out=qa_[0:64, 1, hh, :], in_=src), r=[f'qb0_{cq}'], w=[qatok])
            if j == 0:
                P.pool(lambda qw_=qw_, src=src: nc.gpsimd.tensor_copy(out=qw_[0:64, 0, :], in_=src), r=[f'qb0_{cq}'], w=[qwtok])
            else:
                P.dve(lambda qw_=qw_, src=src: nc.vector.tensor_copy(out=qw_[0:64, 1, :], in_=src), r=[f'qb0_{cq}'], w=[qwtok])

    def run_branch(kts, qk_fn, ob, vslot):
        n = len(kts)
        info = {}

        def do_qk(idx):
            kt = kts[idx]
            i = it[0]; it[0] += 1
            Sd, stok, pt, ptok = sbuf_of(i)
            c0, w = qk_fn(kt, Sd, stok)
            Sv = Sd[:].rearrange("p (j q) -> p j q", j=2)[:, :, c0:c0 + w]
            pv_ = pt[:].rearrange("p (j q) -> p j q", j=2)[:, :, c0:c0 + w]
            P.act(lambda Sv=Sv, pv_=pv_: nc.scalar.activation(out=pv_, in_=Sv, func=AF.Exp), r=stok, w=[ptok])
            info[idx] = (i, c0, w)

        def do_pv(idx):
            kt = kts[idx]
            i, c0, w = info[idx]
            _, _, pt, ptok = sbuf_of(i)
            for j in range(2):
                P.pe(lambda j=j, kt=kt, pt=pt, c0=c0, w=w, idx=idx: nc.tensor.matmul(
                    ps[ob][:, j * 256 + c0:j * 256 + c0 + w], lhsT=vsw[:, kt, vslot:vslot + 2, :], rhs=pt[:, j * 256 + c0:j * 256 + c0 + w],
                    start=(idx == 0 and j == 0), stop=(idx == n - 1)), r=[ptok, f'vsw{kt//4}'], w=[f'ps{ob}'])

        do_qk(0)
        for idx in range(n):
            if idx + 1 < n:
                do_qk(idx + 1)
            do_pv(idx)

    def attend(qt2):
        cq = qt2 // 2
        qa_ = qsa[qt2 % 2]; qatok = f'qsa{qt2 % 2}'
        qs_ = qsw[qt2 % 2]; qstok = f'qsw{qt2 % 2}'

        def qk_sel(kt, Sd, stok):
            hh = kt // 32
            r_ = kt - 2 * qt2
            c0 = 128 if r_ == 1 else 0
            w = 256 - c0
            diag = r_ >= 0
            for j in range(2):
                P.pe(lambda j=j: nc.tensor.matmul(Sd[:, j * 512 + c0:j * 512 + 256], lhsT=ksT2[:, kt * 128:(kt + 1) * 128],
                                                  rhs=qa_[:, j, hh, c0:256], start=True, stop=(not diag)),
                     r=[f'ks{kt//4}', f'ksi{kt//4}', qatok], w=[stok[j]])
            if diag:
                for j in range(2):
                    P.pe(lambda j=j: nc.tensor.matmul(Sd[:, j * 512 + 128 * r_:j * 512 + 128 * r_ + 128], lhsT=ident[:], rhs=tri[:],
                                                      start=False, stop=True), r=['ident', 'tri'], w=[stok[j]])
            return c0, w

        def qk_win(kt, Sd, stok):
            rel = kt - 2 * qt2
            if rel == -4:
                c0, w, mk = 0, 128, (triU, 0)
            elif rel == -3:
                c0, w, mk = 0, 256, (triU, 128)
            elif rel in (-2, -1):
                c0, w, mk = 0, 256, None
            elif rel == 0:
                c0, w, mk = 0, 256, (tri, 0)
            else:
                c0, w, mk = 128, 128, (tri, 128)
            for j in range(2):
                P.pe(lambda j=j: nc.tensor.matmul(Sd[:, j * 512 + c0:j * 512 + c0 + w], lhsT=kwT2[:, kt * 128:(kt + 1) * 128],
                                                  rhs=qs_[:, j, c0:c0 + w], start=True, stop=(mk is None)),
                     r=[f'kw{kt//4}', qstok], w=[stok[j]])
            if mk is not None:
                mt_, mo = mk
                for j in range(2):
                    P.pe(lambda j=j: nc.tensor.matmul(Sd[:, j * 512 + mo:j * 512 + mo + 128], lhsT=ident[:], rhs=mt_[:], start=False, stop=True),
                         r=['ident', 'tri', 'triU'], w=[stok[j]])
            return c0, w

        run_branch(list(range(2 * qt2 + 2)), qk_sel, OS, 0)
        rels = [r for r in (-3, -4, -2, -1, 0, 1) if 2 * qt2 + r >= 0]
        run_branch([2 * qt2 + r for r in rels], qk_win, OW, 2)

    def finalize_tile(qt2):
        G = ps[UB]
        oc_ = occ[qt2 % 2]; octok = occtok[qt2 % 2]
        for p in range(2):
            P.dve(lambda p=p: nc.vector.tensor_copy(out=grep[:, p, :, :],
                                                    in_=gsig[:, 2 * qt2 + p, :].unsqueeze(2).to_broadcast([128, 6, 64])),
                  r=['gsig'], w=['grep'])
        for br, ob in ((1, OS), (2, OW), (0, OC)):
            for p in range(2):
                for j in range(2):
                    P.pe(lambda p=p, j=j, br=br: nc.tensor.matmul(G[0:64, j * 256 + p * 128:j * 256 + p * 128 + 128],
                                                                  lhsT=grep[:, p, 3 * j + br, :], rhs=ident[:], start=True, stop=True),
                         r=['grep', 'ident'], w=[f'ps{UB}'])
            if br == 0:
                P.dve(lambda: nc.vector.tensor_tensor(out=tmpc[:], in0=oc_, in1=G[0:64, :], op=ALU.mult), r=[octok, f'ps{UB}'], w=['rcp1'])
                P.pool(lambda: nc.gpsimd.tensor_tensor(out=obn[:], in0=acc[:], in1=tmpc[:], op=ALU.add), r=['acc', 'rcp1'], w=['osb0'])
                continue
            Zb = t1 if br == 1 else t2
            ztok = 't1' if br == 1 else 't2'
            P.act(lambda ob=ob, Zb=Zb: nc.scalar.activation(out=Zb[64:128, :], in_=ps[ob][64:128, :], func=AF.Ln), r=[f'ps{ob}'], w=[ztok])
            P.act(lambda Zb=Zb: nc.scalar.activation(out=Zb[64:128, :], in_=Zb[64:128, :], func=AF.Exp, scale=-1.0), r=[ztok], w=[ztok])
            P.dve(lambda Zb=Zb: nc.vector.tensor_copy(out=rcn[:], in_=Zb[64:128, :]), r=[ztok], w=['rcp0'])
            P.dve(lambda: nc.vector.tensor_tensor(out=rcn[:], in0=rcn[:], in1=G[0:64, :], op=ALU.mult), r=['rcp0', f'ps{UB}'], w=['rcp0'])
            if br == 1:
                P.dve(lambda ob=ob: nc.vector.tensor_tensor(out=acc[:], in0=ps[ob][0:64, :], in1=rcn[:], op=ALU.mult), r=[f'ps{ob}', 'rcp0'], w=['acc'])
            else:
                P.dve(lambda ob=ob: nc.vector.tensor_tensor(out=tmpc[:], in0=ps[ob][0:64, :], in1=rcn[:], op=ALU.mult), r=[f'ps{ob}', 'rcp0'], w=['rcp1'])
                P.pool(lambda: nc.gpsimd.tensor_tensor(out=acc[:], in0=acc[:], in1=tmpc[:], op=ALU.add), r=['acc', 'rcp1'], w=['acc'])
        for j in range(2):
            P.dma('pool', lambda j=j: g.dma_start(out=onT[64 * j:64 * j + 64, qt2 * 256:(qt2 + 1) * 256], in_=obn[:, j * 256:(j + 1) * 256]),
                  r=['osb0'])

    part1(0)
    part2(0)
    for qt2 in range(NQ):
        if qt2 + 1 < NQ:
            part1(qt2 + 1)
        attend(qt2)
        if qt2 + 1 < NQ:
            part2(qt2 + 1)
        finalize_tile(qt2)


def rope_tables(pos):
    half = 8
    inv = (np.float32(500000.0) ** (-np.arange(half, dtype=np.float32) / np.float32(half))).astype(np.float32)
    ang = pos.astype(np.float32)[:, None] * inv[None, :]
    cos = np.cos(ang).astype(np.float32); sin = np.sin(ang).astype(np.float32)
    L = pos.shape[0]
    c64 = np.ones((64, L), np.float32); s64 = np.zeros((64, L), np.float32)
    c64[0:8] = cos.T; c64[8:16] = cos.T
    s64[0:8] = sin.T; s64[8:16] = sin.T
    return np.concatenate([c64, c64], 0), np.concatenate([s64, s64], 0)


_CONSTS = None


def nsa_consts():
    global _CONSTS
    if _CONSTS is not None:
        return _CONSTS
    d = {}
    cT, sT = rope_tables(np.arange(T))
    d['cosT'] = cT; d['sinT'] = sT
    cc, sc = rope_tables(np.arange(512) * 16 + 31)
    d['cosC'] = cc; d['sinC'] = sc
    pm = np.zeros((128, 128), np.float32)
    for m in range(128):
        mm = m % 64
        if mm < 8:
            pm[m + 8, m] = -1.0
        elif mm < 16:
            pm[m - 8, m] = 1.0
    d['c_perm'] = pm
    k = np.arange(128)[:, None]; q = np.arange(128)[None, :]
    d['c_triU'] = np.where(k <= q, NEG, 0.0).astype(np.float32)
    key = np.arange(T)[None, :]; jj = np.arange(64)[:, None]
    d['c_ind'] = (((key // 64) % 64) == jj).astype(np.float32)
    ncb, nsel = 511, 128
    cs_ = np.arange(ncb) * 16; ce = cs_ + 32
    ss = np.arange(nsel) * 64; se = ss + 64
    ov = np.clip(np.minimum(ce[:, None], se[None, :]) - np.maximum(cs_[:, None], ss[None, :]), 0, None) / 32.0
    sm = np.zeros((512, 128), np.float32); sm[:511] = ov
    d['c_selmap'] = np.ascontiguousarray(sm.reshape(4, 128, 128).transpose(1, 0, 2).reshape(128, 512))
    cm = np.zeros((128, 17, 128), np.float32)
    nn = np.arange(128)[:, None]; qq = np.arange(128)[None, :]
    for Dd in range(17):
        cm[:, Dd, :] = np.where(16 * nn - qq <= 128 * Dd - 31, 0.0, NEG)
    d['c_cmask'] = cm.reshape(128, 17 * 128)
    qq = np.arange(128)[:, None]; u = np.arange(256)[None, :]
    dlt = u - 128; own = qq // 64
    d['c_wk'] = (dlt < own).astype(np.float32)
    d['c_wa'] = np.where(dlt == own, 1e6, np.where(dlt > own, -1.0, 0.0)).astype(np.float32)
    _CONSTS = d
    return d


def inputs_nsa(inp, l, b, hp):
    w_in = inp['w_in'][l]
    offs = np.cumsum([0, 512, 512, 512, 8, 512, 768, 24, 1024, 1024])
    gq = hp // 2
    own = w_in[:, offs[4] + 128 * hp: offs[4] + 128 * hp + 128]
    oth_hp = 2 * gq + (1 - hp % 2)
    oth = w_in[:, offs[4] + 128 * oth_hp: offs[4] + 128 * oth_hp + 128]
    kv = [w_in[:, offs[5] + 128 * s + 64 * gq: offs[5] + 128 * s + 64 * gq + 64] for s in range(6)]
    kc, vc, ks, vs, kw, vw = kv
    gcols = w_in[:, offs[6] + 6 * hp: offs[6] + 6 * hp + 6]
    w_nsa = np.concatenate([own, oth, ks, ks, kw, kw, kc, vc, vs, vw, gcols], axis=1)
    assert w_nsa.shape[1] == NW
    pos = inp['cmp_pos'][l]
    posT = np.zeros((128, 32), np.float32)
    for m in range(2):
        pp = pos[m].reshape(16, 2, 64).reshape(16, 128)
        posT[:, 16 * m:16 * m + 16] = pp.T
    b1 = inp['cmp_b1'][l]
    b1T = np.stack([b1[0, 0:128], b1[0, 128:256], b1[1, 0:128], b1[1, 128:256]], axis=1)
    w2 = inp['cmp_w2'][l]
    b2 = inp['cmp_b2'][l]
    m = dict(
        w_nsa=np.ascontiguousarray(w_nsa),
        posT=posT, w1=np.ascontiguousarray(inp['cmp_w1'][l]), b1T=np.ascontiguousarray(b1T),
        w2kd=np.ascontiguousarray(np.concatenate([w2[0], w2[0]], axis=1)), w2v=np.ascontiguousarray(w2[1]),
        b2k=np.ascontiguousarray(np.concatenate([b2[0], b2[0]])[:, None]),
        b2v=np.ascontiguousarray(np.broadcast_to(b2[1][None, :], (128, 64))),
    )
    m.update(nsa_consts())
    return m


T = 8192
D = 1024
NCH = 16
NKT = 64
EPS = 1e-6
NEG = -30000.0


def build_A(do_fox=True, do_nsa=True, nqt_limit=None, stage=9):
    nc = bass.Bass("TRN2", target_bir_lowering=False)
    P = Prog(nc)
    dram = lambda name, shape, dt=F32, kind="ExternalInput": nc.dram_tensor(name, shape, dt, kind=kind).ap()
    xT = dram("xT", [D, T])
    gmix = dram("gmix", [128, 8])
    w_fm_fox = dram("w_fm_fox", [D, 256])
    w_tm_fox = dram("w_tm_fox", [D, 130])
    bfg = dram("bfg", [128, 2])
    c_tri = dram("c_tri", [128, 128])
    c_ident = dram("c_ident", [128, 128])
    c_ut = dram("c_ut", [128, 128])
    oaT = dram("oaT", [128, T], BF16, kind="ExternalOutput")
    if do_nsa:
        nsa_dram = nsa_declare(dram, nc)

    sb = lambda name, shape, dt=BF16: nc.alloc_sbuf_tensor(name, shape, dt)
    psS = [nc.alloc_psum_tensor(f"psS{i}", [128, 1024], F32) for i in range(2)]
    ps = [psS[0][:, 0:512], psS[0][:, 512:1024], psS[1][:, 0:512], psS[1][:, 512:1024]] + \
         [nc.alloc_psum_tensor(f"ps{i}", [128, 512], F32)[:] for i in range(4, 8)]

    g_sb = sb("g_sb", [128, 8], F32)
    tri = sb("tri", [128, 128]); ident = sb("ident", [128, 128]); ut = sb("ut", [128, 128])
    ones_bf = sb("ones_bf", [128, 128])
    eps_c = sb("eps_c", [128, 1], F32); one_c = sb("one_c", [128, 1], F32); zero_c = sb("zero_c", [128, 64], F32)
    w1b = sb("w1b", [128, 16, 256])
    wfm = w1b[:, 0:8, :]; wtm = w1b[:, 8:16, 0:130]
    bneg = sb("bneg", [128, 2], F32)
    P.dma('sp', lambda: nc.sync.dma_start(out=g_sb[:], in_=gmix[:, :]), w=['g'])
    P.dma('sp', lambda: nc.sync.dma_start(out=bneg[:], in_=bfg[:, :]), w=['bneg'])
    P.dma('pool', lambda: nc.gpsimd.dma_start(out=tri[:], in_=c_tri[:, :]), w=['tri'])
    P.dma('pool', lambda: nc.gpsimd.dma_start(out=ident[:], in_=c_ident[:, :]), w=['ident'])
    P.dma('pool', lambda: nc.gpsimd.dma_start(out=ut[:], in_=c_ut[:, :]), w=['ut'])
    P.dma('pool', lambda: nc.gpsimd.dma_start(out=wfm, in_=w_fm_fox.rearrange("(c p) n -> p c n", p=128)), w=['wfm'])
    P.dma('pool', lambda: nc.gpsimd.dma_start(out=wtm, in_=w_tm_fox.rearrange("(c p) n -> p c n", p=128)), w=['wtm'])
    P.dve(lambda: nc.vector.memset(ones_bf[:], 1.0), w=['ones'])
    P.dve(lambda: nc.vector.memset(eps_c[:], EPS), w=['eps'])
    P.dve(lambda: nc.vector.memset(one_c[:], 1.0), w=['one'])
    P.dve(lambda: nc.vector.memset(zero_c[:], 0.0), w=['zero'])
    P.dve(lambda: nc.vector.tensor_scalar(out=bneg[:], in0=bneg[:], scalar1=-1.0, scalar2=None, op0=ALU.mult), r=['bneg'], w=['bneg'])

    R0 = sb("R0", [128, T]); R1 = sb("R1", [128, T]); R2 = sb("R2", [128, 2 * T])
    qaT = R0; kaT = R1
    va = R2[:].rearrange("p (k j d) -> p k j d", k=NKT, j=2)
    ftm = sb("ftm", [128, NKT, 2], F32)
    P.pool(lambda: nc.gpsimd.memset(va[:, :, :, 64:128], 1.0), w=['va_ones'])

    xin0 = sb("xin0", [128, 8, 512], F32)
    xin = [xin0, xin0]
    sq = [sb(f"sq{i}", [128, 512]) for i in range(2)]
    rstd = sb("rstd", [128, 512], F32)
    hT = sb("hT", [128, 8, 512])

    def norm_chunk(c):
        xb = xin[c % 2]
        P.dma('sp', lambda: nc.sync.dma_start(out=xb[:], in_=xT[:, c * 512:(c + 1) * 512].rearrange("(k p) n -> p k n", p=128)),
              w=['xin'])
        for k in range(8):
            s = sq[k % 2]
            P.act(lambda k=k, s=s: nc.scalar.activation(out=s[:], in_=xb[:, k, :], func=AF.Square), r=['xin'], w=[f'sq{k%2}'])
            P.pe(lambda k=k, s=s: nc.tensor.matmul(ps[5][:], lhsT=ones_bf[:], rhs=s[:], start=(k == 0), stop=(k == 7)),
                 r=[f'sq{k%2}', 'ones'], w=['ps5'])
        P.act(lambda: nc.scalar.activation(out=rstd[:], in_=ps[5][:], func=AF.Ln, bias=eps_c[:], scale=1.0 / D),
              r=['ps5', 'eps'], w=['rstd'])
        P.act(lambda: nc.scalar.activation(out=rstd[:], in_=rstd[:], func=AF.Exp, scale=-0.5), r=['rstd'], w=['rstd'])
        for k in range(8):
            P.dve(lambda k=k: nc.vector.scalar_tensor_tensor(out=hT[:, k, :], in0=xb[:, k, :], scalar=g_sb[:, k:k + 1], in1=rstd[:],
                                                            op0=ALU.mult, op1=ALU.mult),
                  r=['xin', 'rstd', 'g'], w=[f'hT{k}'])

    def fm_proj(wt, col0, bank, extra_r=()):
        for k in range(8):
            P.pe(lambda k=k: nc.tensor.matmul(ps[bank][:], lhsT=wt[:, k, col0:col0 + 128], rhs=hT[:, k, :], start=(k == 0), stop=(k == 7)),
                 r=[f'hT{k}', *extra_r], w=[f'ps{bank}'])

    pt = [sb(f"pt{i}", [128, 512]) for i in range(3)]
    rcp = [sb(f"rcp{i}", [64, 512], F32) for i in range(2)]
    osb = [sb(f"osb{i}", [64, 512]) for i in range(2)]
    if do_fox:
        for c in range(NCH):
            norm_chunk(c)
            cs = slice(c * 512, (c + 1) * 512)
            fm_proj(wfm, 0, 6, ['wfm'])
            P.act(lambda cs=cs: nc.scalar.activation(out=qaT[:, cs], in_=ps[6][:], func=AF.Copy, scale=0.125), r=['ps6'], w=[f'qa{c}'])
            fm_proj(wfm, 128, 7, ['wfm'])
            P.dve(lambda cs=cs: nc.vector.tensor_copy(out=kaT[:, cs], in_=ps[7][:]), r=['ps7'], w=[f'ka{c}'])
            for t in range(4):
                for k in range(8):
                    P.pe(lambda k=k, t=t: nc.tensor.matmul(ps[6][:, t * 128:(t + 1) * 128], lhsT=hT[:, k, t * 128:(t + 1) * 128],
                                                           rhs=wtm[:, k, 0:128], start=(k == 0), stop=(k == 7)),
                         r=[f'hT{k}', 'wtm'], w=['ps6'])
            for t in range(4):
                for k in range(8):
                    P.pe(lambda k=k, t=t: nc.tensor.matmul(ps[7][:, t * 2:(t + 1) * 2], lhsT=hT[:, k, t * 128:(t + 1) * 128],
                                                           rhs=wtm[:, k, 128:130], start=(k == 0), stop=(k == 7)),
                         r=[f'hT{k}', 'wtm'], w=['ps7'])
            P.act(lambda c=c: nc.scalar.copy(out=va[:, 4 * c:4 * c + 4, :, 0:64],
                                             in_=ps[6][:].rearrange("p (t j d) -> p t j d", t=4, j=2)),
                  r=['ps6', 'va_ones'], w=[f'va{c}'])
            P.dve(lambda c=c: nc.vector.tensor_copy(out=ftm[:, 4 * c:4 * c + 4, :], in_=ps[7][:, 0:8].rearrange("p (t j) -> p t j", j=2)),
                  r=['ps7'], w=['ftm'])

        lf = sb("lf", [128, NKT, 2], F32)
        lhi = sb("lhi", [128, 128]); llo = sb("llo", [128, 128])
        ccol = sb("ccol", [128, NKT, 2], F32)
        incl = sb("incl", [128, NKT, 2], F32)
        tot = sb("tot", [128, NKT, 2], F32)
        for j in range(2):
            P.act(lambda j=j: nc.scalar.activation(out=lf[:, :, j], in_=ftm[:, :, j], func=AF.Exp, bias=bneg[:, j:j + 1], scale=-1.0),
                  r=['ftm', 'bneg'], w=['lf'])
        P.act(lambda: nc.scalar.activation(out=lf[:], in_=lf[:], func=AF.Ln, bias=one_c[:], scale=1.0), r=['lf', 'one'], w=['lf'])
        P.dve(lambda: nc.vector.tensor_scalar(out=lf[:], in0=lf[:], scalar1=-1.0, scalar2=None, op0=ALU.mult), r=['lf'], w=['lf'])
        lf2 = lf[:].rearrange("p a b -> p (a b)")
        P.dve(lambda: nc.vector.tensor_copy(out=lhi[:], in_=lf2), r=['lf'], w=['lhi'])
        P.dve(lambda: nc.vector.tensor_tensor(out=llo[:], in0=lf2, in1=lhi[:], op=ALU.subtract), r=['lf', 'lhi'], w=['llo'])
        P.pe(lambda: nc.tensor.matmul(ps[5][:, 0:128], lhsT=ut[:], rhs=lhi[:], start=True, stop=False), r=['ut', 'lhi'], w=['ps5'])
        P.pe(lambda: nc.tensor.matmul(ps[5][:, 0:128], lhsT=ut[:], rhs=llo[:], start=False, stop=True), r=['ut', 'llo'], w=['ps5'])
        P.pe(lambda: nc.tensor.matmul(ps[5][:, 128:256], lhsT=ones_bf[:], rhs=lhi[:], start=True, stop=False), r=['ones', 'lhi'], w=['ps5'])
        P.pe(lambda: nc.tensor.matmul(ps[5][:, 128:256], lhsT=ones_bf[:], rhs=llo[:], start=False, stop=True), r=['ones', 'llo'], w=['ps5'])
        P.dve(lambda: nc.vector.tensor_copy(out=tot[:].rearrange("p a b -> p (a b)"), in_=ps[5][:, 128:256]), r=['ps5'], w=['tot'])
        for j in range(2):
            P.dve(lambda j=j: nc.vector.tensor_tensor_scan(out=incl[:, :, j], data0=tot[:, :, j], data1=zero_c[:, 0:NKT], initial=0.0,
                                                           op0=ALU.add, op1=ALU.add),
                  r=['tot', 'zero'], w=['incl'])
        P.dve(lambda: nc.vector.tensor_tensor(out=ccol[:].rearrange("p a b -> p (a b)"), in0=ps[5][:, 0:128],
                                              in1=incl[:].rearrange("p a b -> p (a b)"), op=ALU.add), r=['ps5', 'incl'], w=['ccol'])
        P.dve(lambda: nc.vector.tensor_tensor(out=ccol[:], in0=ccol[:], in1=tot[:], op=ALU.subtract), r=['ccol', 'tot'], w=['ccol'])

        biasT = [sb(f"biasT{i}", [128, NKT], F32) for i in range(2)]
        qpz = [[sb(f"qpz{j}{b}", [128, 512]) for b in range(2)] for j in range(2)]
        for j in range(2):
            for b in range(2):
                P.pool(lambda j=j, b=b: nc.gpsimd.memset(qpz[j][b][:], 0.0), w=[f'qpz{j}{b}'])
        it = 0
        fin = 0
        nqt = NCH if nqt_limit is None else nqt_limit
        for j in range(2):
            hs = slice(64 * j, 64 * j + 64)
            for qt in range(nqt):
                nk = 4 * qt + 4
                bt = biasT[fin % 2]
                ob = 3 + fin % 2
                P.dve(lambda j=j, qt=qt, nk=nk, bt=bt: nc.vector.tensor_scalar(
                    out=bt[:, 0:nk], in0=ccol[:, 0:nk, j], scalar1=-1.0, scalar2=incl[:, 4 * qt + 3, j:j + 1], op0=ALU.mult, op1=ALU.add),
                    r=['ccol', 'incl'], w=[f'biasT{fin%2}'])
                qs = slice(qt * 512, (qt + 1) * 512)
                qp = qpz[j][qt % 2]; qptok = f'qpz{j}{qt%2}'
                P.pool(lambda qp=qp, hs=hs, qs=qs: nc.gpsimd.tensor_copy(out=qp[hs, :], in_=qaT[hs, qs]), r=[f'qa{qt}'], w=[qptok])

                def qk(kt, i, qt=qt, j=j, bt=bt, hs=hs, ob=ob, nk=nk, fin=fin, qp=qp, qptok=qptok):
                    r_ = kt - 4 * qt
                    c0 = 128 * r_ if r_ > 0 else 0
                    diag = r_ >= 0
                    S = ps[i % 3]
                    P.pe(lambda: nc.tensor.matmul(S[:, c0:512], lhsT=kaT[:, kt * 128:(kt + 1) * 128], rhs=qp[:, c0:512],
                                                  start=True, stop=not diag),
                         r=[f'ka{kt//4}', qptok], w=[f'ps{i%3}'])
                    if diag:
                        P.pe(lambda: nc.tensor.matmul(S[:, c0:c0 + 128], lhsT=ident[:], rhs=tri[:], start=False, stop=True),
                             r=['ident', 'tri'], w=[f'ps{i%3}'])
                    P.act(lambda: nc.scalar.activation(out=pt[i % 3][:, c0:512], in_=S[:, c0:512], func=AF.Exp, bias=bt[:, kt:kt + 1], scale=1.0),
                          r=[f'ps{i%3}', f'biasT{fin%2}'], w=[f'pt{i%3}'])

                def pv(kt, i, qt=qt, j=j, bt=bt, hs=hs, ob=ob, nk=nk, fin=fin):
                    r_ = kt - 4 * qt
                    c0 = 128 * r_ if r_ > 0 else 0
                    P.pe(lambda: nc.tensor.matmul(ps[ob][:, c0:512], lhsT=va[:, kt, j, :], rhs=pt[i % 3][:, c0:512],
                                                  start=(kt == 0), stop=(kt == nk - 1)),
                         r=[f'pt{i%3}', f'va{kt//4}'], w=[f'ps{ob}'])

                base = it
                for kt in range(min(2, nk)):
                    qk(kt, base + kt)
                for kt in range(nk):
                    if kt + 2 < nk:
                        qk(kt + 2, base + kt + 2)
                    pv(kt, base + kt)
                it += nk
                rc = rcp[fin % 2]; o_ = osb[fin % 2]
                P.dve(lambda rc=rc, ob=ob: nc.vector.reciprocal(out=rc[:], in_=ps[ob][64:128, :]), r=[f'ps{ob}'], w=[f'rcp{fin%2}'])
                P.dve(lambda rc=rc, ob=ob, o_=o_: nc.vector.tensor_tensor(out=o_[:], in0=ps[ob][0:64, :], in1=rc[:], op=ALU.mult),
                      r=[f'ps{ob}', f'rcp{fin%2}'], w=[f'osb{fin%2}'])
                P.dma('pool', lambda o_=o_, qs=qs, hs=hs: nc.gpsimd.dma_start(out=oaT[hs, qs], in_=o_[:]), r=[f'osb{fin%2}'])
                fin += 1
    if do_nsa:
        nsa_emit(locals())
    stats = P.finalize()
    return nc, stats


def host_consts():
    k = np.arange(128)[:, None]; q = np.arange(128)[None, :]
    return dict(
        c_tri=np.where(k > q, NEG, 0.0).astype(np.float32),
        c_ident=np.eye(128, dtype=np.float32),
        c_ut=(k <= q).astype(np.float32),
    )


def inputs_A(inp, l, b, hp, x_b=None, nsa=True):
    x = inp['x'][b] if x_b is None else x_b
    w_in = inp['w_in'][l]
    offs = np.cumsum([0, 512, 512, 512, 8, 512, 768, 24, 1024, 1024])
    qa = w_in[:, offs[0] + 128 * hp: offs[0] + 128 * hp + 128]
    ka = w_in[:, offs[1] + 128 * hp: offs[1] + 128 * hp + 128]
    va = w_in[:, offs[2] + 128 * hp: offs[2] + 128 * hp + 128]
    f = w_in[:, offs[3] + 2 * hp: offs[3] + 2 * hp + 2]
    m = dict(
        xT=np.ascontiguousarray(x.T),
        gmix=np.ascontiguousarray(inp['norm_mix'][l].reshape(8, 128).T),
        w_fm_fox=np.ascontiguousarray(np.concatenate([qa, ka], axis=1)),
        w_tm_fox=np.ascontiguousarray(np.concatenate([va, f], axis=1)),
        bfg=np.ascontiguousarray(np.broadcast_to(inp['b_forget'][l][2 * hp:2 * hp + 2][None, :], (128, 2))),
    )
    m.update(host_consts())
    if nsa:
        m.update(inputs_nsa(inp, l, b, hp))
    return m


D = 1024
NTB = 2048
EPS = 1e-6


def _norm(P, nc, ps, bank, xt, xtok, g_sb, gtok, out_bf, otok, ones_bf, eps_c, sq, rstd, n):
    for k in range(8):
        s = sq[k % 2]
        P.act(lambda k=k, s=s: nc.scalar.activation(out=s[:, 0:n], in_=xt[:, k, 0:n], func=AF.Square), r=[xtok], w=[f'sq{k%2}'])
        P.pe(lambda k=k, s=s: nc.tensor.matmul(ps[bank][:, 0:n], lhsT=ones_bf[:], rhs=s[:, 0:n], start=(k == 0), stop=(k == 7)),
             r=[f'sq{k%2}', 'ones'], w=[f'ps{bank}'])
    P.act(lambda: nc.scalar.activation(out=rstd[:, 0:n], in_=ps[bank][:, 0:n], func=AF.Ln, bias=eps_c[:], scale=1.0 / D),
          r=[f'ps{bank}', 'eps'], w=['rstd'])
    P.act(lambda: nc.scalar.activation(out=rstd[:, 0:n], in_=rstd[:, 0:n], func=AF.Exp, scale=-0.5), r=['rstd'], w=['rstd'])
    for k in range(8):
        P.dve(lambda k=k: nc.vector.scalar_tensor_tensor(out=out_bf[:, k, 0:n], in0=xt[:, k, 0:n], scalar=g_sb[:, k:k + 1], in1=rstd[:, 0:n],
                                                        op0=ALU.mult, op1=ALU.mult), r=[xtok, 'rstd', gtok], w=[otok])


def build_B1():
    nc = bass.Bass("TRN2", target_bir_lowering=False)
    P = Prog(nc)
    dram = lambda name, shape, dt=F32, kind="ExternalInput": nc.dram_tensor(name, shape, dt, kind=kind).ap()
    xT = dram("xT", [D, NTB]); oaT = dram("oaT", [512, NTB], BF16); onT = dram("onT", [512, NTB], BF16)
    gmix = dram("gmix", [128, 8]); gmlp = dram("gmlp", [128, 8])
    w_of = dram("w_of", [512, D]); w_on = dram("w_on", [512, D])
    w_ga = dram("w_ga", [D, D]); w_gb = dram("w_gb", [D, D]); w_out = dram("w_out", [D, D])
    x1T = dram("x1T", [D, NTB], F32, kind="ExternalOutput")
    h2T = dram("h2T", [D, NTB], BF16, kind="ExternalOutput")
    sb = lambda name, shape, dt=BF16: nc.alloc_sbuf_tensor(name, shape, dt)
    ps = [nc.alloc_psum_tensor(f"ps{i}", [128, 512], F32) for i in range(8)]
    g = nc.gpsimd; sp = nc.sync
    gm = sb("gm", [128, 8], F32); gl = sb("gl", [128, 8], F32)
    ones_bf = sb("ones_bf", [128, 128]); eps_c = sb("eps_c", [128, 1], F32)
    wof = sb("wof", [128, 4, D]); won = sb("won", [128, 4, D])
    wga = sb("wga", [128, 8, D]); wgb = sb("wgb", [128, 8, D]); wout = sb("wout", [128, 8, D])
    P.dma('sp', lambda: sp.dma_start(out=gm[:], in_=gmix[:, :]), w=['gm'])
    P.dma('sp', lambda: sp.dma_start(out=gl[:], in_=gmlp[:, :]), w=['gl'])
    P.dve(lambda: nc.vector.memset(ones_bf[:], 1.0), w=['ones'])
    P.dve(lambda: nc.vector.memset(eps_c[:], EPS), w=['eps'])
    for nm, wt, src, kc in (('wof', wof, w_of, 4), ('won', won, w_on, 4), ('wga', wga, w_ga, 8), ('wgb', wgb, w_gb, 8), ('wout', wout, w_out, 8)):
        for k in range(kc):
            P.dma('pool', lambda wt=wt, src=src, k=k: g.dma_start(out=wt[:, k, :], in_=src[k * 128:(k + 1) * 128, :]), w=[nm])
    xt = sb("xt", [128, 8, 512], F32); oat = sb("oat", [128, 4, 512]); ont = sb("ont", [128, 4, 512])
    sq = [sb(f"sq{i}", [128, 512]) for i in range(2)]
    rstd = sb("rstd", [128, 512], F32)
    hT = sb("hT", [128, 8, 512]); mixT = sb("mixT", [128, 8, 512])
    sa = sb("sa", [128, 512], F32); sb_ = sb("sb_", [128, 512], F32); ma = sb("ma", [128, 512], F32); mb = sb("mb", [128, 512], F32)
    x1 = sb("x1", [128, 8, 512], F32); h2 = sb("h2", [128, 8, 512])
    for t in range(NTB // 512):
        ts_ = slice(t * 512, (t + 1) * 512)
        P.dma('sp', lambda ts_=ts_: sp.dma_start(out=xt[:], in_=xT[:, ts_].rearrange("(k p) n -> p k n", p=128)), w=['xt'])
        P.dma('sp', lambda ts_=ts_: sp.dma_start(out=oat[:], in_=oaT[:, ts_].rearrange("(k p) n -> p k n", p=128)), w=['oat'])
        P.dma('sp', lambda ts_=ts_: sp.dma_start(out=ont[:], in_=onT[:, ts_].rearrange("(k p) n -> p k n", p=128)), w=['ont'])
        _norm(P, nc, ps, 7, xt, 'xt', gm, 'gm', hT, 'hT', ones_bf, eps_c, sq, rstd, 512)
        for dc in range(8):
            ds_ = slice(dc * 128, (dc + 1) * 128)
            b0 = 4 * (dc % 2)
            for k in range(4):
                P.pe(lambda k=k, ds_=ds_, b0=b0: nc.tensor.matmul(ps[b0][:], lhsT=wof[:, k, ds_], rhs=oat[:, k, :], start=(k == 0), stop=(k == 3)),
                     r=['wof', 'oat'], w=[f'ps{b0}'])
            for k in range(8):
                P.pe(lambda k=k, ds_=ds_, b0=b0: nc.tensor.matmul(ps[b0 + 1][:], lhsT=wga[:, k, ds_], rhs=hT[:, k, :], start=(k == 0), stop=(k == 7)),
                     r=['wga', 'hT'], w=[f'ps{b0+1}'])
            for k in range(4):
                P.pe(lambda k=k, ds_=ds_, b0=b0: nc.tensor.matmul(ps[b0 + 2][:], lhsT=won[:, k, ds_], rhs=ont[:, k, :], start=(k == 0), stop=(k == 3)),
                     r=['won', 'ont'], w=[f'ps{b0+2}'])
            for k in range(8):
                P.pe(lambda k=k, ds_=ds_, b0=b0: nc.tensor.matmul(ps[b0 + 3][:], lhsT=wgb[:, k, ds_], rhs=hT[:, k, :], start=(k == 0), stop=(k == 7)),
                     r=['wgb', 'hT'], w=[f'ps{b0+3}'])
            P.act(lambda b0=b0: nc.scalar.activation(out=sa[:], in_=ps[b0 + 1][:], func=AF.Sigmoid), r=[f'ps{b0+1}'], w=['sa'])
            P.act(lambda b0=b0: nc.scalar.activation(out=sb_[:], in_=ps[b0 + 3][:], func=AF.Sigmoid), r=[f'ps{b0+3}'], w=['sb_'])
            P.dve(lambda b0=b0: nc.vector.tensor_tensor(out=ma[:], in0=ps[b0][:], in1=sa[:], op=ALU.mult), r=[f'ps{b0}', 'sa'], w=['ma'])
            P.dve(lambda b0=b0: nc.vector.tensor_tensor(out=mb[:], in0=ps[b0 + 2][:], in1=sb_[:], op=ALU.mult), r=[f'ps{b0+2}', 'sb_'], w=['mb'])
            P.pool(lambda dc=dc: nc.gpsimd.tensor_tensor(out=mixT[:, dc, :], in0=ma[:], in1=mb[:], op=ALU.add), r=['ma', 'mb'], w=['mixT'])
        for dc in range(8):
            ds_ = slice(dc * 128, (dc + 1) * 128)
            b = dc % 2
            for k in range(8):
                P.pe(lambda k=k, ds_=ds_, b=b: nc.tensor.matmul(ps[b][:], lhsT=wout[:, k, ds_], rhs=mixT[:, k, :], start=(k == 0), stop=(k == 7)),
                     r=['wout', 'mixT'], w=[f'ps{b}'])
            P.dve(lambda dc=dc, b=b: nc.vector.tensor_tensor(out=x1[:, dc, :], in0=ps[b][:], in1=xt[:, dc, :], op=ALU.add), r=[f'ps{b}', 'xt'], w=['x1'])
        P.dma('pool', lambda ts_=ts_: g.dma_start(out=x1T[:, ts_].rearrange("(k p) n -> p k n", p=128), in_=x1[:]), r=['x1'])
        _norm(P, nc, ps, 7, x1, 'x1', gl, 'gl', h2, 'h2', ones_bf, eps_c, sq, rstd, 512)
        P.dma('pool', lambda ts_=ts_: g.dma_start(out=h2T[:, ts_].rearrange("(k p) n -> p k n", p=128), in_=h2[:]), r=['h2'])
    return nc, P.finalize()


def build_B2():
    nc = bass.Bass("TRN2", target_bir_lowering=False)
    P = Prog(nc)
    dram = lambda name, shape, dt=F32, kind="ExternalInput": nc.dram_tensor(name, shape, dt, kind=kind).ap()
    x1T = dram("x1T", [D, NTB]); h2T = dram("h2T", [D, NTB], BF16)
    gfin = dram("gfin", [128, 8])
    w_up = dram("w_up", [D, 4096]); w_down = dram("w_down", [4096, D])
    x2T = dram("x2T", [D, NTB], F32, kind="ExternalOutput")
    yT = dram("yT", [D, NTB], F32, kind="ExternalOutput")
    sb = lambda name, shape, dt=BF16: nc.alloc_sbuf_tensor(name, shape, dt)
    ps = [nc.alloc_psum_tensor(f"ps{i}", [128, 512], F32) for i in range(8)]
    g = nc.gpsimd; sp = nc.sync
    gf = sb("gf", [128, 8], F32)
    ones_bf = sb("ones_bf", [128, 128]); eps_c = sb("eps_c", [128, 1], F32)
    wup = sb("wup", [128, 8, 4096]); wdn = sb("wdn", [128, 32, D])
    P.dma('sp', lambda: sp.dma_start(out=gf[:], in_=gfin[:, :]), w=['gf'])
    P.dve(lambda: nc.vector.memset(ones_bf[:], 1.0), w=['ones'])
    P.dve(lambda: nc.vector.memset(eps_c[:], EPS), w=['eps'])
    for k in range(8):
        for hh in range(2):
            P.dma('pool', lambda k=k, hh=hh: g.dma_start(out=wup[:, k, hh * 2048:(hh + 1) * 2048], in_=w_up[k * 128:(k + 1) * 128, hh * 2048:(hh + 1) * 2048]), w=['wup'])
    for k in range(32):
        P.dma('pool', lambda k=k: g.dma_start(out=wdn[:, k, :], in_=w_down[k * 128:(k + 1) * 128, :]), w=['wdn'])
    N = 256
    h2 = sb("h2", [128, 8, N]); x1 = sb("x1", [128, 8, N], F32)
    uT = sb("uT", [128, 32, N]); rl = [sb(f"rl{i}", [128, N], F32) for i in range(2)]
    x2 = sb("x2", [128, 8, N], F32); yo = sb("yo", [128, 8, N], F32)
    sq = [sb(f"sq{i}", [128, 512]) for i in range(2)]
    rstd = sb("rstd", [128, 512], F32)
    for t in range(NTB // N):
        ts_ = slice(t * N, (t + 1) * N)
        P.dma('sp', lambda ts_=ts_: sp.dma_start(out=h2[:], in_=h2T[:, ts_].rearrange("(k p) n -> p k n", p=128)), w=['h2'])
        P.dma('sp', lambda ts_=ts_: sp.dma_start(out=x1[:], in_=x1T[:, ts_].rearrange("(k p) n -> p k n", p=128)), w=['x1'])
        for fc in range(32):
            b = fc % 4
            for k in range(8):
                P.pe(lambda k=k, fc=fc, b=b: nc.tensor.matmul(ps[b][:, 0:N], lhsT=wup[:, k, fc * 128:(fc + 1) * 128], rhs=h2[:, k, :],
                                                              start=(k == 0), stop=(k == 7)), r=['wup', 'h2'], w=[f'ps{b}'])
            r_ = rl[fc % 2]
            P.act(lambda b=b, r_=r_: nc.scalar.activation(out=r_[:], in_=ps[b][:, 0:N], func=AF.Relu), r=[f'ps{b}'], w=[f'rl{fc%2}'])
            if fc % 2 == 0:
                P.dve(lambda fc=fc, r_=r_: nc.vector.tensor_tensor(out=uT[:, fc, :], in0=r_[:], in1=r_[:], op=ALU.mult), r=[f'rl{fc%2}'], w=[f'uT{fc}'])
            else:
                P.pool(lambda fc=fc, r_=r_: nc.gpsimd.tensor_tensor(out=uT[:, fc, :], in0=r_[:], in1=r_[:], op=ALU.mult), r=[f'rl{fc%2}'], w=[f'uT{fc}'])
        for dc in range(8):
            b = 4 + dc % 2
            for fc in range(32):
                P.pe(lambda fc=fc, dc=dc, b=b: nc.tensor.matmul(ps[b][:, 0:N], lhsT=wdn[:, fc, dc * 128:(dc + 1) * 128], rhs=uT[:, fc, :],
                                                                start=(fc == 0), stop=(fc == 31)), r=['wdn', f'uT{fc}'], w=[f'ps{b}'])
            P.dve(lambda dc=dc, b=b: nc.vector.tensor_tensor(out=x2[:, dc, :], in0=ps[b][:, 0:N], in1=x1[:, dc, :], op=ALU.add), r=[f'ps{b}', 'x1'], w=['x2'])
        P.dma('pool', lambda ts_=ts_: g.dma_start(out=x2T[:, ts_].rearrange("(k p) n -> p k n", p=128), in_=x2[:]), r=['x2'])
        for k in range(8):
            s = sq[k % 2]
            P.act(lambda k=k, s=s: nc.scalar.activation(out=s[:, 0:N], in_=x2[:, k, :], func=AF.Square), r=['x2'], w=[f'sq{k%2}'])
            P.pe(lambda k=k, s=s: nc.tensor.matmul(ps[7][:, 0:N], lhsT=ones_bf[:], rhs=s[:, 0:N], start=(k == 0), stop=(k == 7)),
                 r=[f'sq{k%2}', 'ones'], w=['ps7'])
        P.act(lambda: nc.scalar.activation(out=rstd[:, 0:N], in_=ps[7][:, 0:N], func=AF.Ln, bias=eps_c[:], scale=1.0 / D), r=['ps7', 'eps'], w=['rstd'])
        P.act(lambda: nc.scalar.activation(out=rstd[:, 0:N], in_=rstd[:, 0:N], func=AF.Exp, scale=-0.5), r=['rstd'], w=['rstd'])
        for k in range(8):
            P.dve(lambda k=k: nc.vector.scalar_tensor_tensor(out=yo[:, k, :], in0=x2[:, k, :], scalar=gf[:, k:k + 1], in1=rstd[:, 0:N],
                                                            op0=ALU.mult, op1=ALU.mult), r=['x2', 'rstd', 'gf'], w=['yo'])
        P.dma('pool', lambda ts_=ts_: g.dma_start(out=yT[:, ts_].rearrange("(k p) n -> p k n", p=128), in_=yo[:]), r=['yo'])
    return nc, P.finalize()


_PROGS = {}


def _prog(name, fn):
    if name not in _PROGS:
        _PROGS[name] = fn()[0]
    return _PROGS[name]


def _lay(gv):
    return np.ascontiguousarray(np.asarray(gv, np.float32).reshape(8, 128).T)


def kernel(**inputs):
    import ml_dtypes
    from concourse.bass_utils import run_bass_kernel_spmd
    inp = {k: np.asarray(v) for k, v in inputs.items()}
    B = 2
    cores = list(range(8))
    offs = np.cumsum([0, 512, 512, 512, 8, 512, 768, 24, 1024, 1024])
    x = [np.ascontiguousarray(inp['x'][b]) for b in range(B)]
    y = None
    for l in range(2):
        ncA = _prog('A', lambda: build_A(True, True))
        mapsA = [inputs_A(inp, l, c // 4, c % 4, x_b=x[c // 4]) for c in cores]
        resA = run_bass_kernel_spmd(ncA, mapsA, core_ids=cores).results
        oa = [np.concatenate([resA[4 * b + hp]['oaT'] for hp in range(4)], axis=0) for b in range(B)]
        on = [np.concatenate([resA[4 * b + hp]['onT'] for hp in range(4)], axis=0) for b in range(B)]
        w_in = inp['w_in'][l]
        ncB1 = _prog('B1', build_B1)
        mapsB1 = []
        for c in cores:
            b, r = c // 4, c % 4
            rs = slice(2048 * r, 2048 * r + 2048)
            mapsB1.append(dict(xT=np.ascontiguousarray(x[b][rs].T), oaT=np.ascontiguousarray(oa[b][:, rs]), onT=np.ascontiguousarray(on[b][:, rs]),
                               gmix=_lay(inp['norm_mix'][l]), gmlp=_lay(inp['norm_mlp'][l]),
                               w_of=np.ascontiguousarray(inp['w_o_fox'][l]), w_on=np.ascontiguousarray(inp['w_o_nsa'][l]),
                               w_ga=np.ascontiguousarray(w_in[:, offs[7]:offs[8]]), w_gb=np.ascontiguousarray(w_in[:, offs[8]:offs[9]]),
                               w_out=np.ascontiguousarray(inp['w_out'][l])))
        resB1 = run_bass_kernel_spmd(ncB1, mapsB1, core_ids=cores).results
        ncB2 = _prog('B2', build_B2)
        mapsB2 = [dict(x1T=resB1[c]['x1T'], h2T=resB1[c]['h2T'], gfin=_lay(inp['norm_final']),
                       w_up=np.ascontiguousarray(inp['w_up'][l]), w_down=np.ascontiguousarray(inp['w_down'][l])) for c in cores]
        resB2 = run_bass_kernel_spmd(ncB2, mapsB2, core_ids=cores).results
        x = [np.ascontiguousarray(np.concatenate([resB2[4 * b + r]['x2T'].T for r in range(4)], axis=0)) for b in range(B)]
        y = np.stack([np.concatenate([resB2[4 * b + r]['yT'].T for r in range(4)], axis=0) for b in range(B)], axis=0)
    return np.ascontiguousarray(y.astype(np.float32))
```

```python
import numpy as np
import concourse.bass as bass
import concourse.mybir as mybir

F32 = mybir.dt.float32
BF16 = mybir.dt.bfloat16
AF = mybir.ActivationFunctionType
ALU = mybir.AluOpType


class Prog:
    NDMA = 6

    def __init__(self, nc):
        self.nc = nc
        self.eng = {'pe': nc.tensor, 'act': nc.scalar, 'dve': nc.vector,
                    'pool': nc.gpsimd, 'sp': nc.sync}
        self.ops = []

    def add(self, eng, fn, r=(), w=(), dma=False):
        self.ops.append((eng, fn, tuple(r), tuple(w), dma))

    def pe(self, fn, r=(), w=()): self.add('pe', fn, r, w)
    def act(self, fn, r=(), w=()): self.add('act', fn, r, w)
    def dve(self, fn, r=(), w=()): self.add('dve', fn, r, w)
    def pool(self, fn, r=(), w=()): self.add('pool', fn, r, w)
    def dma(self, q, fn, r=(), w=()): self.add(q, fn, r, w, True)

    def finalize(self):
        nc = self.nc
        ops = self.ops
        n = len(ops)
        last_w = {}
        readers = {}
        deps = [None] * n
        for i, (eng, fn, r, w, dma) in enumerate(ops):
            d = {}
            for t in r:
                j = last_w.get(t)
                if j is not None:
                    d[j] = 'raw'
            for t in w:
                j = last_w.get(t)
                if j is not None and j not in d:
                    d[j] = 'waw'
                for j in readers.get(t, ()):
                    if j not in d:
                        d[j] = 'war'
            for t in r:
                readers.setdefault(t, []).append(i)
            for t in w:
                last_w[t] = i
                readers[t] = []
            keep = []
            for j, kind in d.items():
                if j == i:
                    continue
                je, _, _, _, jdma = ops[j]
                if jdma:
                    keep.append(j)
                elif je == eng and not dma:
                    if (kind == 'raw' and eng != 'pe') or eng == 'pool':
                        keep.append(j)
                elif je == eng and dma:
                    keep.append(j)
                else:
                    keep.append(j)
            deps[i] = keep
        signal = [False] * n
        for i in range(n):
            for j in deps[i]:
                signal[j] = True
        csem = {e: nc.alloc_semaphore(f"c_{e}") for e in ['pe', 'act', 'dve', 'pool']}
        dsem = {q: [nc.alloc_semaphore(f"d_{q}{k}") for k in range(self.NDMA)] for q in ['sp', 'pool']}
        ccount = {e: 0 for e in csem}
        dcount = {q: 0 for q in dsem}
        semval = [None] * n
        for i, (eng, fn, r, w, dma) in enumerate(ops):
            if dma:
                k = dcount[eng]
                dcount[eng] += 1
                semval[i] = (dsem[eng][k % self.NDMA], 16 * (k // self.NDMA + 1), k)
            elif signal[i]:
                ccount[eng] += 1
                semval[i] = (csem[eng], ccount[eng], None)
        waited = {e: {} for e in self.eng}
        nwaits = 0
        for i, (eng, fn, r, w, dma) in enumerate(ops):
            E = self.eng[eng]
            wl = {}
            for j in deps[i]:
                s, v, _ = semval[j]
                key = id(s)
                if waited[eng].get(key, 0) >= v:
                    continue
                if key not in wl or wl[key][1] < v:
                    wl[key] = (s, v)
            if dma:
                s, v, k = semval[i]
                if k >= self.NDMA:
                    key = id(s)
                    pv = v - 16
                    if waited[eng].get(key, 0) < pv and (key not in wl or wl[key][1] < pv):
                        wl[key] = (s, pv)
            for key, (s, v) in wl.items():
                E.wait_ge(s, v)
                waited[eng][key] = v
                nwaits += 1
            ins = fn()
            if dma:
                ins.then_inc(semval[i][0], 16)
            elif signal[i]:
                ins.then_inc(semval[i][0], 1)
        E = nc.sync
        for q in dsem:
            tot = dcount[q]
            for k in range(min(tot, self.NDMA)):
                cnt = (tot - 1 - k) // self.NDMA + 1
                E.wait_ge(dsem[q][k], 16 * cnt)
        end = nc.alloc_semaphore("c_end")
        for e in ['pe', 'act', 'dve', 'pool']:
            self.eng[e].drain().then_inc(end, 1)
        E.wait_ge(end, 4)
        for s_ in list(csem.values()) + [x for q in dsem for x in dsem[q]] + [end]:
            E.sem_clear(s_)
        self.stats = dict(n=n, nwaits=nwaits, counts=dict(ccount), dmas=dict(dcount))
        return self.stats


T = 8192
D = 1024
NCH = 16
NKT = 64
NEG = -30000.0
NW = 774


def nsa_declare(dram, nc):
    d = {}
    d['w_nsa'] = dram("w_nsa", [D, NW])
    d['cosT'] = dram("cosT", [128, T]); d['sinT'] = dram("sinT", [128, T])
    d['cosC'] = dram("cosC", [128, 512]); d['sinC'] = dram("sinC", [128, 512])
    d['c_perm'] = dram("c_perm", [128, 128]); d['c_triU'] = dram("c_triU", [128, 128])
    d['c_ind'] = dram("c_ind", [64, T])
    d['c_selmap'] = dram("c_selmap", [128, 4 * 128])
    d['c_cmask'] = dram("c_cmask", [128, 17 * 128])
    d['c_wk'] = dram("c_wk", [128, 256]); d['c_wa'] = dram("c_wa", [128, 256])
    d['posT'] = dram("posT", [128, 32])
    d['w1'] = dram("w1", [2, 2048, 256])
    d['b1T'] = dram("b1T", [128, 4])
    d['w2kd'] = dram("w2kd", [256, 128]); d['w2v'] = dram("w2v", [256, 64])
    d['b2k'] = dram("b2k", [128, 1]); d['b2v'] = dram("b2v", [128, 64])
    d['kvc'] = nc.dram_tensor("kvc", [128, T], BF16, kind="Internal").ap()
    d['onT'] = dram("onT", [128, T], BF16, kind="ExternalOutput")
    return d


def nsa_emit(L):
    nc = L['nc']; P = L['P']; ps = L['ps']; sb = L['sb']; dd = L['nsa_dram']
    R0 = L['R0']; R1 = L['R1']; R2 = L['R2']; xin0 = L['xin0']; hT = L['hT']
    ident = L['ident']; tri = L['tri']; norm_chunk = L['norm_chunk']; fm_proj = L['fm_proj']
    nq_limit = L['nqt_limit']
    ksT2 = R0; kwT2 = R1
    qbT = R2[:].rearrange("p (i t) -> p i t", i=2)

    wn = sb("wn", [128, 8, NW])
    perm = sb("perm", [128, 128]); triU = sb("triU", [128, 128])
    selmap = sb("selmap", [128, 4, 128]); cmask = sb("cmask", [128, 17, 128])
    wk = sb("wk", [128, 256], F32); wa = sb("wa", [128, 256], F32)
    posT = sb("posT_sb", [128, 32])
    b1T = sb("b1T_sb", [128, 4], F32)
    w2kd = sb("w2kd_sb", [128, 2, 128]); w2v = sb("w2v_sb", [128, 2, 64])
    b2k = sb("b2k_sb", [128, 1], F32); b2v = sb("b2v_sb", [128, 64], F32)
    cosC = sb("cosC_sb", [128, 512], F32); sinC = sb("sinC_sb", [128, 512], F32)
    g = nc.gpsimd; sp = nc.sync
    P.dma('pool', lambda: g.dma_start(out=wn[:], in_=dd['w_nsa'].rearrange("(c p) n -> p c n", p=128)), w=['wn'])
    P.dma('pool', lambda: g.dma_start(out=perm[:], in_=dd['c_perm'][:, :]), w=['perm'])
    P.dma('pool', lambda: g.dma_start(out=triU[:], in_=dd['c_triU'][:, :]), w=['triU'])
    P.dma('pool', lambda: g.dma_start(out=selmap[:], in_=dd['c_selmap'].rearrange("p (k n) -> p k n", n=128)), w=['selmap'])
    P.dma('pool', lambda: g.dma_start(out=cmask[:], in_=dd['c_cmask'].rearrange("p (k n) -> p k n", n=128)), w=['cmask'])
    P.dma('pool', lambda: g.dma_start(out=posT[:], in_=dd['posT'][:, :]), w=['posT'])
    P.dma('pool', lambda: g.dma_start(out=w2kd[:], in_=dd['w2kd'].rearrange("(c p) n -> p c n", p=128)), w=['w2kd'])
    P.dma('pool', lambda: g.dma_start(out=w2v[:], in_=dd['w2v'].rearrange("(c p) n -> p c n", p=128)), w=['w2v'])
    P.dma('sp', lambda: sp.dma_start(out=wk[:], in_=dd['c_wk'][:, :]), w=['wk'])
    P.dma('sp', lambda: sp.dma_start(out=wa[:], in_=dd['c_wa'][:, :]), w=['wa'])
    P.dma('sp', lambda: sp.dma_start(out=b1T[:], in_=dd['b1T'][:, :]), w=['b1T'])
    P.dma('sp', lambda: sp.dma_start(out=b2k[:], in_=dd['b2k'][:, :]), w=['b2k'])
    P.dma('sp', lambda: sp.dma_start(out=b2v[:], in_=dd['b2v'][:, :]), w=['b2v'])
    P.dma('sp', lambda: sp.dma_start(out=cosC[:], in_=dd['cosC'][:, :]), w=['cosC'])
    P.dma('sp', lambda: sp.dma_start(out=sinC[:], in_=dd['sinC'][:, :]), w=['sinC'])

    vsw = sb("vsw", [128, NKT, 4, 64])
    gsig = sb("gsig", [128, NKT, 6], F32)
    P.pool(lambda: nc.gpsimd.memset(vsw[:, :, 1, :], 1.0), w=['vsw_ones'])
    P.pool(lambda: nc.gpsimd.memset(vsw[:, :, 3, :], 1.0), w=['vsw_ones'])
    ct = sb("ct", [128, 512], F32); st = sb("st", [128, 512], F32)
    xs = sb("xs", [128, 512]); t1 = sb("t1", [128, 512], F32); t2 = sb("t2", [128, 512], F32)
    pt_sh = L['pt']; rcp_sh = L['rcp']; osb_sh = L['osb']
    stg = sb("stg", [128, 512])

    def rope(src_bank, cos_ap, sin_ap, dst_ap, n, scale, rtok, wtok, bias=None, rows=128):
        if bias is None:
            P.act(lambda: nc.scalar.activation(out=xs[:, 0:n], in_=ps[src_bank][:, 0:n], func=AF.Copy, scale=scale),
                  r=[f'ps{src_bank}'], w=['xs'])
        else:
            P.act(lambda: nc.scalar.activation(out=xs[:, 0:n], in_=ps[src_bank][:, 0:n], func=AF.Identity, bias=bias, scale=scale),
                  r=[f'ps{src_bank}', 'b2k'], w=['xs'])
        P.pe(lambda: nc.tensor.matmul(ps[7][:, 0:n], lhsT=perm[:], rhs=xs[:, 0:n], start=True, stop=True), r=['xs', 'perm'], w=['ps7'])
        P.dve(lambda: nc.vector.tensor_tensor(out=t1[:, 0:n], in0=xs[:, 0:n], in1=cos_ap, op=ALU.mult), r=['xs', *rtok], w=['t1'])
        P.dve(lambda: nc.vector.tensor_tensor(out=t2[:, 0:n], in0=ps[7][:, 0:n], in1=sin_ap, op=ALU.mult), r=['ps7', *rtok], w=['t2'])
        P.pool(lambda: nc.gpsimd.tensor_tensor(out=dst_ap, in0=t1[0:rows, 0:n], in1=t2[0:rows, 0:n], op=ALU.add), r=['t1', 't2'], w=wtok)

    for c in range(NCH):
        norm_chunk(c)
        cs = slice(c * 512, (c + 1) * 512)
        P.dma('sp', lambda cs=cs: sp.dma_start(out=ct[:], in_=dd['cosT'][:, cs]), w=['ct'])
        P.dma('sp', lambda cs=cs: sp.dma_start(out=st[:], in_=dd['sinT'][:, cs]), w=['st'])
        dsts = [(qbT[:, 0, cs], 0.125, [f'qb0_{c}', f'va{c // 2}']),
                (qbT[:, 1, cs], 0.125, [f'qb1_{c}', f'va{8 + c // 2}']),
                (ksT2[0:64, cs], 1.0, [f'ks{c}', f'qa{c}']),
                (kwT2[:, cs], 1.0, [f'kw{c}', f'ka{c}'])]
        for ti, (dst, scl, wtok) in enumerate(dsts):
            fm_proj(wn, 128 * ti, 6, ['wn'])
            rope(6, ct[:], st[:], dst, 512, scl, ['ct', 'st'], wtok, rows=(64 if ti == 2 else 128))
        P.dma('pool', lambda cs=cs: g.dma_start(out=ksT2[64:128, cs], in_=dd['c_ind'][:, cs]), w=[f'ksi{c}', f'qa{c}'])
        fm_proj(wn, 512, 6, ['wn'])
        P.act(lambda: nc.scalar.copy(out=stg[:], in_=ps[6][:]), r=['ps6'], w=['stg'])
        P.dma('pool', lambda cs=cs: g.dma_start(out=dd['kvc'][:, cs], in_=stg[:]), r=['stg'], w=['kvc'])
        for t in range(4):
            for k in range(8):
                P.pe(lambda k=k, t=t: nc.tensor.matmul(ps[6][:, t * 128:(t + 1) * 128], lhsT=hT[:, k, t * 128:(t + 1) * 128],
                                                       rhs=wn[:, k, 640:768], start=(k == 0), stop=(k == 7)),
                     r=[f'hT{k}', 'wn'], w=['ps6'])
        for t in range(4):
            for k in range(8):
                P.pe(lambda k=k, t=t: nc.tensor.matmul(ps[7][:, t * 6:(t + 1) * 6], lhsT=hT[:, k, t * 128:(t + 1) * 128],
                                                       rhs=wn[:, k, 768:774], start=(k == 0), stop=(k == 7)),
                     r=[f'hT{k}', 'wn'], w=['ps7'])
        psv = ps[6][:].rearrange("p (t j d) -> p t j d", t=4, j=2)
        P.act(lambda c=c, psv=psv: nc.scalar.copy(out=vsw[:, 4 * c:4 * c + 4, 0, :], in_=psv[:, :, 0, :]), r=['ps6', 'vsw_ones'], w=[f'vsw{c}'])
        P.dve(lambda c=c, psv=psv: nc.vector.tensor_copy(out=vsw[:, 4 * c:4 * c + 4, 2, :], in_=psv[:, :, 1, :]), r=['ps6', 'vsw_ones'], w=[f'vsw{c}'])
        P.act(lambda c=c: nc.scalar.activation(out=gsig[:, 4 * c:4 * c + 4, :], in_=ps[7][:, 0:24].rearrange("p (t j) -> p t j", j=6),
                                               func=AF.Sigmoid), r=['ps7'], w=['gsig'])

    if L.get('stage', 9) < 2:
        return
    cb = xin0[:].bitcast(BF16).rearrange("p k n -> p (k n)")
    cbv = cb.rearrange("p (n s) -> p n s", s=16)
    w1b = L['w1b']
    hb = sb("hb", [128, 2], F32)
    hx = t1; gu = t2
    hid = [sb(f"hid{i}", [128, 512]) for i in range(2)]
    kcT2 = sb("kcT2", [128, 512]); vcA = sb("vcA", [128, 4, 128])
    P.dve(lambda: nc.vector.memset(hid[0][:, 511:512], 0.0), w=['hid0'])
    P.dve(lambda: nc.vector.memset(hid[1][:, 511:512], 0.0), w=['hid1'])
    P.dve(lambda: nc.vector.memset(kcT2[:, 511:512], 0.0), w=['kcT2'])
    P.pool(lambda: nc.gpsimd.memset(vcA[:, :, 64:128], 1.0), w=['vcA'])
    for mlp in range(2):
        P.dma('pool', lambda mlp=mlp: g.dma_start(out=w1b[:], in_=dd['w1'][mlp].rearrange("(c p) h -> p c h", p=128)), w=['w1b', 'wfm', 'wtm'])
        P.dma('sp', lambda mlp=mlp: sp.dma_start(out=cb[0:64, :], in_=dd['kvc'][64 * mlp:64 * mlp + 64, :]), r=['kvc'], w=['xin'])
        P.dma('sp', lambda mlp=mlp: sp.dma_start(out=cb[64:128, 0:T - 1], in_=dd['kvc'][64 * mlp:64 * mlp + 64, 1:T]), r=['kvc'], w=['xin'])
        for mt in range(2):
            for c16 in range(16):
                P.pe(lambda mt=mt, c16=c16, mlp=mlp: nc.tensor.matmul(ps[7][:, mt:mt + 1], lhsT=w1b[:, c16, mt * 128:(mt + 1) * 128],
                                                                      rhs=posT[:, 16 * mlp + c16:16 * mlp + c16 + 1],
                                                                      start=(c16 == 0), stop=(c16 == 15)),
                     r=['w1b', 'posT'], w=['ps7'])
        P.dve(lambda mlp=mlp: nc.vector.tensor_tensor(out=hb[:], in0=ps[7][:, 0:2], in1=b1T[:, 2 * mlp:2 * mlp + 2], op=ALU.add),
              r=['ps7', 'b1T'], w=['hb'])
        for mt in range(2):
            for c16 in range(16):
                P.pe(lambda mt=mt, c16=c16: nc.tensor.matmul(ps[6][:, 0:511], lhsT=w1b[:, c16, mt * 128:(mt + 1) * 128],
                                                             rhs=cbv[:, (2 * c16) // 16:(2 * c16) // 16 + 511, (2 * c16) % 16], start=(c16 == 0), stop=(c16 == 15)),
                     r=['w1b', 'xin'], w=['ps6'])
            P.act(lambda mt=mt: nc.scalar.activation(out=hx[:, 0:511], in_=ps[6][:, 0:511], func=AF.Identity, bias=hb[:, mt:mt + 1], scale=1.0),
                  r=['ps6', 'hb'], w=['t1'])
            P.dve(lambda: nc.vector.tensor_tensor(out=gu[:, 0:511], in0=hx[:, 0:511], in1=hx[:, 0:511], op=ALU.mult), r=['t1'], w=['t2'])
            P.dve(lambda: nc.vector.tensor_scalar(out=gu[:, 0:511], in0=gu[:, 0:511], scalar1=0.044715, scalar2=1.0, op0=ALU.mult, op1=ALU.add),
                  r=['t2'], w=['t2'])
            P.dve(lambda: nc.vector.tensor_tensor(out=gu[:, 0:511], in0=gu[:, 0:511], in1=hx[:, 0:511], op=ALU.mult), r=['t2', 't1'], w=['t2'])
            P.act(lambda: nc.scalar.activation(out=gu[:, 0:511], in_=gu[:, 0:511], func=AF.Sigmoid, scale=1.5957691216057308), r=['t2'], w=['t2'])
            P.dve(lambda mt=mt: nc.vector.tensor_tensor(out=hid[mt][:, 0:511], in0=hx[:, 0:511], in1=gu[:, 0:511], op=ALU.mult),
                  r=['t2', 't1'], w=[f'hid{mt}'])
        if mlp == 0:
            for mt in range(2):
                P.pe(lambda mt=mt: nc.tensor.matmul(ps[6][:, 0:511], lhsT=w2kd[:, mt, :], rhs=hid[mt][:, 0:511], start=(mt == 0), stop=(mt == 1)),
                     r=[f'hid{mt}', 'w2kd'], w=['ps6'])
            rope(6, cosC[:, 0:511], sinC[:, 0:511], kcT2[:, 0:511], 511, 1.0, ['cosC', 'sinC'], ['kcT2'], bias=b2k[:])
        else:
            for nt in range(4):
                for mt in range(2):
                    P.pe(lambda mt=mt, nt=nt: nc.tensor.matmul(ps[6][:, nt * 64:(nt + 1) * 64], lhsT=hid[mt][:, nt * 128:(nt + 1) * 128],
                                                               rhs=w2v[:, mt, :], start=(mt == 0), stop=(mt == 1)),
                         r=[f'hid{mt}', 'w2v'], w=['ps6'])
            P.dve(lambda: nc.vector.tensor_tensor(out=vcA[:, :, 0:64], in0=ps[6][:, 0:256].rearrange("p (a b) -> p a b", b=64),
                                                  in1=b2v[:].unsqueeze(1).to_broadcast([128, 4, 64]), op=ALU.add),
                  r=['ps6', 'b2v'], w=['vcA'])

    if L.get('stage', 9) < 3:
        return
    psS = L['psS']
    ptn = pt_sh
    rs4 = sb("rs4", [128, 4], F32); rc4 = sb("rc4", [128, 4], F32)
    imp = sb("imp", [128, 128], F32); impm = sb("impm", [128, 128], F32); imp2 = sb("imp2", [128, 128], F32)
    m8 = sb("m8", [128, 16], F32)
    Mb = sb("Mb", [128, 256])
    grep = sb("grep", [128, 2, 6, 64])
    rcn = rcp_sh[0]; rgd = t1
    acc = sb("acc", [64, 512], F32); tmpc = rcp_sh[1]
    obn = osb_sh[0]
    onT = dd['onT']
    it = [0]
    NQ = 32 if nq_limit is None else nq_limit
    OS, OW, OC, UB = 4, 5, 6, 7
    qpc = [sb(f"qpc{b}", [128, 2, 2, 128]) for b in range(2)]
    qsa = [sb(f"qsa{b}", [128, 2, 2, 256]) for b in range(2)]
    qsw = [sb(f"qsw{b}", [128, 2, 256]) for b in range(2)]
    for b in range(2):
        P.pool(lambda b=b: nc.gpsimd.memset(qpc[b][:], 0.0), w=[f'qpc{b}'])
        P.pool(lambda b=b: nc.gpsimd.memset(qsw[b][:], 0.0), w=[f'qsw{b}'])
    ncmp = [0]

    def sbuf_of(i):
        d = i % 2
        return psS[d], [f'ps{2 * d}', f'ps{2 * d + 1}'], ptn[i % 3], f'pt{i % 3}'

    Mb2 = [Mb, sb("Mb1", [128, 256])]
    occ = [ct[0:64, :], st[0:64, :]]; occtok = ['ct', 'st']
    U = ps[UB]

    def part1(qt2):
        cq = qt2 // 2
        for p in range(2):
            qb = 2 * qt2 + p
            ntmax = qb // 16
            qc = qpc[ncmp[0] % 2]; qctok = f'qpc{ncmp[0] % 2}'; ncmp[0] += 1
            for j in range(2):
                P.pool(lambda j=j, qc=qc, qb=qb: nc.gpsimd.tensor_copy(out=qc[64 * j:64 * j + 64, j, :, :],
                                                                        in_=qbT[64 * j:64 * j + 64, :, qb * 128:(qb + 1) * 128]),
                       r=[f'qb0_{cq}', f'qb1_{cq}'], w=[qctok])
            for nt in range(ntmax + 1):
                i = it[0]; it[0] += 1
                Sd, stok, pt, ptok = sbuf_of(i)
                Dd = qb - 16 * nt
                masked = Dd <= 16
                for j in range(2):
                    for ti in range(2):
                        P.pe(lambda ti=ti, j=j, Sd=Sd, nt=nt, masked=masked, qc=qc: nc.tensor.matmul(
                            Sd[:, j * 512 + ti * 128:j * 512 + ti * 128 + 128], lhsT=kcT2[:, nt * 128:(nt + 1) * 128],
                            rhs=qc[:, j, ti, :], start=(ti == 0), stop=not masked),
                            r=['kcT2', qctok], w=[stok[j]])
                if masked:
                    for j in range(2):
                        for ti in range(2):
                            P.pe(lambda Sd=Sd, Dd=Dd, j=j, ti=ti: nc.tensor.matmul(Sd[:, j * 512 + ti * 128:j * 512 + ti * 128 + 128], lhsT=ident[:],
                                                                                  rhs=cmask[:, Dd, :], start=False, stop=True),
                                 r=['ident', 'cmask'], w=[stok[j]])
                Sv = Sd[:].rearrange("p (j q) -> p j q", j=2)[:, :, 0:256]
                pv_ = pt[:].rearrange("p (j q) -> p j q", j=2)
                P.act(lambda Sv=Sv, pv_=pv_: nc.scalar.activation(out=pv_, in_=Sv, func=AF.Exp), r=stok, w=[ptok])
                for j in range(2):
                    for ti in range(2):
                        h = 2 * j + ti
                        P.pe(lambda h=h, j=j, ti=ti, pt=pt, nt=nt, ntmax=ntmax: nc.tensor.matmul(
                            U[:, h * 128:(h + 1) * 128], lhsT=pt[:, j * 256 + ti * 128:j * 256 + ti * 128 + 128], rhs=selmap[:, nt, :],
                            start=(nt == 0 and h == 0), stop=(nt == ntmax)), r=[ptok, 'selmap'], w=[f'ps{UB}'])
                for j in range(2):
                    P.pe(lambda j=j, pt=pt, nt=nt, ntmax=ntmax, p=p: nc.tensor.matmul(
                        ps[OC][:, j * 256 + p * 128:j * 256 + p * 128 + 128], lhsT=vcA[:, nt, :], rhs=pt[:, j * 256:j * 256 + 128],
                        start=(nt == 0 and p == 0 and j == 0), stop=(nt == ntmax)), r=[ptok, 'vcA'], w=[f'ps{OC}'])
            P.dve(lambda: nc.vector.tensor_reduce(out=rs4[:], in_=U[:].rearrange("p (a b) -> p a b", a=4), axis=mybir.AxisListType.X, op=ALU.add),
                  r=[f'ps{UB}'], w=['rs4'])
            P.dve(lambda: nc.vector.tensor_scalar(out=rs4[:], in0=rs4[:], scalar1=1e-30, scalar2=None, op0=ALU.max), r=['rs4'], w=['rs4'])
            P.dve(lambda: nc.vector.reciprocal(out=rc4[:], in_=rs4[:]), r=['rs4'], w=['rc4'])
            P.dve(lambda: nc.vector.tensor_scalar(out=imp[:], in0=U[:, 0:128], scalar1=rc4[:, 0:1], scalar2=None, op0=ALU.mult),
                  r=[f'ps{UB}', 'rc4'], w=['imp'])
            for h4 in range(1, 4):
                P.dve(lambda h4=h4: nc.vector.scalar_tensor_tensor(out=imp[:], in0=U[:, h4 * 128:(h4 + 1) * 128], scalar=rc4[:, h4:h4 + 1],
                                                                   in1=imp[:], op0=ALU.mult, op1=ALU.add), r=[f'ps{UB}', 'rc4', 'imp'], w=['imp'])
            sl = slice(128 - 2 * qb, 256 - 2 * qb)
            P.dve(lambda sl=sl: nc.vector.tensor_tensor(out=impm[:], in0=imp[:], in1=wk[:, sl], op=ALU.mult), r=['imp', 'wk'], w=['impm'])
            P.dve(lambda sl=sl: nc.vector.tensor_tensor(out=impm[:], in0=impm[:], in1=wa[:, sl], op=ALU.add), r=['impm', 'wa'], w=['impm'])
            P.dve(lambda: nc.vector.memset(impm[:, 0:1], 2e6), r=['impm'], w=['impm'])
            P.dve(lambda: nc.vector.max(out=m8[:, 0:8], in_=impm[:]), r=['impm'], w=['m8a'])
            P.dve(lambda: nc.vector.match_replace(out=imp2[:], in_to_replace=m8[:, 0:8], in_values=impm[:], imm_value=-1e9),
                  r=['impm', 'm8a'], w=['imp2'])
            P.dve(lambda: nc.vector.max(out=m8[:, 8:16], in_=imp2[:]), r=['imp2'], w=['m8b'])
            mb = Mb2[p]
            for hh in range(2):
                P.dve(lambda hh=hh, mb=mb: nc.vector.tensor_scalar(out=mb[:, hh * 128:(hh + 1) * 128], in0=impm[:], scalar1=m8[:, 15:16], scalar2=NEG,
                                                                   op0=ALU.is_lt, op1=ALU.mult), r=['impm', 'm8b'], w=[f'Mb{p}'])
        oc_ = occ[qt2 % 2]; octok = occtok[qt2 % 2]
        if qt2 == 0:
            P.dve(lambda: nc.vector.tensor_scalar(out=rgd[64:128, :], in0=ps[OC][64:128, :], scalar1=1e-30, scalar2=None, op0=ALU.max),
                  r=[f'ps{OC}'], w=['t1'])
            P.dve(lambda: nc.vector.reciprocal(out=rcn[:], in_=rgd[64:128, :]), r=['t1'], w=['rcp0'])
        else:
            P.dve(lambda: nc.vector.reciprocal(out=rcn[:], in_=ps[OC][64:128, :]), r=[f'ps{OC}'], w=['rcp0'])
        P.dve(lambda oc_=oc_: nc.vector.tensor_tensor(out=oc_, in0=ps[OC][0:64, :], in1=rcn[:], op=ALU.mult), r=[f'ps{OC}', 'rcp0'], w=[octok])

    def part2(qt2):
        cq = qt2 // 2
        qa_ = qsa[qt2 % 2]; qatok = f'qsa{qt2 % 2}'
        qw_ = qsw[qt2 % 2]; qwtok = f'qsw{qt2 % 2}'
        for p in range(2):
            mb = Mb2[p]
            P.pe(lambda p=p, mb=mb: nc.tensor.matmul(U[:, p * 256:p * 256 + 128], lhsT=mb[:, 64:192], rhs=ident[:], start=True, stop=True),
                 r=[f'Mb{p}', 'ident'], w=[f'ps{UB}'])
            P.pe(lambda p=p, mb=mb: nc.tensor.matmul(U[:, p * 256 + 128:p * 256 + 256], lhsT=mb[:, 0:128], rhs=ident[:], start=False, stop=True),
                 r=[f'Mb{p}', 'ident'], w=[f'ps{UB}'])
            Uv = U[64:128, p * 256:(p + 1) * 256].rearrange("r (h q) -> r h q", h=2)
            P.act(lambda p=p, qa_=qa_, Uv=Uv: nc.scalar.copy(out=qa_[64:128, 0, :, p * 128:(p + 1) * 128], in_=Uv), r=[f'ps{UB}'], w=[qatok])
            P.dve(lambda p=p, qa_=qa_, Uv=Uv: nc.vector.tensor_copy(out=qa_[64:128, 1, :, p * 128:(p + 1) * 128], in_=Uv), r=[f'ps{UB}'], w=[qatok])
        for j in range(2):
            src = R2[64 * j:64 * j + 64, qt2 * 256:(qt2 + 1) * 256]
            for hh in range(2):
                if j == 0:
                    P.pool(lambda hh=hh, qa_=qa_, src=src: nc.gpsimd.tensor_copy(out=qa_[0:64, 0, hh, :], in_=src), r=[f'qb0_{cq}'], w=[qatok])
                else:
                    P.dve(lambda hh=hh, qa_=qa_, src=src: nc.vector.tensor_copy(out=qa_[0:64, 1, hh, :], in_=src), r=[f'qb0_{cq}'], w=[qatok])
            if j == 0:
                P.pool(lambda qw_=qw_, src=src: nc.gpsimd.tensor_copy(out=qw_[0:64, 0, :], in_=src), r=[f'qb0_{cq}'], w=[qwtok])
            else:
                P.dve(lambda qw_=qw_, src=src: nc.vector.tensor_copy(out=qw_[0:64, 1, :], in_=src), r=[f'qb0_{cq}'], w=[qwtok])

    def run_branch(kts, qk_fn, ob, vslot):
        n = len(kts)
        info = {}

        def do_qk(idx):
            kt = kts[idx]
            i = it[0]; it[0] += 1
            Sd, stok, pt, ptok = sbuf_of(i)
            c0, w = qk_fn(kt, Sd, stok)
            Sv = Sd[:].rearrange("p (j q) -> p j q", j=2)[:, :, c0:c0 + w]
            pv_ = pt[:].rearrange("p (j q) -> p j q", j=2)[:, :, c0:c0 + w]
            P.act(lambda Sv=Sv, pv_=pv_: nc.scalar.activation(out=pv_, in_=Sv, func=AF.Exp), r=stok, w=[ptok])
            info[idx] = (i, c0, w)

        def do_pv(idx):
            kt = kts[idx]
            i, c0, w = info[idx]
            _, _, pt, ptok = sbuf_of(i)
            for j in range(2):
                P.pe(lambda j=j, kt=kt, pt=pt, c0=c0, w=w, idx=idx: nc.tensor.matmul(
                    ps[ob][:, j * 256 + c0:j * 256 + c0 + w], lhsT=vsw[:, kt, vslot:vslot + 2, :], rhs=pt[:, j * 256 + c0:j * 256 + c0 + w],
                    start=(idx == 0 and j == 0), stop=(idx == n - 1)), r=[ptok, f'vsw{kt//4}'], w=[f'ps{ob}'])

        do_qk(0)
        for idx in range(n):
            if idx + 1 < n:
                do_qk(idx + 1)
            do_pv(idx)

    def attend(qt2):
        cq = qt2 // 2
        qa_ = qsa[qt2 % 2]; qatok = f'qsa{qt2 % 2}'
        qs_ = qsw[qt2 % 2]; qstok = f'qsw{qt2 % 2}'

        def qk_sel(kt, Sd, stok):
            hh = kt // 32
            r_ = kt - 2 * qt2
            c0 = 128 if r_ == 1 else 0
            w = 256 - c0
            diag = r_ >= 0
            for j in range(2):
                P.pe(lambda j=j: nc.tensor.matmul(Sd[:, j * 512 + c0:j * 512 + 256], lhsT=ksT2[:, kt * 128:(kt + 1) * 128],
                                                  rhs=qa_[:, j, hh, c0:256], start=True, stop=(not diag)),
                     r=[f'ks{kt//4}', f'ksi{kt//4}', qatok], w=[stok[j]])
            if diag:
                for j in range(2):
                    P.pe(lambda j=j: nc.tensor.matmul(Sd[:, j * 512 + 128 * r_:j * 512 + 128 * r_ + 128], lhsT=ident[:], rhs=tri[:],
                                                      start=False, stop=True), r=['ident', 'tri'], w=[stok[j]])
            return c0, w

        def qk_win(kt, Sd, stok):
            rel = kt - 2 * qt2
            if rel == -4:
                c0, w, mk = 0, 128, (triU, 0)
            elif rel == -3:
                c0, w, mk = 0, 256, (triU, 128)
            elif rel in (-2, -1):
                c0, w, mk = 0, 256, None
            elif rel == 0:
                c0, w, mk = 0, 256, (tri, 0)
            else:
                c0, w, mk = 128, 128, (tri, 128)
            for j in range(2):
                P.pe(lambda j=j: nc.tensor.matmul(Sd[:, j * 512 + c0:j * 512 + c0 + w], lhsT=kwT2[:, kt * 128:(kt + 1) * 128],
                                                  rhs=qs_[:, j, c0:c0 + w], start=True, stop=(mk is None)),
                     r=[f'kw{kt//4}', qstok], w=[stok[j]])
            if mk is not None:
                mt_, mo = mk
                for j in range(2):
                    P.pe(lambda j=j: nc.tensor.matmul(Sd[:, j * 512 + mo:j * 512 + mo + 128], lhsT=ident[:], rhs=mt_[:], start=False, stop=True),
                         r=['ident', 'tri', 'triU'], w=[stok[j]])
            return c0, w

        run_branch(list(range(2 * qt2 + 2)), qk_sel, OS, 0)
        rels = [r for r in (-3, -4, -2, -1, 0, 1) if 2 * qt2 + r >= 0]
        run_branch([2 * qt2 + r for r in rels], qk_win, OW, 2)

    def finalize_tile(qt2):
        G = ps[UB]
        oc_ = occ[qt2 % 2]; octok = occtok[qt2 % 2]
        for p in range(2):
            P.dve(lambda p=p: nc.vector.tensor_copy(out=grep[:, p, :, :],
                                                    in_=gsig[:, 2 * qt2 + p, :].unsqueeze(2).to_broadcast([128, 6, 64])),
                  r=['gsig'], w=['grep'])
        for br, ob in ((1, OS), (2, OW), (0, OC)):
            for p in range(2):
                for j in range(2):
                    P.pe(lambda p=p, j=j, br=br: nc.tensor.matmul(G[0:64, j * 256 + p * 128:j * 256 + p * 128 + 128],
                                                                  lhsT=grep[:, p, 3 * j + br, :], rhs=ident[:], start=True, stop=True),
                         r=['grep', 'ident'], w=[f'ps{UB}'])
            if br == 0:
                P.dve(lambda: nc.vector.tensor_tensor(out=tmpc[:], in0=oc_, in1=G[0:64, :], op=ALU.mult), r=[octok, f'ps{UB}'], w=['rcp1'])
                P.pool(lambda: nc.gpsimd.tensor_tensor(out=obn[:], in0=acc[:], in1=tmpc[:], op=ALU.add), r=['acc', 'rcp1'], w=['osb0'])
                continue
            Zb = t1 if br == 1 else t2
            ztok = 't1' if br == 1 else 't2'
            P.act(lambda ob=ob, Zb=Zb: nc.scalar.activation(out=Zb[64:128, :], in_=ps[ob][64:128, :], func=AF.Ln), r=[f'ps{ob}'], w=[ztok])
            P.act(lambda Zb=Zb: nc.scalar.activation(out=Zb[64:128, :], in_=Zb[64:128, :], func=AF.Exp, scale=-1.0), r=[ztok], w=[ztok])
            P.dve(lambda Zb=Zb: nc.vector.tensor_copy(out=rcn[:], in_=Zb[64:128, :]), r=[ztok], w=['rcp0'])
            P.dve(lambda: nc.vector.tensor_tensor(out=rcn[:], in0=rcn[:], in1=G[0:64, :], op=ALU.mult), r=['rcp0', f'ps{UB}'], w=['rcp0'])
            if br == 1:
                P.dve(lambda ob=ob: nc.vector.tensor_tensor(out=acc[:], in0=ps[ob][0:64, :], in1=rcn[:], op=ALU.mult), r=[f'ps{ob}', 'rcp0'], w=['acc'])
            else:
                P.dve(lambda ob=ob: nc.vector.tensor_tensor(out=tmpc[:], in0=ps[ob][0:64, :], in1=rcn[:], op=ALU.mult), r=[f'ps{ob}', 'rcp0'], w=['rcp1'])
                P.pool(lambda: nc.gpsimd.tensor_tensor(out=acc[:], in0=acc[:], in1=tmpc[:], op=ALU.add), r=['acc', 'rcp1'], w=['acc'])
        for j in range(2):
            P.dma('pool', lambda j=j: g.dma_start(out=onT[64 * j:64 * j + 64, qt2 * 256:(qt2 + 1) * 256], in_=obn[:, j * 256:(j + 1) * 256]),
                  r=['osb0'])

    part1(0)
    part2(0)
    for qt2 in range(NQ):
        if qt2 + 1 < NQ:
            part1(qt2 + 1)
        attend(qt2)
        if qt2 + 1 < NQ:
            part2(qt2 + 1)
        finalize_tile(qt2)


def rope_tables(pos):
    half = 8
    inv = (np.float32(500000.0) ** (-np.arange(half, dtype=np.float32) / np.float32(half))).astype(np.float32)
    ang = pos.astype(np.float32)[:, None] * inv[None, :]
    cos = np.cos(ang).astype(np.float32); sin = np.sin(ang).astype(np.float32)
    L = pos.shape[0]
    c64 = np.ones((64, L), np.float32); s64 = np.zeros((64, L), np.float32)
    c64[0:8] = cos.T; c64[8:16] = cos.T
    s64[0:8] = sin.T; s64[8:16] = sin.T
    return np.concatenate([c64, c64], 0), np.concatenate([s64, s64], 0)


_CONSTS = None


def nsa_consts():
    global _CONSTS
    if _CONSTS is not None:
        return _CONSTS
    d = {}
    cT, sT = rope_tables(np.arange(T))
    d['cosT'] = cT; d['sinT'] = sT
    cc, sc = rope_tables(np.arange(512) * 16 + 31)
    d['cosC'] = cc; d['sinC'] = sc
    pm = np.zeros((128, 128), np.float32)
    for m in range(128):
        mm = m % 64
        if mm < 8:
            pm[m + 8, m] = -1.0
        elif mm < 16:
            pm[m - 8, m] = 1.0
    d['c_perm'] = pm
    k = np.arange(128)[:, None]; q = np.arange(128)[None, :]
    d['c_triU'] = np.where(k <= q, NEG, 0.0).astype(np.float32)
    key = np.arange(T)[None, :]; jj = np.arange(64)[:, None]
    d['c_ind'] = (((key // 64) % 64) == jj).astype(np.float32)
    ncb, nsel = 511, 128
    cs_ = np.arange(ncb) * 16; ce = cs_ + 32
    ss = np.arange(nsel) * 64; se = ss + 64
    ov = np.clip(np.minimum(ce[:, None], se[None, :]) - np.maximum(cs_[:, None], ss[None, :]), 0, None) / 32.0
    sm = np.zeros((512, 128), np.float32); sm[:511] = ov
    d['c_selmap'] = np.ascontiguousarray(sm.reshape(4, 128, 128).transpose(1, 0, 2).reshape(128, 512))
    cm = np.zeros((128, 17, 128), np.float32)
    nn = np.arange(128)[:, None]; qq = np.arange(128)[None, :]
    for Dd in range(17):
        cm[:, Dd, :] = np.where(16 * nn - qq <= 128 * Dd - 31, 0.0, NEG)
    d['c_cmask'] = cm.reshape(128, 17 * 128)
    qq = np.arange(128)[:, None]; u = np.arange(256)[None, :]
    dlt = u - 128; own = qq // 64
    d['c_wk'] = (dlt < own).astype(np.float32)
    d['c_wa'] = np.where(dlt == own, 1e6, np.where(dlt > own, -1.0, 0.0)).astype(np.float32)
    _CONSTS = d
    return d


def inputs_nsa(inp, l, b, hp):
    w_in = inp['w_in'][l]
    offs = np.cumsum([0, 512, 512, 512, 8, 512, 768, 24, 1024, 1024])
    gq = hp // 2
    own = w_in[:, offs[4] + 128 * hp: offs[4] + 128 * hp + 128]
    oth_hp = 2 * gq + (1 - hp % 2)
    oth = w_in[:, offs[4] + 128 * oth_hp: offs[4] + 128 * oth_hp + 128]
    kv = [w_in[:, offs[5] + 128 * s + 64 * gq: offs[5] + 128 * s + 64 * gq + 64] for s in range(6)]
    kc, vc, ks, vs, kw, vw = kv
    gcols = w_in[:, offs[6] + 6 * hp: offs[6] + 6 * hp + 6]
    w_nsa = np.concatenate([own, oth, ks, ks, kw, kw, kc, vc, vs, vw, gcols], axis=1)
    assert w_nsa.shape[1] == NW
    pos = inp['cmp_pos'][l]
    posT = np.zeros((128, 32), np.float32)
    for m in range(2):
        pp = pos[m].reshape(16, 2, 64).reshape(16, 128)
        posT[:, 16 * m:16 * m + 16] = pp.T
    b1 = inp['cmp_b1'][l]
    b1T = np.stack([b1[0, 0:128], b1[0, 128:256], b1[1, 0:128], b1[1, 128:256]], axis=1)
    w2 = inp['cmp_w2'][l]
    b2 = inp['cmp_b2'][l]
    m = dict(
        w_nsa=np.ascontiguousarray(w_nsa),
        posT=posT, w1=np.ascontiguousarray(inp['cmp_w1'][l]), b1T=np.ascontiguousarray(b1T),
        w2kd=np.ascontiguousarray(np.concatenate([w2[0], w2[0]], axis=1)), w2v=np.ascontiguousarray(w2[1]),
        b2k=np.ascontiguousarray(np.concatenate([b2[0], b2[0]])[:, None]),
        b2v=np.ascontiguousarray(np.broadcast_to(b2[1][None, :], (128, 64))),
    )
    m.update(nsa_consts())
    return m


T = 8192
D = 1024
NCH = 16
NKT = 64
EPS = 1e-6
NEG = -30000.0


def build_A(do_fox=True, do_nsa=True, nqt_limit=None, stage=9):
    nc = bass.Bass("TRN2", target_bir_lowering=False)
    P = Prog(nc)
    dram = lambda name, shape, dt=F32, kind="ExternalInput": nc.dram_tensor(name, shape, dt, kind=kind).ap()
    xT = dram("xT", [D, T])
    gmix = dram("gmix", [128, 8])
    w_fm_fox = dram("w_fm_fox", [D, 256])
    w_tm_fox = dram("w_tm_fox", [D, 130])
    bfg = dram("bfg", [128, 2])
    c_tri = dram("c_tri", [128, 128])
    c_ident = dram("c_ident", [128, 128])
    c_ut = dram("c_ut", [128, 128])
    oaT = dram("oaT", [128, T], BF16, kind="ExternalOutput")
    if do_nsa:
        nsa_dram = nsa_declare(dram, nc)

    sb = lambda name, shape, dt=BF16: nc.alloc_sbuf_tensor(name, shape, dt)
    psS = [nc.alloc_psum_tensor(f"psS{i}", [128, 1024], F32) for i in range(2)]
    ps = [psS[0][:, 0:512], psS[0][:, 512:1024], psS[1][:, 0:512], psS[1][:, 512:1024]] + \
         [nc.alloc_psum_tensor(f"ps{i}", [128, 512], F32)[:] for i in range(4, 8)]

    g_sb = sb("g_sb", [128, 8], F32)
    tri = sb("tri", [128, 128]); ident = sb("ident", [128, 128]); ut = sb("ut", [128, 128])
    ones_bf = sb("ones_bf", [128, 128])
    eps_c = sb("eps_c", [128, 1], F32); one_c = sb("one_c", [128, 1], F32); zero_c = sb("zero_c", [128, 64], F32)
    w1b = sb("w1b", [128, 16, 256])
    wfm = w1b[:, 0:8, :]; wtm = w1b[:, 8:16, 0:130]
    bneg = sb("bneg", [128, 2], F32)
    P.dma('sp', lambda: nc.sync.dma_start(out=g_sb[:], in_=gmix[:, :]), w=['g'])
    P.dma('sp', lambda: nc.sync.dma_start(out=bneg[:], in_=bfg[:, :]), w=['bneg'])
    P.dma('pool', lambda: nc.gpsimd.dma_start(out=tri[:], in_=c_tri[:, :]), w=['tri'])
    P.dma('pool', lambda: nc.gpsimd.dma_start(out=ident[:], in_=c_ident[:, :]), w=['ident'])
    P.dma('pool', lambda: nc.gpsimd.dma_start(out=ut[:], in_=c_ut[:, :]), w=['ut'])
    P.dma('pool', lambda: nc.gpsimd.dma_start(out=wfm, in_=w_fm_fox.rearrange("(c p) n -> p c n", p=128)), w=['wfm'])
    P.dma('pool', lambda: nc.gpsimd.dma_start(out=wtm, in_=w_tm_fox.rearrange("(c p) n -> p c n", p=128)), w=['wtm'])
    P.dve(lambda: nc.vector.memset(ones_bf[:], 1.0), w=['ones'])
    P.dve(lambda: nc.vector.memset(eps_c[:], EPS), w=['eps'])
    P.dve(lambda: nc.vector.memset(one_c[:], 1.0), w=['one'])
    P.dve(lambda: nc.vector.memset(zero_c[:], 0.0), w=['zero'])
    P.dve(lambda: nc.vector.tensor_scalar(out=bneg[:], in0=bneg[:], scalar1=-1.0, scalar2=None, op0=ALU.mult), r=['bneg'], w=['bneg'])

    R0 = sb("R0", [128, T]); R1 = sb("R1", [128, T]); R2 = sb("R2", [128, 2 * T])
    qaT = R0; kaT = R1
    va = R2[:].rearrange("p (k j d) -> p k j d", k=NKT, j=2)
    ftm = sb("ftm", [128, NKT, 2], F32)
    P.pool(lambda: nc.gpsimd.memset(va[:, :, :, 64:128], 1.0), w=['va_ones'])

    xin0 = sb("xin0", [128, 8, 512], F32)
    xin = [xin0, xin0]
    sq = [sb(f"sq{i}", [128, 512]) for i in range(2)]
    rstd = sb("rstd", [128, 512], F32)
    hT = sb("hT", [128, 8, 512])

    def norm_chunk(c):
        xb = xin[c % 2]
        P.dma('sp', lambda: nc.sync.dma_start(out=xb[:], in_=xT[:, c * 512:(c + 1) * 512].rearrange("(k p) n -> p k n", p=128)),
              w=['xin'])
        for k in range(8):
            s = sq[k % 2]
            P.act(lambda k=k, s=s: nc.scalar.activation(out=s[:], in_=xb[:, k, :], func=AF.Square), r=['xin'], w=[f'sq{k%2}'])
            P.pe(lambda k=k, s=s: nc.tensor.matmul(ps[5][:], lhsT=ones_bf[:], rhs=s[:], start=(k == 0), stop=(k == 7)),
                 r=[f'sq{k%2}', 'ones'], w=['ps5'])
        P.act(lambda: nc.scalar.activation(out=rstd[:], in_=ps[5][:], func=AF.Ln, bias=eps_c[:], scale=1.0 / D),
              r=['ps5', 'eps'], w=['rstd'])
        P.act(lambda: nc.scalar.activation(out=rstd[:], in_=rstd[:], func=AF.Exp, scale=-0.5), r=['rstd'], w=['rstd'])
        for k in range(8):
            P.dve(lambda k=k: nc.vector.scalar_tensor_tensor(out=hT[:, k, :], in0=xb[:, k, :], scalar=g_sb[:, k:k + 1], in1=rstd[:],
                                                            op0=ALU.mult, op1=ALU.mult),
                  r=['xin', 'rstd', 'g'], w=[f'hT{k}'])

    def fm_proj(wt, col0, bank, extra_r=()):
        for k in range(8):
            P.pe(lambda k=k: nc.tensor.matmul(ps[bank][:], lhsT=wt[:, k, col0:col0 + 128], rhs=hT[:, k, :], start=(k == 0), stop=(k == 7)),
                 r=[f'hT{k}', *extra_r], w=[f'ps{bank}'])

    pt = [sb(f"pt{i}", [128, 512]) for i in range(3)]
    rcp = [sb(f"rcp{i}", [64, 512], F32) for i in range(2)]
    osb = [sb(f"osb{i}", [64, 512]) for i in range(2)]
    if do_fox:
        for c in range(NCH):
            norm_chunk(c)
            cs = slice(c * 512, (c + 1) * 512)
            fm_proj(wfm, 0, 6, ['wfm'])
            P.act(lambda cs=cs: nc.scalar.activation(out=qaT[:, cs], in_=ps[6][:], func=AF.Copy, scale=0.125), r=['ps6'], w=[f'qa{c}'])
            fm_proj(wfm, 128, 7, ['wfm'])
            P.dve(lambda cs=cs: nc.vector.tensor_copy(out=kaT[:, cs], in_=ps[7][:]), r=['ps7'], w=[f'ka{c}'])
            for t in range(4):
                for k in range(8):
                    P.pe(lambda k=k, t=t: nc.tensor.matmul(ps[6][:, t * 128:(t + 1) * 128], lhsT=hT[:, k, t * 128:(t + 1) * 128],
                                                           rhs=wtm[:, k, 0:128], start=(k == 0), stop=(k == 7)),
                         r=[f'hT{k}', 'wtm'], w=['ps6'])
            for t in range(4):
                for k in range(8):
                    P.pe(lambda k=k, t=t: nc.tensor.matmul(ps[7][:, t * 2:(t + 1) * 2], lhsT=hT[:, k, t * 128:(t + 1) * 128],
                                                           rhs=wtm[:, k, 128:130], start=(k == 0), stop=(k == 7)),
                         r=[f'hT{k}', 'wtm'], w=['ps7'])
            P.act(lambda c=c: nc.scalar.copy(out=va[:, 4 * c:4 * c + 4, :, 0:64],
                                             in_=ps[6][:].rearrange("p (t j d) -> p t j d", t=4, j=2)),
                  r=['ps6', 'va_ones'], w=[f'va{c}'])
            P.dve(lambda c=c: nc.vector.tensor_copy(out=ftm[:, 4 * c:4 * c + 4, :], in_=ps[7][:, 0:8].rearrange("p (t j) -> p t j", j=2)),
                  r=['ps7'], w=['ftm'])

        lf = sb("lf", [128, NKT, 2], F32)
        lhi = sb("lhi", [128, 128]); llo = sb("llo", [128, 128])
        ccol = sb("ccol", [128, NKT, 2], F32)
        incl = sb("incl", [128, NKT, 2], F32)
        tot = sb("tot", [128, NKT, 2], F32)
        for j in range(2):
            P.act(lambda j=j: nc.scalar.activation(out=lf[:, :, j], in_=ftm[:, :, j], func=AF.Exp, bias=bneg[:, j:j + 1], scale=-1.0),
                  r=['ftm', 'bneg'], w=['lf'])
        P.act(lambda: nc.scalar.activation(out=lf[:], in_=lf[:], func=AF.Ln, bias=one_c[:], scale=1.0), r=['lf', 'one'], w=['lf'])
        P.dve(lambda: nc.vector.tensor_scalar(out=lf[:], in0=lf[:], scalar1=-1.0, scalar2=None, op0=ALU.mult), r=['lf'], w=['lf'])
        lf2 = lf[:].rearrange("p a b -> p (a b)")
        P.dve(lambda: nc.vector.tensor_copy(out=lhi[:], in_=lf2), r=['lf'], w=['lhi'])
        P.dve(lambda: nc.vector.tensor_tensor(out=llo[:], in0=lf2, in1=lhi[:], op=ALU.subtract), r=['lf', 'lhi'], w=['llo'])
        P.pe(lambda: nc.tensor.matmul(ps[5][:, 0:128], lhsT=ut[:], rhs=lhi[:], start=True, stop=False), r=['ut', 'lhi'], w=['ps5'])
        P.pe(lambda: nc.tensor.matmul(ps[5][:, 0:128], lhsT=ut[:], rhs=llo[:], start=False, stop=True), r=['ut', 'llo'], w=['ps5'])
        P.pe(lambda: nc.tensor.matmul(ps[5][:, 128:256], lhsT=ones_bf[:], rhs=lhi[:], start=True, stop=False), r=['ones', 'lhi'], w=['ps5'])
        P.pe(lambda: nc.tensor.matmul(ps[5][:, 128:256], lhsT=ones_bf[:], rhs=llo[:], start=False, stop=True), r=['ones', 'llo'], w=['ps5'])
        P.dve(lambda: nc.vector.tensor_copy(out=tot[:].rearrange("p a b -> p (a b)"), in_=ps[5][:, 128:256]), r=['ps5'], w=['tot'])
        for j in range(2):
            P.dve(lambda j=j: nc.vector.tensor_tensor_scan(out=incl[:, :, j], data0=tot[:, :, j], data1=zero_c[:, 0:NKT], initial=0.0,
                                                           op0=ALU.add, op1=ALU.add),
                  r=['tot', 'zero'], w=['incl'])
        P.dve(lambda: nc.vector.tensor_tensor(out=ccol[:].rearrange("p a b -> p (a b)"), in0=ps[5][:, 0:128],
                                              in1=incl[:].rearrange("p a b -> p (a b)"), op=ALU.add), r=['ps5', 'incl'], w=['ccol'])
        P.dve(lambda: nc.vector.tensor_tensor(out=ccol[:], in0=ccol[:], in1=tot[:], op=ALU.subtract), r=['ccol', 'tot'], w=['ccol'])

        biasT = [sb(f"biasT{i}", [128, NKT], F32) for i in range(2)]
        qpz = [[sb(f"qpz{j}{b}", [128, 512]) for b in range(2)] for j in range(2)]
        for j in range(2):
            for b in range(2):
                P.pool(lambda j=j, b=b: nc.gpsimd.memset(qpz[j][b][:], 0.0), w=[f'qpz{j}{b}'])
        it = 0
        fin = 0
        nqt = NCH if nqt_limit is None else nqt_limit
        for j in range(2):
            hs = slice(64 * j, 64 * j + 64)
            for qt in range(nqt):
                nk = 4 * qt + 4
                bt = biasT[fin % 2]
                ob = 3 + fin % 2
                P.dve(lambda j=j, qt=qt, nk=nk, bt=bt: nc.vector.tensor_scalar(
                    out=bt[:, 0:nk], in0=ccol[:, 0:nk, j], scalar1=-1.0, scalar2=incl[:, 4 * qt + 3, j:j + 1], op0=ALU.mult, op1=ALU.add),
                    r=['ccol', 'incl'], w=[f'biasT{fin%2}'])
                qs = slice(qt * 512, (qt + 1) * 512)
                qp = qpz[j][qt % 2]; qptok = f'qpz{j}{qt%2}'
                P.pool(lambda qp=qp, hs=hs, qs=qs: nc.gpsimd.tensor_copy(out=qp[hs, :], in_=qaT[hs, qs]), r=[f'qa{qt}'], w=[qptok])

                def qk(kt, i, qt=qt, j=j, bt=bt, hs=hs, ob=ob, nk=nk, fin=fin, qp=qp, qptok=qptok):
                    r_ = kt - 4 * qt
                    c0 = 128 * r_ if r_ > 0 else 0
                    diag = r_ >= 0
                    S = ps[i % 3]
                    P.pe(lambda: nc.tensor.matmul(S[:, c0:512], lhsT=kaT[:, kt * 128:(kt + 1) * 128], rhs=qp[:, c0:512],
                                                  start=True, stop=not diag),
                         r=[f'ka{kt//4}', qptok], w=[f'ps{i%3}'])
                    if diag:
                        P.pe(lambda: nc.tensor.matmul(S[:, c0:c0 + 128], lhsT=ident[:], rhs=tri[:], start=False, stop=True),
                             r=['ident', 'tri'], w=[f'ps{i%3}'])
                    P.act(lambda: nc.scalar.activation(out=pt[i % 3][:, c0:512], in_=S[:, c0:512], func=AF.Exp, bias=bt[:, kt:kt + 1], scale=1.0),
                          r=[f'ps{i%3}', f'biasT{fin%2}'], w=[f'pt{i%3}'])

                def pv(kt, i, qt=qt, j=j, bt=bt, hs=hs, ob=ob, nk=nk, fin=fin):
                    r_ = kt - 4 * qt
                    c0 = 128 * r_ if r_ > 0 else 0
                    P.pe(lambda: nc.tensor.matmul(ps[ob][:, c0:512], lhsT=va[:, kt, j, :], rhs=pt[i % 3][:, c0:512],
                                                  start=(kt == 0), stop=(kt == nk - 1)),
                         r=[f'pt{i%3}', f'va{kt//4}'], w=[f'ps{ob}'])

                base = it
                for kt in range(min(2, nk)):
                    qk(kt, base + kt)
                for kt in range(nk):
                    if kt + 2 < nk:
                        qk(kt + 2, base + kt + 2)
                    pv(kt, base + kt)
                it += nk
                rc = rcp[fin % 2]; o_ = osb[fin % 2]
                P.dve(lambda rc=rc, ob=ob: nc.vector.reciprocal(out=rc[:], in_=ps[ob][64:128, :]), r=[f'ps{ob}'], w=[f'rcp{fin%2}'])
                P.dve(lambda rc=rc, ob=ob, o_=o_: nc.vector.tensor_tensor(out=o_[:], in0=ps[ob][0:64, :], in1=rc[:], op=ALU.mult),
                      r=[f'ps{ob}', f'rcp{fin%2}'], w=[f'osb{fin%2}'])
                P.dma('pool', lambda o_=o_, qs=qs, hs=hs: nc.gpsimd.dma_start(out=oaT[hs, qs], in_=o_[:]), r=[f'osb{fin%2}'])
                fin += 1
    if do_nsa:
        nsa_emit(locals())
    stats = P.finalize()
    return nc, stats


def host_consts():
    k = np.arange(128)[:, None]; q = np.arange(128)[None, :]
    return dict(
        c_tri=np.where(k > q, NEG, 0.0).astype(np.float32),
        c_ident=np.eye(128, dtype=np.float32),
        c_ut=(k <= q).astype(np.float32),
    )


def inputs_A(inp, l, b, hp, x_b=None, nsa=True):
    x = inp['x'][b] if x_b is None else x_b
    w_in = inp['w_in'][l]
    offs = np.cumsum([0, 512, 512, 512, 8, 512, 768, 24, 1024, 1024])
    qa = w_in[:, offs[0] + 128 * hp: offs[0] + 128 * hp + 128]
    ka = w_in[:, offs[1] + 128 * hp: offs[1] + 128 * hp + 128]
    va = w_in[:, offs[2] + 128 * hp: offs[2] + 128 * hp + 128]
    f = w_in[:, offs[3] + 2 * hp: offs[3] + 2 * hp + 2]
    m = dict(
        xT=np.ascontiguousarray(x.T),
        gmix=np.ascontiguousarray(inp['norm_mix'][l].reshape(8, 128).T),
        w_fm_fox=np.ascontiguousarray(np.concatenate([qa, ka], axis=1)),
        w_tm_fox=np.ascontiguousarray(np.concatenate([va, f], axis=1)),
        bfg=np.ascontiguousarray(np.broadcast_to(inp['b_forget'][l][2 * hp:2 * hp + 2][None, :], (128, 2))),
    )
    m.update(host_consts())
    if nsa:
        m.update(inputs_nsa(inp, l, b, hp))
    return m


D = 1024
NTB = 2048
EPS = 1e-6


def _norm(P, nc, ps, bank, xt, xtok, g_sb, gtok, out_bf, otok, ones_bf, eps_c, sq, rstd, n):
    for k in range(8):
        s = sq[k % 2]
        P.act(lambda k=k, s=s: nc.scalar.activation(out=s[:, 0:n], in_=xt[:, k, 0:n], func=AF.Square), r=[xtok], w=[f'sq{k%2}'])
        P.pe(lambda k=k, s=s: nc.tensor.matmul(ps[bank][:, 0:n], lhsT=ones_bf[:], rhs=s[:, 0:n], start=(k == 0), stop=(k == 7)),
             r=[f'sq{k%2}', 'ones'], w=[f'ps{bank}'])
    P.act(lambda: nc.scalar.activation(out=rstd[:, 0:n], in_=ps[bank][:, 0:n], func=AF.Ln, bias=eps_c[:], scale=1.0 / D),
          r=[f'ps{bank}', 'eps'], w=['rstd'])
    P.act(lambda: nc.scalar.activation(out=rstd[:, 0:n], in_=rstd[:, 0:n], func=AF.Exp, scale=-0.5), r=['rstd'], w=['rstd'])
    for k in range(8):
        P.dve(lambda k=k: nc.vector.scalar_tensor_tensor(out=out_bf[:, k, 0:n], in0=xt[:, k, 0:n], scalar=g_sb[:, k:k + 1], in1=rstd[:, 0:n],
                                                        op0=ALU.mult, op1=ALU.mult), r=[xtok, 'rstd', gtok], w=[otok])


def build_B1():
    nc = bass.Bass("TRN2", target_bir_lowering=False)
    P = Prog(nc)
    dram = lambda name, shape, dt=F32, kind="ExternalInput": nc.dram_tensor(name, shape, dt, kind=kind).ap()
    xT = dram("xT", [D, NTB]); oaT = dram("oaT", [512, NTB], BF16); onT = dram("onT", [512, NTB], BF16)
    gmix = dram("gmix", [128, 8]); gmlp = dram("gmlp", [128, 8])
    w_of = dram("w_of", [512, D]); w_on = dram("w_on", [512, D])
    w_ga = dram("w_ga", [D, D]); w_gb = dram("w_gb", [D, D]); w_out = dram("w_out", [D, D])
    x1T = dram("x1T", [D, NTB], F32, kind="ExternalOutput")
    h2T = dram("h2T", [D, NTB], BF16, kind="ExternalOutput")
    sb = lambda name, shape, dt=BF16: nc.alloc_sbuf_tensor(name, shape, dt)
    ps = [nc.alloc_psum_tensor(f"ps{i}", [128, 512], F32) for i in range(8)]
    g = nc.gpsimd; sp = nc.sync
    gm = sb("gm", [128, 8], F32); gl = sb("gl", [128, 8], F32)
    ones_bf = sb("ones_bf", [128, 128]); eps_c = sb("eps_c", [128, 1], F32)
    wof = sb("wof", [128, 4, D]); won = sb("won", [128, 4, D])
    wga = sb("wga", [128, 8, D]); wgb = sb("wgb", [128, 8, D]); wout = sb("wout", [128, 8, D])
    P.dma('sp', lambda: sp.dma_start(out=gm[:], in_=gmix[:, :]), w=['gm'])
    P.dma('sp', lambda: sp.dma_start(out=gl[:], in_=gmlp[:, :]), w=['gl'])
    P.dve(lambda: nc.vector.memset(ones_bf[:], 1.0), w=['ones'])
    P.dve(lambda: nc.vector.memset(eps_c[:], EPS), w=['eps'])
    for nm, wt, src, kc in (('wof', wof, w_of, 4), ('won', won, w_on, 4), ('wga', wga, w_ga, 8), ('wgb', wgb, w_gb, 8), ('wout', wout, w_out, 8)):
        for k in range(kc):
            P.dma('pool', lambda wt=wt, src=src, k=k: g.dma_start(out=wt[:, k, :], in_=src[k * 128:(k + 1) * 128, :]), w=[nm])
    xt = sb("xt", [128, 8, 512], F32); oat = sb("oat", [128, 4, 512]); ont = sb("ont", [128, 4, 512])
    sq = [sb(f"sq{i}", [128, 512]) for i in range(2)]
    rstd = sb("rstd", [128, 512], F32)
    hT = sb("hT", [128, 8, 512]); mixT = sb("mixT", [128, 8, 512])
    sa = sb("sa", [128, 512], F32); sb_ = sb("sb_", [128, 512], F32); ma = sb("ma", [128, 512], F32); mb = sb("mb", [128, 512], F32)
    x1 = sb("x1", [128, 8, 512], F32); h2 = sb("h2", [128, 8, 512])
    for t in range(NTB // 512):
        ts_ = slice(t * 512, (t + 1) * 512)
        P.dma('sp', lambda ts_=ts_: sp.dma_start(out=xt[:], in_=xT[:, ts_].rearrange("(k p) n -> p k n", p=128)), w=['xt'])
        P.dma('sp', lambda ts_=ts_: sp.dma_start(out=oat[:], in_=oaT[:, ts_].rearrange("(k p) n -> p k n", p=128)), w=['oat'])
        P.dma('sp', lambda ts_=ts_: sp.dma_start(out=ont[:], in_=onT[:, ts_].rearrange("(k p) n -> p k n", p=128)), w=['ont'])
        _norm(P, nc, ps, 7, xt, 'xt', gm, 'gm', hT, 'hT', ones_bf, eps_c, sq, rstd, 512)
        for dc in range(8):
            ds_ = slice(dc * 128, (dc + 1) * 128)
            b0 = 4 * (dc % 2)
            for k in range(4):
                P.pe(lambda k=k, ds_=ds_, b0=b0: nc.tensor.matmul(ps[b0][:], lhsT=wof[:, k, ds_], rhs=oat[:, k, :], start=(k == 0), stop=(k == 3)),
                     r=['wof', 'oat'], w=[f'ps{b0}'])
            for k in range(8):
                P.pe(lambda k=k, ds_=ds_, b0=b0: nc.tensor.matmul(ps[b0 + 1][:], lhsT=wga[:, k, ds_], rhs=hT[:, k, :], start=(k == 0), stop=(k == 7)),
                     r=['wga', 'hT'], w=[f'ps{b0+1}'])
            for k in range(4):
                P.pe(lambda k=k, ds_=ds_, b0=b0: nc.tensor.matmul(ps[b0 + 2][:], lhsT=won[:, k, ds_], rhs=ont[:, k, :], start=(k == 0), stop=(k == 3)),
                     r=['won', 'ont'], w=[f'ps{b0+2}'])
            for k in range(8):
                P.pe(lambda k=k, ds_=ds_, b0=b0: nc.tensor.matmul(ps[b0 + 3][:], lhsT=wgb[:, k, ds_], rhs=hT[:, k, :], start=(k == 0), stop=(k == 7)),
                     r=['wgb', 'hT'], w=[f'ps{b0+3}'])
            P.act(lambda b0=b0: nc.scalar.activation(out=sa[:], in_=ps[b0 + 1][:], func=AF.Sigmoid), r=[f'ps{b0+1}'], w=['sa'])
            P.act(lambda b0=b0: nc.scalar.activation(out=sb_[:], in_=ps[b0 + 3][:], func=AF.Sigmoid), r=[f'ps{b0+3}'], w=['sb_'])
            P.dve(lambda b0=b0: nc.vector.tensor_tensor(out=ma[:], in0=ps[b0][:], in1=sa[:], op=ALU.mult), r=[f'ps{b0}', 'sa'], w=['ma'])
            P.dve(lambda b0=b0: nc.vector.tensor_tensor(out=mb[:], in0=ps[b0 + 2][:], in1=sb_[:], op=ALU.mult), r=[f'ps{b0+2}', 'sb_'], w=['mb'])
            P.pool(lambda dc=dc: nc.gpsimd.tensor_tensor(out=mixT[:, dc, :], in0=ma[:], in1=mb[:], op=ALU.add), r=['ma', 'mb'], w=['mixT'])
        for dc in range(8):
            ds_ = slice(dc * 128, (dc + 1) * 128)
            b = dc % 2
            for k in range(8):
                P.pe(lambda k=k, ds_=ds_, b=b: nc.tensor.matmul(ps[b][:], lhsT=wout[:, k, ds_], rhs=mixT[:, k, :], start=(k == 0), stop=(k == 7)),
                     r=['wout', 'mixT'], w=[f'ps{b}'])
            P.dve(lambda dc=dc, b=b: nc.vector.tensor_tensor(out=x1[:, dc, :], in0=ps[b][:], in1=xt[:, dc, :], op=ALU.add), r=[f'ps{b}', 'xt'], w=['x1'])
        P.dma('pool', lambda ts_=ts_: g.dma_start(out=x1T[:, ts_].rearrange("(k p) n -> p k n", p=128), in_=x1[:]), r=['x1'])
        _norm(P, nc, ps, 7, x1, 'x1', gl, 'gl', h2, 'h2', ones_bf, eps_c, sq, rstd, 512)
        P.dma('pool', lambda ts_=ts_: g.dma_start(out=h2T[:, ts_].rearrange("(k p) n -> p k n", p=128), in_=h2[:]), r=['h2'])
    return nc, P.finalize()


def build_B2():
    nc = bass.Bass("TRN2", target_bir_lowering=False)
    P = Prog(nc)
    dram = lambda name, shape, dt=F32, kind="ExternalInput": nc.dram_tensor(name, shape, dt, kind=kind).ap()
    x1T = dram("x1T", [D, NTB]); h2T = dram("h2T", [D, NTB], BF16)
    gfin = dram("gfin", [128, 8])
    w_up = dram("w_up", [D, 4096]); w_down = dram("w_down", [4096, D])
    x2T = dram("x2T", [D, NTB], F32, kind="ExternalOutput")
    yT = dram("yT", [D, NTB], F32, kind="ExternalOutput")
    sb = lambda name, shape, dt=BF16: nc.alloc_sbuf_tensor(name, shape, dt)
    ps = [nc.alloc_psum_tensor(f"ps{i}", [128, 512], F32) for i in range(8)]
    g = nc.gpsimd; sp = nc.sync
    gf = sb("gf", [128, 8], F32)
    ones_bf = sb("ones_bf", [128, 128]); eps_c = sb("eps_c", [128, 1], F32)
    wup = sb("wup", [128, 8, 4096]); wdn = sb("wdn", [128, 32, D])
    P.dma('sp', lambda: sp.dma_start(out=gf[:], in_=gfin[:, :]), w=['gf'])
    P.dve(lambda: nc.vector.memset(ones_bf[:], 1.0), w=['ones'])
    P.dve(lambda: nc.vector.memset(eps_c[:], EPS), w=['eps'])
    for cb in range(8):
        P.dma('pool', lambda cb=cb: g.dma_start(out=wup[:, :, cb * 512:(cb + 1) * 512],
                                               in_=w_up[:, cb * 512:(cb + 1) * 512].rearrange("(k p) n -> p k n", p=128)), w=[f'wup{cb}'])
    for k in range(32):
        P.dma('pool', lambda k=k: g.dma_start(out=wdn[:, k, :], in_=w_down[k * 128:(k + 1) * 128, :]), w=[f'wdn{k}'])
    N = 256
    h2 = sb("h2", [128, 8, N]); x1 = sb("x1", [128, 8, N], F32)
    uT = sb("uT", [128, 32, N]); rl = [sb(f"rl{i}", [128, N], F32) for i in range(2)]
    x2 = sb("x2", [128, 8, N], F32); yo = sb("yo", [128, 8, N], F32)
    sq = [sb(f"sq{i}", [128, 512]) for i in range(2)]
    rstd = sb("rstd", [128, 512], F32)
    for t in range(NTB // N):
        ts_ = slice(t * N, (t + 1) * N)
        P.dma('sp', lambda ts_=ts_: sp.dma_start(out=h2[:], in_=h2T[:, ts_].rearrange("(k p) n -> p k n", p=128)), w=['h2'])
        P.dma('sp', lambda ts_=ts_: sp.dma_start(out=x1[:], in_=x1T[:, ts_].rearrange("(k p) n -> p k n", p=128)), w=['x1'])
        for fc in range(32):
            b = fc % 4
            for k in range(8):
                P.pe(lambda k=k, fc=fc, b=b: nc.tensor.matmul(ps[b][:, 0:N], lhsT=wup[:, k, fc * 128:(fc + 1) * 128], rhs=h2[:, k, :],
                                                              start=(k == 0), stop=(k == 7)), r=[f'wup{fc // 4}', 'h2'], w=[f'ps{b}'])
            r_ = rl[fc % 2]
            P.act(lambda b=b, r_=r_: nc.scalar.activation(out=r_[:], in_=ps[b][:, 0:N], func=AF.Relu), r=[f'ps{b}'], w=[f'rl{fc%2}'])
            if fc % 2 == 0:
                P.dve(lambda fc=fc, r_=r_: nc.vector.tensor_tensor(out=uT[:, fc, :], in0=r_[:], in1=r_[:], op=ALU.mult), r=[f'rl{fc%2}'], w=[f'uT{fc}'])
            else:
                P.pool(lambda fc=fc, r_=r_: nc.gpsimd.tensor_tensor(out=uT[:, fc, :], in0=r_[:], in1=r_[:], op=ALU.mult), r=[f'rl{fc%2}'], w=[f'uT{fc}'])
        for dc in range(8):
            b = 4 + dc % 2
            for fc in range(32):
                P.pe(lambda fc=fc, dc=dc, b=b: nc.tensor.matmul(ps[b][:, 0:N], lhsT=wdn[:, fc, dc * 128:(dc + 1) * 128], rhs=uT[:, fc, :],
                                                                start=(fc == 0), stop=(fc == 31)), r=[f'wdn{fc}', f'uT{fc}'], w=[f'ps{b}'])
            P.dve(lambda dc=dc, b=b: nc.vector.tensor_tensor(out=x2[:, dc, :], in0=ps[b][:, 0:N], in1=x1[:, dc, :], op=ALU.add), r=[f'ps{b}', 'x1'], w=['x2'])
        P.dma('pool', lambda ts_=ts_: g.dma_start(out=x2T[:, ts_].rearrange("(k p) n -> p k n", p=128), in_=x2[:]), r=['x2'])
        for k in range(8):
            s = sq[k % 2]
            P.act(lambda k=k, s=s: nc.scalar.activation(out=s[:, 0:N], in_=x2[:, k, :], func=AF.Square), r=['x2'], w=[f'sq{k%2}'])
            P.pe(lambda k=k, s=s: nc.tensor.matmul(ps[7][:, 0:N], lhsT=ones_bf[:], rhs=s[:, 0:N], start=(k == 0), stop=(k == 7)),
                 r=[f'sq{k%2}', 'ones'], w=['ps7'])
        P.act(lambda: nc.scalar.activation(out=rstd[:, 0:N], in_=ps[7][:, 0:N], func=AF.Ln, bias=eps_c[:], scale=1.0 / D), r=['ps7', 'eps'], w=['rstd'])
        P.act(lambda: nc.scalar.activation(out=rstd[:, 0:N], in_=rstd[:, 0:N], func=AF.Exp, scale=-0.5), r=['rstd'], w=['rstd'])
        for k in range(8):
            P.dve(lambda k=k: nc.vector.scalar_tensor_tensor(out=yo[:, k, :], in0=x2[:, k, :], scalar=gf[:, k:k + 1], in1=rstd[:, 0:N],
                                                            op0=ALU.mult, op1=ALU.mult), r=['x2', 'rstd', 'gf'], w=['yo'])
        P.dma('pool', lambda ts_=ts_: g.dma_start(out=yT[:, ts_].rearrange("(k p) n -> p k n", p=128), in_=yo[:]), r=['yo'])
    return nc, P.finalize()


_PROGS = {}


def _prog(name, fn):
    if name not in _PROGS:
        _PROGS[name] = fn()[0]
    return _PROGS[name]


def _lay(gv):
    return np.ascontiguousarray(np.asarray(gv, np.float32).reshape(8, 128).T)


def kernel(**inputs):
    import ml_dtypes
    from concourse.bass_utils import run_bass_kernel_spmd
    inp = {k: np.asarray(v) for k, v in inputs.items()}
    B = 2
    cores = list(range(8))
    offs = np.cumsum([0, 512, 512, 512, 8, 512, 768, 24, 1024, 1024])
    x = [np.ascontiguousarray(inp['x'][b]) for b in range(B)]
    y = None
    for l in range(2):
        ncA = _prog('A', lambda: build_A(True, True))
        mapsA = [inputs_A(inp, l, c // 4, c % 4, x_b=x[c // 4]) for c in cores]
        resA = run_bass_kernel_spmd(ncA, mapsA, core_ids=cores).results
        oa = [np.concatenate([resA[4 * b + hp]['oaT'] for hp in range(4)], axis=0) for b in range(B)]
        on = [np.concatenate([resA[4 * b + hp]['onT'] for hp in range(4)], axis=0) for b in range(B)]
        w_in = inp['w_in'][l]
        ncB1 = _prog('B1', build_B1)
        mapsB1 = []
        for c in cores:
            b, r = c // 4, c % 4
            rs = slice(2048 * r, 2048 * r + 2048)
            mapsB1.append(dict(xT=np.ascontiguousarray(x[b][rs].T), oaT=np.ascontiguousarray(oa[b][:, rs]), onT=np.ascontiguousarray(on[b][:, rs]),
                               gmix=_lay(inp['norm_mix'][l]), gmlp=_lay(inp['norm_mlp'][l]),
                               w_of=np.ascontiguousarray(inp['w_o_fox'][l]), w_on=np.ascontiguousarray(inp['w_o_nsa'][l]),
                               w_ga=np.ascontiguousarray(w_in[:, offs[7]:offs[8]]), w_gb=np.ascontiguousarray(w_in[:, offs[8]:offs[9]]),
                               w_out=np.ascontiguousarray(inp['w_out'][l])))
        resB1 = run_bass_kernel_spmd(ncB1, mapsB1, core_ids=cores).results
        ncB2 = _prog('B2', build_B2)
        mapsB2 = [dict(x1T=resB1[c]['x1T'], h2T=resB1[c]['h2T'], gfin=_lay(inp['norm_final']),
                       w_up=np.ascontiguousarray(inp['w_up'][l]), w_down=np.ascontiguousarray(inp['w_down'][l])) for c in cores]
        resB2 = run_bass_kernel_spmd(ncB2, mapsB2, core_ids=cores).results
        x = [np.ascontiguousarray(np.concatenate([resB2[4 * b + r]['x2T'].T for r in range(4)], axis=0)) for b in range(B)]
        y = np.stack([np.concatenate([resB2[4 * b + r]['yT'].T for r in range(4)], axis=0) for b in range(B)], axis=0)
    return np.ascontiguousarray(y.astype(np.float32))
```
